# Optimizing a Trainium2 kernel written in Bass

```python
import jax, jax.numpy as jnp
from jax import lax
import numpy as np

D_MODEL = 1024
BATCH = 8
SEQ = 2048
DEPTH = 2
DEC_BATCH = 1
DEC_SEQ = 16384
PAST_LEN = 128

GRID_W = 64
N_MIXERS = 2
NA_HEADS = 16
NA_HEAD_DIM = 64
NA_WIN_ROWS = 8
NA_WIN_COLS = 16
NA_COL_BLOCK = 16
NA_KEY_COLS = 32
GQA_Q_HEADS = 8
GQA_KV_HEADS = 4
GQA_GROUP = GQA_Q_HEADS // GQA_KV_HEADS
GQA_HEAD_DIM = 128
GQA_Q_BLOCK = 128
ROPE_THETA = 10000.0
PLE_DIM = 256
D_FF = ((8 * D_MODEL // 3 + 255) // 256) * 256
EPS = 1e-6
NEG_INF = -1e30

kernel_name = "hybrid_natten_gqa_sandwich_ple_encoder"


def _rms(x):
    xf = x.astype(jnp.float32)
    return xf * lax.rsqrt(jnp.mean(xf * xf, axis=-1, keepdims=True) + EPS)


def rms_norm(x, g):
    return (_rms(x) * g.astype(jnp.float32)).astype(x.dtype)


def _na_static_tables():
    n_cb = GRID_W // NA_COL_BLOCK
    qcol = np.arange(GRID_W).reshape(n_cb, NA_COL_BLOCK)
    kstart = np.clip(np.arange(n_cb) * NA_COL_BLOCK - NA_WIN_COLS // 2, 0, GRID_W - NA_KEY_COLS)
    kcol = kstart[:, None] + np.arange(NA_KEY_COLS)
    wstart = np.clip(qcol - NA_WIN_COLS // 2, 0, GRID_W - NA_WIN_COLS)
    kc = kcol[:, None, :]
    col_mask = (kc >= wstart[..., None]) & (kc < wstart[..., None] + NA_WIN_COLS)
    col_off = np.clip(kc - qcol[..., None] + NA_WIN_COLS - 1, 0, 2 * NA_WIN_COLS - 2)
    return n_cb, kcol, col_mask, col_off


def neighborhood_attention(h, w_qkv, rpb, w_o):
    B, L, _ = h.shape
    rows = L // GRID_W
    kr = min(NA_WIN_ROWS, rows)
    n_cb, kcol, col_mask, col_off = _na_static_tables()
    qkv = (h @ w_qkv).reshape(B, rows, GRID_W, 3, NA_HEADS, NA_HEAD_DIM)
    q = qkv[:, :, :, 0] * (NA_HEAD_DIM ** -0.5)
    k = qkv[:, :, :, 1]
    v = qkv[:, :, :, 2]
    qcb = q.reshape(B, rows, n_cb, NA_COL_BLOCK, NA_HEADS, NA_HEAD_DIM)
    kcb = k[:, :, kcol]
    vcb = v[:, :, kcol]
    rpb_cols = rpb[:, :, col_off]
    mask = jnp.asarray(col_mask)[:, :, None, :]

    def row_fn(r):
        rs = jnp.clip(r - kr // 2, 0, rows - kr)
        kb = lax.dynamic_slice_in_dim(kcb, rs, kr, axis=1)
        vb = lax.dynamic_slice_in_dim(vcb, rs, kr, axis=1)
        qr = lax.dynamic_index_in_dim(qcb, r, axis=1, keepdims=False)
        s = jnp.einsum('bjqhd,bujkhd->bhjquk', qr, kb, preferred_element_type=jnp.float32)
        row_idx = rs + jnp.arange(kr) - r + NA_WIN_ROWS - 1
        bias = jnp.take(rpb_cols, row_idx, axis=1).astype(jnp.float32)
        bias = bias.transpose(0, 2, 3, 1, 4)
        s = jnp.where(mask, s + bias[None], NEG_INF)
        s_shape = s.shape
        p = jax.nn.softmax(s.reshape(s_shape[:4] + (kr * NA_KEY_COLS,)), axis=-1).reshape(s_shape)
        return jnp.einsum('bhjquk,bujkhd->bjqhd', p.astype(vb.dtype), vb)

    o = lax.map(row_fn, jnp.arange(rows))
    o = o.transpose(1, 0, 2, 3, 4, 5).reshape(B, L, NA_HEADS * NA_HEAD_DIM)
    return o @ w_o


def _axial_rope(L):
    t = jnp.arange(L)
    row = (t // GRID_W).astype(jnp.float32)
    col = (t % GRID_W).astype(jnp.float32)
    axis_dim = GQA_HEAD_DIM // 2
    inv_freq = ROPE_THETA ** (-jnp.arange(0, axis_dim, 2, dtype=jnp.float32) / axis_dim)
    ang = jnp.stack([row[:, None] * inv_freq, col[:, None] * inv_freq], axis=1)
    return jnp.cos(ang), jnp.sin(ang)


def _apply_rope(x, cos, sin):
    B, L, H, _ = x.shape
    n_f = GQA_HEAD_DIM // 4
    xr = x.astype(jnp.float32).reshape(B, L, H, 2, 2, n_f)
    a, b = xr[..., 0, :], xr[..., 1, :]
    c = cos[None, :, None]
    s = sin[None, :, None]
    out = jnp.stack([a * c - b * s, b * c + a * s], axis=-2)
    return out.reshape(B, L, H, GQA_HEAD_DIM).astype(x.dtype)


def gqa_attention(h, w_qkv, q_norm, k_norm, w_o):
    B, L, _ = h.shape
    qd = GQA_Q_HEADS * GQA_HEAD_DIM
    kd = GQA_KV_HEADS * GQA_HEAD_DIM
    qkv = h @ w_qkv
    q = qkv[..., :qd].reshape(B, L, GQA_Q_HEADS, GQA_HEAD_DIM)
    k = qkv[..., qd:qd + kd].reshape(B, L, GQA_KV_HEADS, GQA_HEAD_DIM)
    v = qkv[..., qd + kd:].reshape(B, L, GQA_KV_HEADS, GQA_HEAD_DIM)
    cos, sin = _axial_rope(L)
    q = _apply_rope(rms_norm(q, q_norm), cos, sin) * (GQA_HEAD_DIM ** -0.5)
    k = _apply_rope(rms_norm(k, k_norm), cos, sin)
    nb = L // GQA_Q_BLOCK
    qb = q.reshape(B, nb, GQA_Q_BLOCK, GQA_KV_HEADS, GQA_GROUP, GQA_HEAD_DIM).transpose(1, 0, 2, 3, 4, 5)

    def blk(qi):
        s = jnp.einsum('bqkgd,bskd->bkgqs', qi, k, preferred_element_type=jnp.float32)
        p = jax.nn.softmax(s, axis=-1).astype(v.dtype)
        return jnp.einsum('bkgqs,bskd->bqkgd', p, v)

    o = lax.map(blk, qb)
    o = o.transpose(1, 0, 2, 3, 4, 5).reshape(B, L, GQA_Q_HEADS * GQA_HEAD_DIM)
    return o @ w_o


def swiglu(h, w_gate_up, w_down):
    gu = h @ w_gate_up
    g, u = gu[..., :D_FF], gu[..., D_FF:]
    return (jax.nn.silu(g) * u) @ w_down


def trunk(x, p, mix_pre_norm, mix_post_norm, ffn_pre_norm, ffn_post_norm,
          na_w_qkv, na_rpb, na_w_o, gqa_w_qkv, gqa_q_norm, gqa_k_norm, gqa_w_o,
          ffn_w_gate_up, ffn_w_down, ple_w_gate, ple_w_proj):
    h = x
    for i in range(DEPTH):
        j = i // N_MIXERS
        a = rms_norm(h, mix_pre_norm[i])
        if i % N_MIXERS == 0:
            m = neighborhood_attention(a, na_w_qkv[j], na_rpb[j], na_w_o[j])
        else:
            m = gqa_attention(a, gqa_w_qkv[j], gqa_q_norm[j], gqa_k_norm[j], gqa_w_o[j])
        h = h + rms_norm(m, mix_post_norm[i])
        f = swiglu(rms_norm(h, ffn_pre_norm[i]), ffn_w_gate_up[i], ffn_w_down[i])
        h = h + rms_norm(f, ffn_post_norm[i])
        gate = jax.nn.sigmoid(_rms(h).astype(h.dtype) @ ple_w_gate[i])
        h = h + (p[i] @ ple_w_proj[i]) * gate
    return h


def setup_inputs(seed: int = 0) -> dict:
    key = jax.random.key(seed)
    ks = jax.random.split(key, 20)
    n_a = (DEPTH + 1) // 2
    n_b = DEPTH // 2
    f32 = jnp.float32
    nrm = lambda k, shape, scale: jax.random.normal(k, shape, f32) * scale
    gain = lambda k, shape: 1.0 + 0.05 * jax.random.normal(k, shape, f32)
    na_w = NA_HEADS * NA_HEAD_DIM
    gqa_w = (GQA_Q_HEADS + 2 * GQA_KV_HEADS) * GQA_HEAD_DIM
    return {
        "x_prompt": jax.random.normal(ks[0], (BATCH, SEQ, D_MODEL), f32),
        "x_sample": jax.random.normal(ks[1], (DEC_BATCH, DEC_SEQ, D_MODEL), f32),
        "p_prompt": jax.random.normal(ks[2], (DEPTH, BATCH, SEQ, PLE_DIM), f32),
        "p_sample": jax.random.normal(ks[3], (DEPTH, DEC_BATCH, DEC_SEQ, PLE_DIM), f32),
        "mix_pre_norm": gain(ks[4], (DEPTH, D_MODEL)),
        "mix_post_norm": gain(ks[5], (DEPTH, D_MODEL)),
        "ffn_pre_norm": gain(ks[6], (DEPTH, D_MODEL)),
        "ffn_post_norm": gain(ks[7], (DEPTH, D_MODEL)),
        "na_w_qkv": nrm(ks[8], (n_a, D_MODEL, 3 * na_w), D_MODEL ** -0.5),
        "na_rpb": nrm(ks[9], (n_a, NA_HEADS, 2 * NA_WIN_ROWS - 1, 2 * NA_WIN_COLS - 1), 0.5),
        "na_w_o": nrm(ks[10], (n_a, na_w, D_MODEL), na_w ** -0.5),
        "gqa_w_qkv": nrm(ks[11], (n_b, D_MODEL, gqa_w), D_MODEL ** -0.5),
        "gqa_q_norm": gain(ks[12], (n_b, GQA_HEAD_DIM)),
        "gqa_k_norm": gain(ks[13], (n_b, GQA_HEAD_DIM)),
        "gqa_w_o": nrm(ks[14], (n_b, GQA_Q_HEADS * GQA_HEAD_DIM, D_MODEL), (GQA_Q_HEADS * GQA_HEAD_DIM) ** -0.5),
        "ffn_w_gate_up": nrm(ks[15], (DEPTH, D_MODEL, 2 * D_FF), D_MODEL ** -0.5),
        "ffn_w_down": nrm(ks[16], (DEPTH, D_FF, D_MODEL), D_FF ** -0.5),
        "ple_w_gate": nrm(ks[17], (DEPTH, D_MODEL, D_MODEL), D_MODEL ** -0.5),
        "ple_w_proj": nrm(ks[18], (DEPTH, PLE_DIM, D_MODEL), PLE_DIM ** -0.5),
    }


def reference(x_prompt, x_sample, p_prompt, p_sample, mix_pre_norm, mix_post_norm,
              ffn_pre_norm, ffn_post_norm, na_w_qkv, na_rpb, na_w_o, gqa_w_qkv,
              gqa_q_norm, gqa_k_norm, gqa_w_o, ffn_w_gate_up, ffn_w_down,
              ple_w_gate, ple_w_proj):
    y_prompt = trunk(x_prompt, p_prompt, mix_pre_norm, mix_post_norm, ffn_pre_norm, ffn_post_norm,
                     na_w_qkv, na_rpb, na_w_o, gqa_w_qkv, gqa_q_norm, gqa_k_norm, gqa_w_o,
                     ffn_w_gate_up, ffn_w_down, ple_w_gate, ple_w_proj)
    y_sample = trunk(x_sample, p_sample, mix_pre_norm, mix_post_norm, ffn_pre_norm, ffn_post_norm,
                     na_w_qkv, na_rpb, na_w_o, gqa_w_qkv, gqa_q_norm, gqa_k_norm, gqa_w_o,
                     ffn_w_gate_up, ffn_w_down, ple_w_gate, ple_w_proj)
    return (y_prompt, y_sample)
```

```python
import numpy as np
import concourse.bass as bass
import concourse.mybir as mybir
from concourse.bass_utils import run_bass_kernel_spmd

F32 = mybir.dt.float32
BF16 = mybir.dt.bfloat16
AF = mybir.ActivationFunctionType
ALU = mybir.AluOpType

NCORES = 8
D = 1024
KC = 8
TOK = 2048
HTOK = 2560
DFF = 2816
NF = 22
EPS = 1e-6
NEG = -30000.0
NUNITS = 9
U_OWN = 7
U_PROMPT = 8
ND = 14
KB = 1024


class Prog:
    ENG = ("pe", "act", "dve", "pool", "sp")

    def __init__(self, nc, n_dma_sp=24, n_dma_pool=8):
        self.nc = nc
        self.ops = {e: [] for e in self.ENG}
        self.cnt = {}
        self.waited = {e: {} for e in self.ENG}
        self.last_w = {}
        self.readers = {}
        self.sem_names = ["pe", "act", "dve", "pool"]
        self.dma_slots = {"sp": [f"dsp{i}" for i in range(n_dma_sp)],
                          "pool": [f"dpl{i}" for i in range(n_dma_pool)]}
        self.dma_next = {"sp": 0, "pool": 0}
        for q in self.dma_slots.values():
            self.sem_names += q
        for s in self.sem_names:
            self.cnt[s] = 0
        self.handles = {}
        self.nops = 0

    def _deps(self, reads, writes):
        deps = set()
        for k in reads:
            if k in self.last_w:
                deps.add(self.last_w[k])
        for k in writes:
            if k in self.last_w:
                deps.add(self.last_w[k])
            for r in self.readers.get(k, ()):
                deps.add(r)
        return deps

    def _record(self, reads, writes, done):
        for k in reads:
            self.readers.setdefault(k, []).append(done)
        for k in writes:
            self.last_w[k] = done
            self.readers[k] = []

    def _waits(self, eng, deps):
        need = {}
        for (s, v) in deps:
            if eng == "pe" and s == "pe":
                continue
            if self.waited[eng].get(s, 0) < v:
                need[s] = max(need.get(s, 0), v)
        for s, v in need.items():
            self.waited[eng][s] = v
        return list(need.items())

    def op(self, eng, fn, reads=(), writes=()):
        deps = self._deps(reads, writes)
        waits = self._waits(eng, deps)
        self.cnt[eng] += 1
        done = (eng, self.cnt[eng])
        self.ops[eng].append((waits, fn, (eng, 1)))
        self._record(reads, writes, done)
        self.nops += 1
        return done

    def dma(self, q, fn, reads=(), writes=()):
        slots = self.dma_slots[q]
        s = slots[self.dma_next[q] % len(slots)]
        self.dma_next[q] += 1
        deps = self._deps(reads, writes)
        if self.cnt[s] > 0:
            deps.add((s, self.cnt[s]))
        waits = self._waits(q, deps)
        self.cnt[s] += 16
        done = (s, self.cnt[s])
        self.ops[q].append((waits, fn, (s, 16)))
        self._record(reads, writes, done)
        self.nops += 1
        return done

    def barrier(self, engines=None):
        deps = set((s, c) for s, c in self.cnt.items() if c > 0)
        for e in (engines or self.ENG):
            waits = self._waits(e, set(deps))
            if waits:
                self.ops[e].append((waits, None, None))
        self.last_w = {}
        self.readers = {}

    def emit(self, block):
        H = self.handles

        class FirstCatcher:
            def __init__(self, e):
                self._e = e
                self.first = None

            def __getattr__(self, name):
                attr = getattr(self._e, name)
                if not callable(attr):
                    return attr

                def w(*a, **k):
                    r = attr(*a, **k)
                    if self.first is None and hasattr(r, "then_inc"):
                        self.first = r
                    return r
                return w

        def run(eng_name):
            def body(e):
                for (waits, fn, sig) in self.ops[eng_name]:
                    if fn is None:
                        for (s, v) in waits:
                            e.wait_ge(H[s], v)
                        continue
                    for (s, v) in waits[:-1]:
                        e.wait_ge(H[s], v)
                    fc = FirstCatcher(e)
                    ins = fn(fc)
                    if waits:
                        s, v = waits[-1]
                        fc.first._wait_ge(H[s], v)
                    if sig is not None:
                        ins.then_inc(H[sig[0]], sig[1])
            return body

        block.tensor(run("pe"))
        block.scalar(run("act"))
        block.vector(run("dve"))
        block.gpsimd(run("pool"))
        block.sync(run("sp"))


class Arena:
    def __init__(self, nc, nbytes):
        self.t = nc.alloc_sbuf_tensor("arena", [128, nbytes // 4], F32)
        self.t16 = self.t.bitcast(BF16)
        self.nbytes = nbytes

    def f32(self, off, n):
        assert off % 4 == 0 and off + 4 * n <= self.nbytes, (off, n)
        return self.t[:, off // 4: off // 4 + n]

    def bf16(self, off, n):
        assert off % 2 == 0 and off + 2 * n <= self.nbytes, (off, n)
        return self.t16[:, off // 2: off // 2 + n]


def r3(ap, a):
    return ap.rearrange("p (a b) -> p a b", a=a)


def r4(ap, a, b):
    return ap.rearrange("p (a b c) -> p a b c", a=a, b=b)


def na_row_spec(i):
    l = i + 4
    if l < 8:
        tiles = list(range(0, 6))
        masks = [3 + (l - 4) * 6 + j for j in range(6)]
    elif l >= 32:
        tiles = list(range(14, 20))
        masks = [3 + 24 + (l - 32) * 6 + j for j in range(6)]
    elif l % 2 == 0:
        tiles = list(range(l // 2 - 2, l // 2 + 2))
        masks = [0, 0, 0, 0]
    else:
        t0 = (l - 5) // 2
        tiles = list(range(t0, t0 + 5))
        masks = [1, 0, 0, 0, 2]
    return l, tiles, masks


def build(cfg=None):
    cfg = cfg or {}
    units_l0 = cfg.get("units", list(range(NUNITS)))
    do_l1 = cfg.get("do_l1", True)
    dbg = cfg.get("dbg", None)

    nc = bass.Bass("TRN2", target_bir_lowering=False)
    pg = Prog(nc)

    def din(name, shape, dt=F32):
        return nc.dram_tensor(name, list(shape), dt, kind="ExternalInput").ap()

    xu = din("xu", [NUNITS, HTOK, D])
    p0u = din("p0u", [NUNITS, TOK, 256])
    p1u = din("p1u", [2, TOK, 256])
    rvd = din("rv", [NUNITS, 128, 48])
    ropec = din("ropec", [NUNITS, 128, TOK])
    ropes = din("ropes", [NUNITS, 128, TOK])
    utab = din("utab", [8, 128, 2 * ND * 64])
    utab2 = din("utab2", [8, 128, 2 * ND * 64])
    cst = din("cst", [128, 387])
    gains = din("gains", [128, 64])
    gqk = din("gqk", [128, 2])
    w_na_qkv = din("na_w_qkv", [D, 3072])
    w_na_o = din("na_w_o", [D, D])
    w_gqa_qkv = din("gqa_w_qkv", [D, 2048])
    w_gqa_o = din("gqa_w_o", [D, D])
    w_gu = din("ffn_w_gate_up", [2, D, 2 * DFF])
    w_dn = din("ffn_w_down", [2, DFF, D])
    w_pg = din("ple_w_gate", [2, D, D])
    w_pp = din("ple_w_proj", [2, 256, D])

    yp = nc.dram_tensor("yp", [TOK, D], F32, kind="ExternalOutput").ap()
    ys = nc.dram_tensor("ys", [TOK, D], F32, kind="ExternalOutput").ap()

    def wscratch(name, npanel, kcn, ow):
        return nc.dram_tensor(name, [npanel, 128, kcn * ow], BF16).ap()

    wb = {
        "na_q": wscratch("wb_na_q", 8, 8, 128), "na_k": wscratch("wb_na_k", 8, 8, 128),
        "na_v": wscratch("wb_na_v", 2, 8, 512), "na_o": wscratch("wb_na_o", 8, 8, 128),
        "gq": wscratch("wb_gq", 8, 8, 128), "gk": wscratch("wb_gk", 4, 8, 128),
        "gv": wscratch("wb_gv", 1, 8, 512), "go": wscratch("wb_go", 8, 8, 128),
    }
    for L in range(2):
        wb[f"fg{L}"] = wscratch(f"wb_fg{L}", NF, 8, 128)
        wb[f"fu{L}"] = wscratch(f"wb_fu{L}", NF, 8, 128)
        wb[f"fd{L}"] = wscratch(f"wb_fd{L}", 8, NF, 128)
        wb[f"pg{L}"] = wscratch(f"wb_pg{L}", 8, 8, 128)
        wb[f"pp{L}"] = wscratch(f"wb_pp{L}", 8, 2, 128)
    kvs = nc.dram_tensor("kvs", [NUNITS, 4, 2, 128, TOK], BF16).ap()

    A = Arena(nc, 206 * KB)
    PB = 196 * KB
    cst_f = A.f32(PB, 387); PB += 387 * 4 + 4
    ident_f = cst_f[:, 0:128]
    ident_b = A.bf16(PB, 128); PB += 256
    ones_b = A.bf16(PB, 128); PB += 256
    rm_b = A.bf16(PB, 128); PB += 256
    gains_f = A.f32(PB, 64); PB += 256
    gqk_f = A.f32(PB, 4); PB += 16
    rv_f = A.f32(PB, 51); PB += 208
    sm_f = A.f32(PB, 64); PB += 256
    assert PB <= 206 * KB

    def gain(kind, layer):
        i = (kind * 2 + layer) * 8
        return gains_f[:, i:i + 8]

    PS = nc.alloc_psum_tensor("psum_all", [128, 4096], F32)
    PS16 = PS.bitcast(BF16)

    def bank(i, n=512, o=0):
        return PS[:, i * 512 + o: i * 512 + o + n]

    def bank16(i, n=1024, o=0):
        return PS16[:, i * 1024 + o: i * 1024 + o + n]

    def pk(*banks):
        return tuple(f"ps{b}" for b in banks)

    def mm(out, pairs, reads, writes):
        def fn(e):
            n = len(pairs)
            ins = None
            for i, (l, r) in enumerate(pairs):
                ins = e.matmul(out, l, r, start=(i == 0), stop=(i == n - 1))
            return ins
        return pg.op("pe", fn, reads, writes)

    def tr(out, in_, idn, reads, writes):
        return pg.op("pe", lambda e: e.transpose(out, in_, idn), reads, writes)

    def act(out, in_, func, reads, writes, **kw):
        return pg.op("act", lambda e: e.activation(out=out, in_=in_, func=func, **kw), reads, writes)

    def tt(eng, out, a, b, op, reads, writes):
        return pg.op(eng, lambda e: e.tensor_tensor(out, a, b, op), reads, writes)

    def ts(eng, out, a, s1, s2, op0, op1, reads, writes):
        return pg.op(eng, lambda e: e.tensor_scalar(out, a, s1, s2, op0, op1), reads, writes)

    def stt(out, a, s, b, op0, op1, reads, writes):
        return pg.op("dve", lambda e: e.scalar_tensor_tensor(out, a, s, b, op0, op1), reads, writes)

    def cp(eng, out, in_, reads, writes):
        if eng == "act":
            return pg.op("act", lambda e: e.copy(out, in_), reads, writes)
        return pg.op(eng, lambda e: e.tensor_copy(out, in_), reads, writes)

    def ld(out, in_, reads, writes, q="sp"):
        return pg.dma(q, lambda e: e.dma_start(out=out, in_=in_), reads, writes)

    ld(cst_f, cst, [], ["cst"])
    ld(gains_f, gains, [], ["gains"])
    ld(gqk_f[:, 0:2], gqk, [], ["gqk"])
    cp("dve", ident_b, cst_f[:, 0:128], ["cst"], ["ident_b"])
    cp("dve", ones_b, cst_f[:, 128:256], ["cst"], ["ones_b"])
    cp("dve", rm_b, cst_f[:, 256:384], ["cst"], ["rm_b"])
    cp("dve", rv_f[:, 0:3], cst_f[:, 384:387], ["cst"], ["rvc"])
    ts("dve", gqk_f[:, 2:3], gqk_f[:, 0:1], float(128 ** -0.5), None, ALU.mult, ALU.bypass, ["gqk"], ["gqs"])

    conv = []
    for c in range(8):
        conv.append((w_na_qkv, c * 128, 8, 128, "na_q", c))
        conv.append((w_na_qkv, 1024 + c * 128, 8, 128, "na_k", c))
    for g in range(2):
        conv.append((w_na_qkv, 2048 + g * 512, 8, 512, "na_v", g))
    for c in range(8):
        conv.append((w_na_o, c * 128, 8, 128, "na_o", c))
    for L in range(2):
        for f in range(NF):
            conv.append((w_gu[L], f * 128, 8, 128, f"fg{L}", f))
            conv.append((w_gu[L], DFF + f * 128, 8, 128, f"fu{L}", f))
        for c in range(8):
            conv.append((w_dn[L], c * 128, NF, 128, f"fd{L}", c))
            conv.append((w_pg[L], c * 128, 8, 128, f"pg{L}", c))
            conv.append((w_pp[L], c * 128, 2, 128, f"pp{L}", c))
    for c in range(8):
        conv.append((w_gqa_qkv, c * 128, 8, 128, "gq", c))
    for c in range(4):
        conv.append((w_gqa_qkv, 1024 + c * 128, 8, 128, "gk", c))
    conv.append((w_gqa_qkv, 1536, 8, 512, "gv", 0))
    for c in range(8):
        conv.append((w_gqa_o, c * 128, 8, 128, "go", c))

    NSTG = 8
    stg_f = [A.f32(i * 16 * KB, 4096) for i in range(NSTG)]
    stg_b = [A.bf16(128 * KB + i * 8 * KB, 4096) for i in range(NSTG)]
    ceng = ["pool", "dve", "act"]
    for i, (src, col0, kcn, ow, name, panel) in enumerate(conv):
        b = i % NSTG
        n = kcn * ow
        srcv = src.rearrange("(kc p) o -> p kc o", p=128)[:, :, col0:col0 + ow]
        ld(r3(stg_f[b][:, 0:n], kcn), srcv, [], [f"stgf{b}"])
        cp(ceng[i % 3], stg_b[b][:, 0:n], stg_f[b][:, 0:n], [f"stgf{b}"], [f"stgb{b}"])
        ld(wb[name][panel], stg_b[b][:, 0:n], [f"stgb{b}"], [f"wb_{name}_{panel}"], q="pool")
    pg.barrier()
    WB_KEYS = {}

    def wkey(name, panel):
        return f"wb_{name}_{panel}"

    HT_OFF = 0

    def hT_view():
        return r3(A.f32(HT_OFF, 8 * TOK), 8)

    def norm_stats(src_fn, ntok, sq_b, rstd_f, psb, inv_n, rkey):
        for kc in range(KC):
            s_ap, rk = src_fn(kc)
            eng = "act" if kc % 2 == 0 else "pool"
            if eng == "act":
                act(sq_b[:, kc, 0:ntok], s_ap, AF.Square, rk, [f"nsq{kc}"])
            else:
                tt("pool", sq_b[:, kc, 0:ntok], s_ap, s_ap, ALU.mult, rk, [f"nsq{kc}"])
        mm(bank(psb, ntok), [(ones_b, sq_b[:, kc, 0:ntok]) for kc in range(KC)],
           [f"nsq{kc}" for kc in range(KC)] + ["ones_b"], pk(psb))
        act(rstd_f[:, 0:ntok], bank(psb, ntok), AF.Sqrt, pk(psb), [rkey], scale=inv_n, bias=sm_f[:, 0:1])
        pg.op("dve", lambda e: e.reciprocal(rstd_f[:, 0:ntok], rstd_f[:, 0:ntok]), [rkey], [rkey])

    pg.op("pool", lambda e: e.memset(sm_f[:, 0:1], EPS), [], ["eps"])
    pg.barrier()

    AT_OFF = 0
    OT0_OFF = 64 * KB

    def phase_x(u):
        aT = r3(A.bf16(AT_OFF, 8 * HTOK), 8)
        xs_f = [A.f32(141 * KB + b * 4 * KB, 1024) for b in range(2)]
        xs_b = [A.bf16(149 * KB + b * 2 * KB, 1024) for b in range(2)]
        junk = A.bf16(153 * KB, 1024)
        g = gain(0, 0)
        for t in range(HTOK // 128):
            b = t % 2
            ld(xs_f[b], xu[u, t * 128:(t + 1) * 128, :], [], [f"xsf{b}"])
            act(junk, xs_f[b], AF.Square, [f"xsf{b}"], ["junk", f"ss{b}"], accum_out=sm_f[:, 2 + b:3 + b])
            act(sm_f[:, 4 + b:5 + b], sm_f[:, 2 + b:3 + b], AF.Sqrt, [f"ss{b}", "eps"], [f"sr{b}"],
                scale=1.0 / D, bias=sm_f[:, 0:1])
            pg.op("dve", lambda e, b=b: e.reciprocal(sm_f[:, 6 + b:7 + b], sm_f[:, 4 + b:5 + b]),
                  [f"sr{b}"], [f"rs{b}"])
            ts("dve", xs_b[b], xs_f[b], sm_f[:, 6 + b:7 + b], None, ALU.mult, ALU.bypass,
               [f"xsf{b}", f"rs{b}"], [f"xsb{b}"])
            for kc in range(KC):
                tr(bank16(b, 128, kc * 128), xs_b[b][:, kc * 128:(kc + 1) * 128], ident_b,
                   [f"xsb{b}", "ident_b"], pk(b))
            gb = g.unsqueeze(2).to_broadcast([128, 8, 128])
            tt("dve", aT[:, :, t * 128:(t + 1) * 128], r3(bank16(b, 1024), 8), gb, ALU.mult,
               list(pk(b)) + ["gains"], [f"aT{t // 4}"])

    def phase_na(u):
        aT = r3(A.bf16(AT_OFF, 8 * HTOK), 8)
        oT = r3(A.bf16(OT0_OFF, 8 * TOK), 8)
        qT = [A.bf16(40 * KB + b * 4 * KB, TOK) for b in range(2)]
        kT = [A.bf16(48 * KB + b * 5 * KB, HTOK) for b in range(2)]
        Vg = [r3(A.bf16(170 * KB + k * 5248, 20 * 130), 20) for k in range(4)]
        Wvg = r3(A.bf16(96 * KB, 8 * 512), 8)
        Wp = [[r3(A.bf16(107 * KB + (b * 3 + k) * 2 * KB, 1024), 8) for k in range(3)] for b in range(2)]
        Ut = [A.f32(119 * KB + b * 7 * KB, 2 * ND * 64) for b in range(2)]
        Ui = [A.f32(156 * KB + b * 7 * KB, 2 * ND * 64) for b in range(2)]
        tmp = [A.f32(133 * KB + b * 4 * KB, 1024) for b in range(2)]
        PT = [A.bf16(60 * KB + b * 2 * KB, 1024) for b in range(2)]
        otok = [A.bf16(58 * KB + b * 256, 128) for b in range(2)]
        rcp = A.f32(59 * KB, 8)
        ld(rv_f[:, 3:51], rvd[u], [], ["rvu"])
        for k in range(4):
            pg.op("pool", lambda e, k=k: e.memset(Vg[k][:, :, 64:65], 1.0), [], [f"V{k}"])
            pg.op("pool", lambda e, k=k: e.memset(Vg[k][:, :, 129:130], 1.0), [], [f"V{k}"])
        aT_keys = [f"aT{i}" for i in range(5)]

        def load_w(c):
            b = c % 2
            for k, nm in enumerate(("na_q", "na_k")):
                ld(Wp[b][k], r3(wb[nm][c], 8), [wkey(nm, c)], [f"W{b}_{k}"])
            ld(Ut[b], utab[c], [], [f"U{b}"])
            ld(Ui[b], utab2[c], [], [f"Ui{b}"])

        load_w(0)
        for c in range(8):
            b = c % 2
            if c + 1 < 8:
                load_w(c + 1)
            for t4 in range(4):
                pb = t4 % 2
                mm(bank(pb), [(Wp[b][0][:, kc, :], aT[:, kc, 256 + t4 * 512: 256 + (t4 + 1) * 512]) for kc in range(KC)],
                   [f"W{b}_0"] + aT_keys, pk(pb))
                act(qT[b][:, t4 * 512:(t4 + 1) * 512], bank(pb), AF.Copy, pk(pb), [f"q{b}"], scale=0.125)
            for t5 in range(5):
                pb = t5 % 2
                mm(bank(pb), [(Wp[b][1][:, kc, :], aT[:, kc, t5 * 512:(t5 + 1) * 512]) for kc in range(KC)],
                   [f"W{b}_1"] + aT_keys, pk(pb))
                cp("act" if t5 % 2 else "dve", kT[b][:, t5 * 512:(t5 + 1) * 512], bank(pb), pk(pb), [f"k{b}"])
            if c % 4 == 0:
                ld(Wvg, r3(wb["na_v"][c // 4], 8), [wkey("na_v", c // 4)], ["Wvg"])
                vall = A.bf16(170 * KB, 4 * 2624).rearrange("p (k x) -> p k x", k=4)
                for t in range(20):
                    pb = 2 + t % 2
                    mm(bank(pb), [(aT[:, kc, t * 128:(t + 1) * 128], Wvg[:, kc, :]) for kc in range(KC)],
                       ["Wvg"] + aT_keys, pk(pb))
                    vout = vall[:, :, t * 130:(t + 1) * 130].rearrange("p k (h e) -> p k h e", h=2)[:, :, :, 0:64]
                    cp("dve" if t % 2 else "act", vout, bank(pb).rearrange("p (k h e) -> p k h e", k=4, h=2),
                       pk(pb), [f"V{k}" for k in range(4)])
            def row_qk(i, b=b):
                l, tiles, masks = na_row_spec(i)
                sb = 4 + 2 * (i % 2)

                def qk(e, b=b, i=i, tiles=tiles, sb=sb):
                    ins = None
                    for hd in range(2):
                        for j, t in enumerate(tiles):
                            ins = e.matmul(bank(sb + hd, 64, j * 64),
                                           kT[b][hd * 64:(hd + 1) * 64, t * 128:(t + 1) * 128],
                                           qT[b][hd * 64:(hd + 1) * 64, i * 64:(i + 1) * 64],
                                           start=True, stop=True)
                    return ins
                pg.op("pe", qk, [f"q{b}", f"k{b}"], pk(sb, sb + 1))

            def row_rest(i, b=b, c=c):
                l, tiles, masks = na_row_spec(i)
                nt = len(tiles)
                sb = 4 + 2 * (i % 2)
                pob = 2 + i % 2
                tb = i % 2
                skeys = pk(sb, sb + 1)
                d0 = 2 * tiles[0] - l + 7
                s_in = PS[:, sb * 512: sb * 512 + 1024].rearrange("p (h j q) -> p h j q", h=2, j=8)[:, :, 0:nt, :]
                edge = masks[0] >= 3
                Utab_ = Ut[b] if edge else Ui[b]
                ukey = f"U{b}" if edge else f"Ui{b}"
                u_in = Utab_.rearrange("p (h d q) -> p h d q", h=2, d=ND)[:, :, d0:d0 + 2 * nt - 1:2, :]
                t4v = tmp[tb].rearrange("p (h j q) -> p h j q", h=2, j=8)
                tt("dve", t4v[:, :, 0:nt, :], s_in, u_in, ALU.add, list(skeys) + [ukey], [f"tmp{tb}"])
                p_out = PT[tb].rearrange("p (h j q) -> p h j q", h=2, j=8)
                if edge:
                    for j in range(nt):
                        mcol = masks[j]
                        act(p_out[:, :, j:j + 1, :], t4v[:, :, j:j + 1, :],
                            AF.Exp, [f"tmp{tb}", "rvu", "rvc"], [f"PT{tb}"], bias=rv_f[:, mcol:mcol + 1])
                else:
                    act(p_out[:, :, 0:nt, :], t4v[:, :, 0:nt, :], AF.Exp, [f"tmp{tb}"], [f"PT{tb}"])

            def row_pv(i, b=b, c=c):
                l, tiles, masks = na_row_spec(i)
                nt = len(tiles)
                pob = 2 + i % 2
                tb = i % 2

                def pv(e, c=c, tiles=tiles, pob=pob, tb=tb, nt=nt):
                    ins = None
                    for hd in range(2):
                        for j, t in enumerate(tiles):
                            ins = e.matmul(bank(pob, 65, hd * 65)[0:64, :],
                                           PT[tb].rearrange("p (h j q) -> p h j q", h=2, j=8)[:, hd, j, :],
                                           Vg[c % 4][:, t, hd * 65:(hd + 1) * 65],
                                           start=(j == 0), stop=(j == nt - 1))
                    return ins
                pg.op("pe", pv, [f"PT{tb}", f"V{c % 4}"], pk(pob))

            def row_fin(i, b=b, c=c):
                pob = 2 + i % 2
                half = i % 2
                ob = (i // 2) % 2
                po3 = r3(bank(pob, 130), 2)[0:64]
                pg.op("dve", lambda e, po3=po3, half=half: e.reciprocal(rcp[0:64, half * 2:half * 2 + 2], po3[:, :, 64]),
                      pk(pob), [f"rcp{half}"])
                rb = rcp[0:64, half * 2:half * 2 + 2].unsqueeze(2).to_broadcast([64, 2, 64])
                tt("dve", r3(otok[ob], 2)[half * 64:(half + 1) * 64], po3[:, :, 0:64], rb, ALU.mult,
                   list(pk(pob)) + [f"rcp{half}"], [f"otok{ob}"])
                if half == 1:
                    tpb = ob
                    tr(bank16(tpb, 128), otok[ob], ident_b, [f"otok{ob}", "ident_b"], pk(tpb))
                    cp("act", oT[:, c, (i - 1) * 64:(i + 1) * 64], bank16(tpb, 128), pk(tpb), [f"oT{c}"])

            NROWS = cfg.get("na_rows", 32)
            for n in range(-2, NROWS + 1):
                if 0 <= n + 2 < NROWS:
                    row_qk(n + 2)
                if 0 <= n + 1 < NROWS:
                    row_rest(n + 1)
                if 0 <= n < NROWS:
                    row_pv(n)
                if 0 <= n - 1 < NROWS:
                    row_fin(n - 1)

    def proj8(wname, src3, src_keys, m3, ntok, woff, kcn=8, evac_scale=None, mtag="m"):
        Wd = [r3(A.bf16(woff + b * (kcn * 256), kcn * 128), kcn) for b in range(2)]
        ld(Wd[0], r3(wb[wname][0], kcn), [wkey(wname, 0)], ["Wd0"])
        n = 0
        for oc in range(8):
            b = oc % 2
            if oc + 1 < 8:
                ld(Wd[1 - b], r3(wb[wname][oc + 1], kcn), [wkey(wname, oc + 1)], [f"Wd{1 - b}"])
            for t4 in range(ntok // 512):
                pb = n % 4
                n += 1
                mm(bank(pb), [(Wd[b][:, k, :], src3[:, k, t4 * 512:(t4 + 1) * 512]) for k in range(kcn)],
                   [f"Wd{b}"] + src_keys, pk(pb))
                cp("act", m3[:, oc, t4 * 512:(t4 + 1) * 512], bank(pb), pk(pb), [f"{mtag}{oc}_{t4}"])

    def postnorm_residual(u, m3, ntok, tok0, gvec, sq_off, misc_off, x_init):
        hT = hT_view()
        sq_b = r3(A.bf16(sq_off, 8 * 512), 8)
        rstd = [A.f32(misc_off + b * 2 * KB, 512) for b in range(2)]
        tmpf = [A.f32(misc_off + 4 * KB + b * 2 * KB, 512) for b in range(2)]
        xst = [A.f32(misc_off + 8 * KB + b * 4 * KB, 1024) for b in range(2)]
        for t4 in range(ntok // 512):
            gt = (tok0 // 512) + t4
            b = t4 % 2
            norm_stats(lambda kc: (m3[:, kc, t4 * 512:(t4 + 1) * 512], [f"m{kc}_{t4}"]), 512, sq_b, rstd[b],
                       4 + b, 1.0 / D, f"rstd{b}")
            if x_init:
                for s in range(4):
                    xb = s % 2
                    tok = tok0 + t4 * 512 + s * 128
                    ld(xst[xb], xu[u, 256 + tok: 256 + tok + 128, :], [], [f"xst{xb}"])
                    for kc in range(KC):
                        tr(bank(6 + kc // 4, 128, (kc % 4) * 128), xst[xb][:, kc * 128:(kc + 1) * 128], ident_f,
                           [f"xst{xb}", "cst"], pk(6 + kc // 4))
                    cp("act", hT[:, :, tok:tok + 128],
                       PS[:, 6 * 512: 8 * 512].rearrange("p (k t) -> p k t", k=8), pk(6, 7), [f"h{gt}"])
            for kc in range(KC):
                tb = kc % 2
                stt(tmpf[tb], m3[:, kc, t4 * 512:(t4 + 1) * 512], gvec[:, kc:kc + 1], rstd[b], ALU.mult, ALU.mult,
                    [f"m{kc}_{t4}", f"rstd{b}", "gains"], [f"pnt{tb}"])
                tt("pool" if kc % 2 else "dve", hT[:, kc, tok0 + t4 * 512: tok0 + (t4 + 1) * 512],
                   hT[:, kc, tok0 + t4 * 512: tok0 + (t4 + 1) * 512], tmpf[tb], ALU.add,
                   [f"pnt{tb}", f"h{gt}"], [f"h{gt}"])

    def rmsnorm_fm(dst3, ntok, tok0, gvec, sq_off, misc_off, dtag):
        hT = hT_view()
        sq_b = r3(A.bf16(sq_off, 8 * 512), 8)
        rstd = [A.f32(misc_off + b * 2 * KB, 512) for b in range(2)]
        for t4 in range(ntok // 512):
            gt = (tok0 // 512) + t4
            b = t4 % 2
            norm_stats(lambda kc: (hT[:, kc, tok0 + t4 * 512: tok0 + (t4 + 1) * 512], [f"h{gt}"]), 512, sq_b,
                       rstd[b], 6 + b, 1.0 / D, f"rstd{b}")
            for kc in range(KC):
                src = hT[:, kc, tok0 + t4 * 512: tok0 + (t4 + 1) * 512]
                if gvec is not None:
                    stt(dst3[:, kc, t4 * 512:(t4 + 1) * 512], src, gvec[:, kc:kc + 1], rstd[b], ALU.mult, ALU.mult,
                        [f"h{gt}", f"rstd{b}", "gains"], [f"{dtag}{t4}"])
                else:
                    tt("dve", dst3[:, kc, t4 * 512:(t4 + 1) * 512], src, rstd[b], ALU.mult,
                       [f"h{gt}", f"rstd{b}"], [f"{dtag}{t4}"])

    def phase_wo_pn(u, L, wname, oT_off, m_off, w_off, sq_off, misc_off, x_init):
        oT = r3(A.bf16(oT_off, 8 * TOK), 8)
        m3 = r3(A.f32(m_off, 8 * 1024), 8)
        for half in range(2):
            src = oT[:, :, half * 1024:(half + 1) * 1024]
            proj8(wname, src, [f"oT{c}" for c in range(8)], m3, 1024, w_off)
            postnorm_residual(u, m3, 1024, half * 1024, gain(1, L), sq_off, misc_off, x_init)

    def phase_ffn(u, L):
        hT = hT_view()
        a2 = r3(A.bf16(64 * KB, 8 * 1024), 8)
        actT = r3(A.bf16(80 * KB, NF * 1024), NF)
        m3 = r3(A.f32(124 * KB, 8 * 1024), 8)
        Wgu = [[r3(A.bf16(156 * KB + (b * 2 + k) * 2 * KB, 1024), 8) for k in range(2)] for b in range(2)]
        sg = [A.f32(175 * KB + b * 2 * KB, 512) for b in range(2)]
        a2k = ["a2_0", "a2_1"]
        rmsnorm_fm(a2, 1024, 0, gain(2, L), 179 * KB, 187 * KB, "a2_")
        for half in range(2):

            def load_gu(f):
                b = f % 2
                ld(Wgu[b][0], r3(wb[f"fg{L}"][f], 8), [wkey(f"fg{L}", f)], [f"Wg{b}"])
                ld(Wgu[b][1], r3(wb[f"fu{L}"][f], 8), [wkey(f"fu{L}", f)], [f"Wu{b}"])
            load_gu(0)
            n = 0
            for f in range(NF):
                b = f % 2
                if f + 1 < NF:
                    load_gu(f + 1)
                for t2 in range(2):
                    pg_b = n % 2
                    pu_b = 2 + n % 2
                    sb_ = n % 2
                    n += 1
                    rhs = lambda kc: a2[:, kc, t2 * 512:(t2 + 1) * 512]
                    mm(bank(pg_b), [(Wgu[b][0][:, kc, :], rhs(kc)) for kc in range(KC)], [f"Wg{b}"] + a2k, pk(pg_b))
                    mm(bank(pu_b), [(Wgu[b][1][:, kc, :], rhs(kc)) for kc in range(KC)], [f"Wu{b}"] + a2k, pk(pu_b))
                    act(sg[sb_], bank(pg_b), AF.Silu, pk(pg_b), [f"sg{sb_}"])
                    tt("dve", actT[:, f, t2 * 512:(t2 + 1) * 512], sg[sb_], bank(pu_b), ALU.mult,
                       [f"sg{sb_}"] + list(pk(pu_b)), [f"act{f}"])
            proj8(f"fd{L}", actT, [f"act{f}" for f in range(NF)], m3, 1024, 164 * KB, kcn=NF)
            if half == 0:
                rmsnorm_fm(a2, 1024, 1024, gain(2, L), 179 * KB, 187 * KB, "a2_")
            postnorm_residual(u, m3, 1024, half * 1024, gain(3, L), 179 * KB, 187 * KB, False)

    def phase_ple(u, L, psrc):
        hT = hT_view()
        rT = r3(A.bf16(64 * KB, 8 * TOK), 8)
        pT = r3(A.bf16(96 * KB, 2 * TOK), 2)
        Wg = [r3(A.bf16(104 * KB + b * 2 * KB, 1024), 8) for b in range(2)]
        Wpp = [r3(A.bf16(108 * KB + b * 512, 256), 2) for b in range(2)]
        pst = [A.f32(109 * KB + b * KB, 256) for b in range(2)]
        sig = [A.f32(111 * KB + b * 2 * KB, 512) for b in range(2)]
        tmpf = [A.f32(115 * KB + b * 2 * KB, 512) for b in range(2)]
        rmsnorm_fm(rT, TOK, 0, None, 119 * KB, 127 * KB, "rT_")
        for t in range(16):
            b = t % 2
            ld(pst[b], psrc[t * 128:(t + 1) * 128, :], [], [f"pst{b}"])
            for j in range(2):
                tr(bank(4 + b, 128, j * 128), pst[b][:, j * 128:(j + 1) * 128], ident_f, [f"pst{b}", "cst"], pk(4 + b))
            cp("act", pT[:, :, t * 128:(t + 1) * 128], r3(bank(4 + b, 256), 2), pk(4 + b), [f"pT{t // 4}"])
        ld(Wg[0], r3(wb[f"pg{L}"][0], 8), [wkey(f"pg{L}", 0)], ["Wpg0"])
        ld(Wpp[0], r3(wb[f"pp{L}"][0], 2), [wkey(f"pp{L}", 0)], ["Wpp0"])
        n = 0
        for oc in range(8):
            b = oc % 2
            if oc + 1 < 8:
                ld(Wg[1 - b], r3(wb[f"pg{L}"][oc + 1], 8), [wkey(f"pg{L}", oc + 1)], [f"Wpg{1 - b}"])
                ld(Wpp[1 - b], r3(wb[f"pp{L}"][oc + 1], 2), [wkey(f"pp{L}", oc + 1)], [f"Wpp{1 - b}"])
            for t4 in range(4):
                gb_ = n % 2
                pb_ = 2 + n % 2
                s_ = n % 2
                n += 1
                mm(bank(gb_), [(Wg[b][:, kc, :], rT[:, kc, t4 * 512:(t4 + 1) * 512]) for kc in range(KC)],
                   [f"Wpg{b}", f"rT_{t4}"], pk(gb_))
                mm(bank(pb_), [(Wpp[b][:, j, :], pT[:, j, t4 * 512:(t4 + 1) * 512]) for j in range(2)],
                   [f"Wpp{b}", f"pT{t4}"], pk(pb_))
                act(sig[s_], bank(gb_), AF.Sigmoid, pk(gb_), [f"sig{s_}"])
                tt("dve", tmpf[s_], sig[s_], bank(pb_), ALU.mult, [f"sig{s_}"] + list(pk(pb_)), [f"plt{s_}"])
                tt("pool", hT[:, oc, t4 * 512:(t4 + 1) * 512], hT[:, oc, t4 * 512:(t4 + 1) * 512], tmpf[s_], ALU.add,
                   [f"plt{s_}", f"h{t4}"], [f"h{t4}"])

    QT_OFF = 96 * KB

    def phase_kv(u, with_q):
        a1 = r3(A.bf16(64 * KB, 8 * TOK), 8)
        qT = r3(A.bf16(QT_OFF, 8 * TOK), 8)
        kv_f = [A.f32(128 * KB + i * 2 * KB, 512) for i in range(6)]
        knb = [A.bf16(140 * KB + b * KB, 512) for b in range(2)]
        sqb = [A.bf16(142 * KB + b * KB, 512) for b in range(2)]
        kTg = [A.bf16(144 * KB + b * 4 * KB, TOK) for b in range(2)]
        Vst = r4(A.bf16(152 * KB, 4 * 16 * 128), 4, 16)
        Wk = [r3(A.bf16(168 * KB + b * 2 * KB, 1024), 8) for b in range(2)]
        Wv = r3(A.bf16(172 * KB, 8 * 512), 8)
        cs = [[A.f32(180 * KB + (b * 2 + k) * 2 * KB, 512) for k in range(2)] for b in range(2)]
        rmsnorm_fm(a1, TOK, 0, gain(0, 1), 188 * KB, 128 * KB + 8 * KB, "a1_")
        pg.barrier()
        a1k = [f"a1_{t}" for t in range(4)]
        heads = [("gk", g, False) for g in range(4)]
        if with_q:
            heads += [("gq", h, True) for h in range(8)]
        kraw = [kv_f[0], kv_f[1]]
        kn = [kv_f[2], kv_f[3]]
        rstd2 = [kv_f[4], kv_f[5]]
        t1 = [A.f32(188 * KB + b * 2 * KB, 512) for b in range(2)]
        t2 = [A.f32(192 * KB + b * 2 * KB, 512) for b in range(2)]
        work = [(hi, t4) for hi in range(len(heads)) for t4 in range(4)]

        def stage_a(n):
            hi, t4 = work[n]
            wn, hidx, isq = heads[hi]
            b = hi % 2
            cb = n % 2
            if t4 == 0:
                ld(Wk[b], r3(wb[wn][hidx], 8), [wkey(wn, hidx)], [f"Wk{b}"])
            gcol = gqk_f[:, 2:3] if isq else gqk_f[:, 1:2]
            ld(cs[cb][0], ropec[u, :, t4 * 512:(t4 + 1) * 512], [], [f"cos{cb}"])
            ld(cs[cb][1], ropes[u, :, t4 * 512:(t4 + 1) * 512], [], [f"sin{cb}"])
            mm(bank(cb), [(Wk[b][:, kc, :], a1[:, kc, t4 * 512:(t4 + 1) * 512]) for kc in range(KC)],
               [f"Wk{b}"] + a1k, pk(cb))
            cp("act", kraw[cb], bank(cb), pk(cb), [f"kraw{cb}"])
            act(sqb[cb], bank(cb), AF.Square, pk(cb), [f"sqb{cb}"])
            mm(bank(2 + cb), [(ones_b, sqb[cb])], [f"sqb{cb}", "ones_b"], pk(2 + cb))
            act(rstd2[cb], bank(2 + cb), AF.Sqrt, pk(2 + cb), [f"krs{cb}"], scale=1.0 / 128, bias=sm_f[:, 0:1])
            pg.op("dve", lambda e, r=rstd2[cb]: e.reciprocal(r, r), [f"krs{cb}"], [f"krs{cb}"])
            stt(kn[cb], kraw[cb], gcol, rstd2[cb], ALU.mult, ALU.mult, [f"kraw{cb}", f"krs{cb}", "gqk", "gqs"], [f"kn{cb}"])
            cp("pool", knb[cb], kn[cb], [f"kn{cb}"], [f"knb{cb}"])

        def stage_b(n):
            hi, t4 = work[n]
            wn, hidx, isq = heads[hi]
            b = hi % 2
            cb = n % 2
            mm(bank(4 + cb), [(rm_b, knb[cb])], [f"knb{cb}", "rm_b"], pk(4 + cb))
            tt("pool", t1[cb], kn[cb], cs[cb][0], ALU.mult, [f"kn{cb}", f"cos{cb}"], [f"t1{cb}"])
            tt("dve", t2[cb], bank(4 + cb), cs[cb][1], ALU.mult, list(pk(4 + cb)) + [f"sin{cb}"], [f"t2{cb}"])
            if isq:
                tt("dve", qT[:, hidx, t4 * 512:(t4 + 1) * 512], t1[cb], t2[cb], ALU.add, [f"t1{cb}", f"t2{cb}"], [f"qT{hidx}"])
            else:
                tt("dve", kTg[b][:, t4 * 512:(t4 + 1) * 512], t1[cb], t2[cb], ALU.add, [f"t1{cb}", f"t2{cb}"], [f"kTg{b}"])
                if t4 == 3:
                    ld(kvs[u, hidx, 0], kTg[b], [f"kTg{b}"], [f"kvs{u}_{hidx}_0"], q="pool")

        stage_a(0)
        for n in range(len(work)):
            if n + 1 < len(work):
                stage_a(n + 1)
            stage_b(n)
        ld(Wv, r3(wb["gv"][0], 8), [wkey("gv", 0)], ["Wv"])
        for t in range(16):
            pb = 6 + t % 2
            mm(bank(pb), [(a1[:, kc, t * 128:(t + 1) * 128], Wv[:, kc, :]) for kc in range(KC)], ["Wv"] + a1k, pk(pb))
            cp("act" if t % 2 else "dve", Vst[:, :, t, :], r3(bank(pb), 4), pk(pb), ["Vst"])
        for g in range(4):
            ld(kvs[u, g, 1], Vst[:, g].rearrange("p t d -> p (t d)"), ["Vst"], [f"kvs{u}_{g}_1"], q="pool")

    OT1_OFF = 64 * KB

    def phase_attn(u, key_units):
        qT = r3(A.bf16(QT_OFF, 8 * TOK), 8)
        oT = r3(A.bf16(OT1_OFF, 8 * TOK), 8)
        ring = [(A.bf16(128 * KB + b * 8 * KB, TOK), r3(A.bf16(128 * KB + b * 8 * KB + 4 * KB, TOK), 16))
                for b in range(3)]
        NPT = 6
        PT = [A.bf16(152 * KB + b * KB, 512) for b in range(NPT)]
        rc = [A.f32(164 * KB + b * 2 * KB, 512) for b in range(2)]
        seq = [(g, q4, ku) for g in range(4) for q4 in range(4) for ku in key_units]

        def load(idx):
            g, q4, ku = seq[idx]
            b = idx % 3
            ld(ring[b][0], kvs[ku, g, 0], [f"kvs{ku}_{g}_0"], [f"rk{b}"])
            ld(ring[b][1].rearrange("p t d -> p (t d)"), kvs[ku, g, 1], [f"kvs{ku}_{g}_1"], [f"rv{b}"])
        its = [(idx, kt, hh) for idx in range(len(seq)) for kt in range(16) for hh in range(2)]

        def emit_s(n):
            idx, kt, hh = its[n]
            g, q4, ku = seq[idx]
            b = idx % 3
            h = 2 * g + hh
            sbk = 4 + n % 4
            mm(bank(sbk), [(ring[b][0][:, kt * 128:(kt + 1) * 128], qT[:, h, q4 * 512:(q4 + 1) * 512])],
               [f"rk{b}", f"qT{h}"], pk(sbk))
            act(PT[n % NPT], bank(sbk), AF.Exp, pk(sbk), [f"PTa{n % NPT}"])

        PP = [[A.bf16(160 * KB + (hh * 2 + m) * KB, 512) for m in range(2)] for hh in range(2)]
        pending = []

        def flush(upto):
            while pending and pending[0][0] <= upto:
                _, fn_ = pending.pop(0)
                fn_()

        def emit_pv(n):
            idx, kt, hh = its[n]
            g, q4, ku = seq[idx]
            b = idx % 3
            first = (ku == key_units[0])
            last = (ku == key_units[-1])
            st = first and kt == 0
            sp_ = last and kt == 15
            pt = PT[n % NPT]
            Vt = ring[b][1]
            flush(n)
            pg.op("pe", lambda e, hh=hh, pt=pt, Vt=Vt, kt=kt, st=st, sp_=sp_:
                  e.matmul(bank(hh), Vt[:, kt, :], pt, start=st, stop=sp_),
                  [f"PTa{n % NPT}", f"rv{b}"], pk(hh))
            if kt % 2 == 1:
                m_ = (kt // 2) % 2
                pp = PP[hh][m_]
                tt("pool", pp, PT[n % NPT], PT[(n - 2) % NPT], ALU.add, [f"PTa{n % NPT}", f"PTa{(n - 2) % NPT}"], [f"PP{hh}{m_}"])
                st2 = first and kt == 1
                sp2 = last and kt == 15

                def ones_mm(hh=hh, pp=pp, st2=st2, sp2=sp2, m_=m_):
                    pg.op("pe", lambda e: e.matmul(bank(2 + hh), ones_b, pp, start=st2, stop=sp2),
                          [f"PP{hh}{m_}", "ones_b"], pk(2 + hh))
                pending.append((n + 3, ones_mm))
            if sp_:
                flush(1 << 60)
                h = 2 * g + hh
                pg.op("dve", lambda e, hh=hh: e.reciprocal(rc[hh], bank(2 + hh)), pk(2 + hh), [f"rc{hh}"])
                tt("dve", oT[:, h, q4 * 512:(q4 + 1) * 512], bank(hh), rc[hh], ALU.mult,
                   list(pk(hh)) + [f"rc{hh}"], [f"oT{h}"])
            if kt == 0 and hh == 0 and idx + 2 < len(seq):
                load(idx + 2)

        load(0)
        if len(seq) > 1:
            load(1)
        AHEAD = 2
        for n in range(min(AHEAD, len(its))):
            emit_s(n)
        for n in range(len(its)):
            if n + AHEAD < len(its):
                emit_s(n + AHEAD)
            emit_pv(n)

    def phase_out(dst):
        hT = hT_view()
        ost = [A.f32(64 * KB + b * 4 * KB, 1024) for b in range(2)]
        for t in range(16):
            b = t % 2
            for kc in range(KC):
                tr(bank(2 * b + kc // 4, 128, (kc % 4) * 128), hT[:, kc, t * 128:(t + 1) * 128], ident_f,
                   [f"h{t // 4}", "cst"], pk(2 * b + kc // 4))
            cp("act" if b else "dve", ost[b], PS[:, 2 * b * 512: (2 * b + 2) * 512], pk(2 * b, 2 * b + 1), [f"ost{b}"])
            ld(dst[t * 128:(t + 1) * 128, :], ost[b], [f"ost{b}"], [f"y{t}"], q="pool")

    def layer0(u):
        phase_x(u)
        if cfg.get("stop") == "x":
            pg.barrier()
            return
        phase_na(u)
        pg.barrier()
        if cfg.get("stop") == "na":
            oT = r3(A.bf16(OT0_OFF, 8 * TOK), 8)
            stg = [A.f32(100 * KB + b * 2 * KB, 512) for b in range(2)]
            ypv = yp.rearrange("(k p a) f -> k p (a f)", k=8, p=128)
            n = 0
            for kc in range(8):
                for t4 in range(4):
                    b = n % 2
                    n += 1
                    cp("dve", stg[b], oT[:, kc, t4 * 512:(t4 + 1) * 512], [], [f"dstg{b}"])
                    ld(ypv[kc][:, t4 * 512:(t4 + 1) * 512], stg[b], [f"dstg{b}"], [f"dy{n}"], q="pool")
            pg.barrier()
            return
        phase_wo_pn(u, 0, "na_o", OT0_OFF, 96 * KB, 128 * KB, 132 * KB, 140 * KB, True)
        pg.barrier()
        if cfg.get("stop") == "wo":
            return
        phase_ffn(u, 0)
        pg.barrier()
        if cfg.get("stop") == "ffn":
            return
        phase_ple(u, 0, p0u[u])
        pg.barrier()

    def layer1_rest(u, pidx, dst):
        phase_wo_pn(u, 1, "go", OT1_OFF, 96 * KB, 128 * KB, 132 * KB, 140 * KB, False)
        pg.barrier()
        phase_ffn(u, 1)
        pg.barrier()
        phase_ple(u, 1, p1u[pidx])
        pg.barrier()
        phase_out(dst)
        pg.barrier()

    for u in units_l0:
        layer0(u)
        if dbg == "l0" and u == units_l0[-1]:
            if cfg.get("stop") != "na":
                phase_out(yp)
            pg.barrier()
            break
        if not do_l1:
            continue
        is_q = u in (U_OWN, U_PROMPT)
        phase_kv(u, is_q)
        pg.barrier()
        if u == U_OWN:
            phase_attn(u, list(range(8)))
            pg.barrier()
            layer1_rest(u, 0, ys)
        elif u == U_PROMPT:
            phase_attn(u, [U_PROMPT])
            pg.barrier()
            layer1_rest(u, 1, yp)
    pg.barrier()

    for s in pg.sem_names:
        pg.handles[s] = nc.alloc_semaphore(f"s_{s}")
    with nc.Block() as block:
        pg.emit(block)
    return nc, pg


def _na_static():
    qc = np.arange(64)[:, None]
    kc = np.arange(64)[None, :]
    wstart = np.clip(qc - 8, 0, 48)
    colmask = (kc >= wstart) & (kc < wstart + 16)
    return colmask


def _utab(rpb, interior=False):
    colmask = _na_static()
    out = np.full((8, 128, 2, ND, 64), NEG, np.float32)
    kp = np.arange(128)
    kcol = kp % 64
    khalf = kp // 64
    qcs = np.arange(64)
    ci = np.clip(kcol[:, None] - qcs[None, :] + 15, 0, 30)
    cm = colmask.T[kcol, :]
    for c in range(8):
        for hd in range(2):
            h = 2 * c + hd
            for di in range(ND):
                d = di - 7
                ri = d + 7 + khalf
                ok = (ri >= 0) & (ri <= 14)
                ric = np.clip(ri, 0, 14)
                vals = rpb[h][ric[:, None], ci]
                if interior:
                    ok = ok & (ri >= 3) & (ri <= 10)
                out[c, :, hd, di, :] = np.where(cm & ok[:, None], vals, np.float32(NEG))
    return out.reshape(8, 128, 2 * ND * 64)


def _rv_masks(g0, R):
    out = np.zeros((128, 48), np.float32)
    for e, l in enumerate(list(range(4, 8)) + list(range(32, 36))):
        g = g0 + l
        rs = min(max(g - 4, 0), R - 8)
        tiles = range(0, 6) if l < 8 else range(14, 20)
        for j, t in enumerate(tiles):
            for half in range(2):
                gk = g0 + 2 * t + half
                valid = (rs <= gk < rs + 8)
                if not valid:
                    out[half * 64:(half + 1) * 64, e * 6 + j] = NEG
    return out


def _rope_tables(tok0):
    t = np.arange(tok0, tok0 + TOK)
    row = (t // 64).astype(np.float32)
    col = (t % 64).astype(np.float32)
    inv = (np.float32(10000.0) ** (-np.arange(0, 64, 2, dtype=np.float32) / np.float32(64))).astype(np.float32)
    ang = np.zeros((128, TOK), np.float32)
    for d in range(128):
        f = d % 32
        pos = row if d < 64 else col
        ang[d] = (pos * inv[f]).astype(np.float32)
    return np.cos(ang).astype(np.float32), np.sin(ang).astype(np.float32)


def _consts():
    c = np.zeros((128, 387), np.float32)
    c[:, 0:128] = np.eye(128, dtype=np.float32)
    c[:, 128:256] = 1.0
    rm = np.zeros((128, 128), np.float32)
    for m in range(128):
        if (m % 64) < 32:
            rm[m + 32, m] = -1.0
        else:
            rm[m - 32, m] = 1.0
    c[:, 256:384] = rm
    c[0:64, 385] = NEG
    c[64:128, 386] = NEG
    return c


_CACHE = {}


def _prep_inputs(inp):
    xs = inp["x_sample"][0]
    xpad = np.concatenate([np.zeros((256, D), np.float32), xs, np.zeros((256, D), np.float32)], axis=0)
    zeros_h = np.zeros((256, D), np.float32)
    g_all = np.stack([inp["mix_pre_norm"], inp["mix_post_norm"], inp["ffn_pre_norm"], inp["ffn_post_norm"]], 0)
    gains = np.ascontiguousarray(g_all.reshape(4, 2, 8, 128).transpose(3, 0, 1, 2).reshape(128, 64)).astype(np.float32)
    gqk = np.ascontiguousarray(np.stack([inp["gqa_q_norm"][0], inp["gqa_k_norm"][0]], 1)).astype(np.float32)
    utab = _utab(np.asarray(inp["na_rpb"][0], np.float32))
    utab2 = _utab(np.asarray(inp["na_rpb"][0], np.float32), interior=True)
    cst = _consts()
    rope_p = _rope_tables(0)
    maps = []
    for c in range(NCORES):
        chunks = [(c + 1 + u) % 8 for u in range(7)] + [c]
        xu = np.empty((NUNITS, HTOK, D), np.float32)
        p0 = np.empty((NUNITS, TOK, 256), np.float32)
        rv = np.empty((NUNITS, 128, 48), np.float32)
        rc = np.empty((NUNITS, 128, TOK), np.float32)
        rs = np.empty((NUNITS, 128, TOK), np.float32)
        for u, gj in enumerate(chunks):
            xu[u] = xpad[gj * TOK: gj * TOK + HTOK]
            p0[u] = inp["p_sample"][0, 0, gj * TOK:(gj + 1) * TOK]
            rv[u] = _rv_masks(32 * gj - 4, 256)
            key = ("rope", gj)
            if key not in _CACHE:
                _CACHE[key] = _rope_tables(gj * TOK)
            rc[u], rs[u] = _CACHE[key]
        xu[U_PROMPT] = np.concatenate([zeros_h, inp["x_prompt"][c], zeros_h], 0)
        p0[U_PROMPT] = inp["p_prompt"][0, c]
        rv[U_PROMPT] = _rv_masks(-4, 32)
        rc[U_PROMPT], rs[U_PROMPT] = rope_p
        p1 = np.stack([inp["p_sample"][1, 0, c * TOK:(c + 1) * TOK], inp["p_prompt"][1, c]], 0)
        maps.append({
            "xu": xu, "p0u": p0, "p1u": np.ascontiguousarray(p1), "rv": rv, "ropec": rc, "ropes": rs,
            "utab": utab, "utab2": utab2, "cst": cst, "gains": gains, "gqk": gqk,
            "na_w_qkv": inp["na_w_qkv"][0], "na_w_o": inp["na_w_o"][0],
            "gqa_w_qkv": inp["gqa_w_qkv"][0], "gqa_w_o": inp["gqa_w_o"][0],
            "ffn_w_gate_up": inp["ffn_w_gate_up"], "ffn_w_down": inp["ffn_w_down"],
            "ple_w_gate": inp["ple_w_gate"], "ple_w_proj": inp["ple_w_proj"],
        })
    return maps


def kernel(**inputs):
    inp = {k: np.asarray(v) for k, v in inputs.items()}
    maps = _prep_inputs(inp)
    nc, pg = build()
    res = run_bass_kernel_spmd(nc, maps, core_ids=list(range(NCORES)))
    y_prompt = np.stack([np.asarray(res.results[c]["yp"], np.float32) for c in range(NCORES)], 0)
    y_sample = np.concatenate([np.asarray(res.results[c]["ys"], np.float32) for c in range(NCORES)], 0)[None]
    return (y_prompt, y_sample)
```

```python
import numpy as np
import concourse.bass as bass
import concourse.mybir as mybir
from concourse.bass_utils import run_bass_kernel_spmd

F32 = mybir.dt.float32
BF16 = mybir.dt.bfloat16
AF = mybir.ActivationFunctionType
ALU = mybir.AluOpType

NCORES = 8
D = 1024
KC = 8
TOK = 2048
HTOK = 2560
DFF = 2816
NF = 22
EPS = 1e-6
NEG = -30000.0
NUNITS = 9
U_OWN = 7
U_PROMPT = 8
ND = 14
KB = 1024


class Prog:
    ENG = ("pe", "act", "dve", "pool", "sp")

    def __init__(self, nc, n_dma_sp=24, n_dma_pool=8):
        self.nc = nc
        self.ops = {e: [] for e in self.ENG}
        self.cnt = {}
        self.waited = {e: {} for e in self.ENG}
        self.last_w = {}
        self.readers = {}
        self.sem_names = ["pe", "act", "dve", "pool"]
        self.dma_slots = {"sp": [f"dsp{i}" for i in range(n_dma_sp)],
                          "pool": [f"dpl{i}" for i in range(n_dma_pool)]}
        self.dma_next = {"sp": 0, "pool": 0}
        for q in self.dma_slots.values():
            self.sem_names += q
        for s in self.sem_names:
            self.cnt[s] = 0
        self.handles = {}
        self.nops = 0

    def _deps(self, reads, writes):
        deps = set()
        for k in reads:
            if k in self.last_w:
                deps.add(self.last_w[k])
        for k in writes:
            if k in self.last_w:
                deps.add(self.last_w[k])
            for r in self.readers.get(k, ()):
                deps.add(r)
        return deps

    def _record(self, reads, writes, done):
        for k in reads:
            self.readers.setdefault(k, []).append(done)
        for k in writes:
            self.last_w[k] = done
            self.readers[k] = []

    def _waits(self, eng, deps):
        need = {}
        for (s, v) in deps:
            if eng == "pe" and s == "pe":
                continue
            if self.waited[eng].get(s, 0) < v:
                need[s] = max(need.get(s, 0), v)
        for s, v in need.items():
            self.waited[eng][s] = v
        return list(need.items())

    def op(self, eng, fn, reads=(), writes=()):
        deps = self._deps(reads, writes)
        waits = self._waits(eng, deps)
        self.cnt[eng] += 1
        done = (eng, self.cnt[eng])
        self.ops[eng].append((waits, fn, (eng, 1)))
        self._record(reads, writes, done)
        self.nops += 1
        return done

    def dma(self, q, fn, reads=(), writes=()):
        slots = self.dma_slots[q]
        s = slots[self.dma_next[q] % len(slots)]
        self.dma_next[q] += 1
        deps = self._deps(reads, writes)
        if self.cnt[s] > 0:
            deps.add((s, self.cnt[s]))
        waits = self._waits(q, deps)
        self.cnt[s] += 16
        done = (s, self.cnt[s])
        self.ops[q].append((waits, fn, (s, 16)))
        self._record(reads, writes, done)
        self.nops += 1
        return done

    def barrier(self, engines=None):
        deps = set((s, c) for s, c in self.cnt.items() if c > 0)
        for e in (engines or self.ENG):
            waits = self._waits(e, set(deps))
            if waits:
                self.ops[e].append((waits, None, None))
        self.last_w = {}
        self.readers = {}

    def emit(self, block):
        H = self.handles

        class FirstCatcher:
            def __init__(self, e):
                self._e = e
                self.first = None

            def __getattr__(self, name):
                attr = getattr(self._e, name)
                if not callable(attr):
                    return attr

                def w(*a, **k):
                    r = attr(*a, **k)
                    if self.first is None and hasattr(r, "then_inc"):
                        self.first = r
                    return r
                return w

        def run(eng_name):
            def body(e):
                for (waits, fn, sig) in self.ops[eng_name]:
                    if fn is None:
                        for (s, v) in waits:
                            e.wait_ge(H[s], v)
                        continue
                    for (s, v) in waits[:-1]:
                        e.wait_ge(H[s], v)
                    fc = FirstCatcher(e)
                    ins = fn(fc)
                    if waits:
                        s, v = waits[-1]
                        fc.first._wait_ge(H[s], v)
                    if sig is not None:
                        ins.then_inc(H[sig[0]], sig[1])
            return body

        block.tensor(run("pe"))
        block.scalar(run("act"))
        block.vector(run("dve"))
        block.gpsimd(run("pool"))
        block.sync(run("sp"))


class Arena:
    def __init__(self, nc, nbytes):
        self.t = nc.alloc_sbuf_tensor("arena", [128, nbytes // 4], F32)
        self.t16 = self.t.bitcast(BF16)
        self.nbytes = nbytes

    def f32(self, off, n):
        assert off % 4 == 0 and off + 4 * n <= self.nbytes, (off, n)
        return self.t[:, off // 4: off // 4 + n]

    def bf16(self, off, n):
        assert off % 2 == 0 and off + 2 * n <= self.nbytes, (off, n)
        return self.t16[:, off // 2: off // 2 + n]


def r3(ap, a):
    return ap.rearrange("p (a b) -> p a b", a=a)


def r4(ap, a, b):
    return ap.rearrange("p (a b c) -> p a b c", a=a, b=b)


def na_row_spec(i):
    l = i + 4
    if l < 8:
        tiles = list(range(0, 6))
        masks = [3 + (l - 4) * 6 + j for j in range(6)]
    elif l >= 32:
        tiles = list(range(14, 20))
        masks = [3 + 24 + (l - 32) * 6 + j for j in range(6)]
    elif l % 2 == 0:
        tiles = list(range(l // 2 - 2, l // 2 + 2))
        masks = [0, 0, 0, 0]
    else:
        t0 = (l - 5) // 2
        tiles = list(range(t0, t0 + 5))
        masks = [1, 0, 0, 0, 2]
    return l, tiles, masks


def build(cfg=None):
    cfg = cfg or {}
    units_l0 = cfg.get("units", list(range(NUNITS)))
    do_l1 = cfg.get("do_l1", True)
    dbg = cfg.get("dbg", None)

    nc = bass.Bass("TRN2", target_bir_lowering=False)
    pg = Prog(nc)

    def din(name, shape, dt=F32):
        return nc.dram_tensor(name, list(shape), dt, kind="ExternalInput").ap()

    xu = din("xu", [NUNITS, HTOK, D])
    p0u = din("p0u", [NUNITS, TOK, 256])
    p1u = din("p1u", [2, TOK, 256])
    rvd = din("rv", [NUNITS, 128, 48])
    ropec = din("ropec", [NUNITS, 128, TOK])
    ropes = din("ropes", [NUNITS, 128, TOK])
    utab = din("utab", [8, 128, 2 * ND * 64])
    utab2 = din("utab2", [8, 128, 2 * ND * 64])
    cst = din("cst", [128, 387])
    gains = din("gains", [128, 64])
    gqk = din("gqk", [128, 2])
    w_na_qkv = din("na_w_qkv", [D, 3072])
    w_na_o = din("na_w_o", [D, D])
    w_gqa_qkv = din("gqa_w_qkv", [D, 2048])
    w_gqa_o = din("gqa_w_o", [D, D])
    w_gu = din("ffn_w_gate_up", [2, D, 2 * DFF])
    w_dn = din("ffn_w_down", [2, DFF, D])
    w_pg = din("ple_w_gate", [2, D, D])
    w_pp = din("ple_w_proj", [2, 256, D])

    yp = nc.dram_tensor("yp", [TOK, D], F32, kind="ExternalOutput").ap()
    ys = nc.dram_tensor("ys", [TOK, D], F32, kind="ExternalOutput").ap()

    def wscratch(name, npanel, kcn, ow):
        return nc.dram_tensor(name, [npanel, 128, kcn * ow], BF16).ap()

    wb = {
        "na_q": wscratch("wb_na_q", 8, 8, 128), "na_k": wscratch("wb_na_k", 8, 8, 128),
        "na_v": wscratch("wb_na_v", 2, 8, 512), "na_o": wscratch("wb_na_o", 8, 8, 128),
        "gq": wscratch("wb_gq", 8, 8, 128), "gk": wscratch("wb_gk", 4, 8, 128),
        "gv": wscratch("wb_gv", 1, 8, 512), "go": wscratch("wb_go", 8, 8, 128),
    }
    for L in range(2):
        wb[f"fg{L}"] = wscratch(f"wb_fg{L}", NF, 8, 128)
        wb[f"fu{L}"] = wscratch(f"wb_fu{L}", NF, 8, 128)
        wb[f"fd{L}"] = wscratch(f"wb_fd{L}", 8, NF, 128)
        wb[f"pg{L}"] = wscratch(f"wb_pg{L}", 8, 8, 128)
        wb[f"pp{L}"] = wscratch(f"wb_pp{L}", 8, 2, 128)
    kvs = nc.dram_tensor("kvs", [NUNITS, 4, 2, 128, TOK], BF16).ap()

    A = Arena(nc, 206 * KB)
    PB = 196 * KB
    cst_f = A.f32(PB, 387); PB += 387 * 4 + 4
    ident_f = cst_f[:, 0:128]
    ident_b = A.bf16(PB, 128); PB += 256
    ones_b = A.bf16(PB, 128); PB += 256
    rm_b = A.bf16(PB, 128); PB += 256
    gains_f = A.f32(PB, 64); PB += 256
    gqk_f = A.f32(PB, 4); PB += 16
    rv_f = A.f32(PB, 51); PB += 208
    sm_f = A.f32(PB, 64); PB += 256
    assert PB <= 206 * KB

    def gain(kind, layer):
        i = (kind * 2 + layer) * 8
        return gains_f[:, i:i + 8]

    PS = nc.alloc_psum_tensor("psum_all", [128, 4096], F32)
    PS16 = PS.bitcast(BF16)

    def bank(i, n=512, o=0):
        return PS[:, i * 512 + o: i * 512 + o + n]

    def bank16(i, n=1024, o=0):
        return PS16[:, i * 1024 + o: i * 1024 + o + n]

    def pk(*banks):
        return tuple(f"ps{b}" for b in banks)

    def mm(out, pairs, reads, writes):
        def fn(e):
            n = len(pairs)
            ins = None
            for i, (l, r) in enumerate(pairs):
                ins = e.matmul(out, l, r, start=(i == 0), stop=(i == n - 1))
            return ins
        return pg.op("pe", fn, reads, writes)

    def tr(out, in_, idn, reads, writes):
        return pg.op("pe", lambda e: e.transpose(out, in_, idn), reads, writes)

    def act(out, in_, func, reads, writes, **kw):
        return pg.op("act", lambda e: e.activation(out=out, in_=in_, func=func, **kw), reads, writes)

    def tt(eng, out, a, b, op, reads, writes):
        return pg.op(eng, lambda e: e.tensor_tensor(out, a, b, op), reads, writes)

    def ts(eng, out, a, s1, s2, op0, op1, reads, writes):
        return pg.op(eng, lambda e: e.tensor_scalar(out, a, s1, s2, op0, op1), reads, writes)

    def stt(out, a, s, b, op0, op1, reads, writes):
        return pg.op("dve", lambda e: e.scalar_tensor_tensor(out, a, s, b, op0, op1), reads, writes)

    def cp(eng, out, in_, reads, writes):
        if eng == "act":
            return pg.op("act", lambda e: e.copy(out, in_), reads, writes)
        return pg.op(eng, lambda e: e.tensor_copy(out, in_), reads, writes)

    def ld(out, in_, reads, writes, q="sp"):
        return pg.dma(q, lambda e: e.dma_start(out=out, in_=in_), reads, writes)

    ld(cst_f, cst, [], ["cst"])
    ld(gains_f, gains, [], ["gains"])
    ld(gqk_f[:, 0:2], gqk, [], ["gqk"])
    cp("dve", ident_b, cst_f[:, 0:128], ["cst"], ["ident_b"])
    cp("dve", ones_b, cst_f[:, 128:256], ["cst"], ["ones_b"])
    cp("dve", rm_b, cst_f[:, 256:384], ["cst"], ["rm_b"])
    cp("dve", rv_f[:, 0:3], cst_f[:, 384:387], ["cst"], ["rvc"])
    ts("dve", gqk_f[:, 2:3], gqk_f[:, 0:1], float(128 ** -0.5), None, ALU.mult, ALU.bypass, ["gqk"], ["gqs"])

    conv = []
    for c in range(8):
        conv.append((w_na_qkv, c * 128, 8, 128, "na_q", c))
        conv.append((w_na_qkv, 1024 + c * 128, 8, 128, "na_k", c))
    for g in range(2):
        conv.append((w_na_qkv, 2048 + g * 512, 8, 512, "na_v", g))
    for c in range(8):
        conv.append((w_na_o, c * 128, 8, 128, "na_o", c))
    for L in range(2):
        for f in range(NF):
            conv.append((w_gu[L], f * 128, 8, 128, f"fg{L}", f))
            conv.append((w_gu[L], DFF + f * 128, 8, 128, f"fu{L}", f))
        for c in range(8):
            conv.append((w_dn[L], c * 128, NF, 128, f"fd{L}", c))
            conv.append((w_pg[L], c * 128, 8, 128, f"pg{L}", c))
            conv.append((w_pp[L], c * 128, 2, 128, f"pp{L}", c))
    for c in range(8):
        conv.append((w_gqa_qkv, c * 128, 8, 128, "gq", c))
    for c in range(4):
        conv.append((w_gqa_qkv, 1024 + c * 128, 8, 128, "gk", c))
    conv.append((w_gqa_qkv, 1536, 8, 512, "gv", 0))
    for c in range(8):
        conv.append((w_gqa_o, c * 128, 8, 128, "go", c))

    NSTG = 8
    stg_f = [A.f32(i * 16 * KB, 4096) for i in range(NSTG)]
    stg_b = [A.bf16(128 * KB + i * 8 * KB, 4096) for i in range(NSTG)]
    ceng = ["pool", "dve", "act"]
    for i, (src, col0, kcn, ow, name, panel) in enumerate(conv):
        b = i % NSTG
        n = kcn * ow
        srcv = src.rearrange("(kc p) o -> p kc o", p=128)[:, :, col0:col0 + ow]
        ld(r3(stg_f[b][:, 0:n], kcn), srcv, [], [f"stgf{b}"])
        cp(ceng[i % 3], stg_b[b][:, 0:n], stg_f[b][:, 0:n], [f"stgf{b}"], [f"stgb{b}"])
        ld(wb[name][panel], stg_b[b][:, 0:n], [f"stgb{b}"], [f"wb_{name}_{panel}"], q="pool")
    pg.barrier()
    WB_KEYS = {}

    def wkey(name, panel):
        return f"wb_{name}_{panel}"

    HT_OFF = 0

    def hT_view():
        return r3(A.f32(HT_OFF, 8 * TOK), 8)

    def norm_stats(src_fn, ntok, sq_b, rstd_f, psb, inv_n, rkey):
        for kc in range(KC):
            s_ap, rk = src_fn(kc)
            eng = "act" if kc % 2 == 0 else "pool"
            if eng == "act":
                act(sq_b[:, kc, 0:ntok], s_ap, AF.Square, rk, [f"nsq{kc}"])
            else:
                tt("pool", sq_b[:, kc, 0:ntok], s_ap, s_ap, ALU.mult, rk, [f"nsq{kc}"])
        mm(bank(psb, ntok), [(ones_b, sq_b[:, kc, 0:ntok]) for kc in range(KC)],
           [f"nsq{kc}" for kc in range(KC)] + ["ones_b"], pk(psb))
        act(rstd_f[:, 0:ntok], bank(psb, ntok), AF.Sqrt, pk(psb), [rkey], scale=inv_n, bias=sm_f[:, 0:1])
        pg.op("dve", lambda e: e.reciprocal(rstd_f[:, 0:ntok], rstd_f[:, 0:ntok]), [rkey], [rkey])

    pg.op("pool", lambda e: e.memset(sm_f[:, 0:1], EPS), [], ["eps"])
    pg.barrier()

    AT_OFF = 0
    OT0_OFF = 64 * KB

    def phase_x(u):
        aT = r3(A.bf16(AT_OFF, 8 * HTOK), 8)
        xs_f = [A.f32(141 * KB + b * 4 * KB, 1024) for b in range(2)]
        xs_b = [A.bf16(149 * KB + b * 2 * KB, 1024) for b in range(2)]
        junk = A.bf16(153 * KB, 1024)
        g = gain(0, 0)
        for t in range(HTOK // 128):
            b = t % 2
            ld(xs_f[b], xu[u, t * 128:(t + 1) * 128, :], [], [f"xsf{b}"])
            act(junk, xs_f[b], AF.Square, [f"xsf{b}"], ["junk", f"ss{b}"], accum_out=sm_f[:, 2 + b:3 + b])
            act(sm_f[:, 4 + b:5 + b], sm_f[:, 2 + b:3 + b], AF.Sqrt, [f"ss{b}", "eps"], [f"sr{b}"],
                scale=1.0 / D, bias=sm_f[:, 0:1])
            pg.op("dve", lambda e, b=b: e.reciprocal(sm_f[:, 6 + b:7 + b], sm_f[:, 4 + b:5 + b]),
                  [f"sr{b}"], [f"rs{b}"])
            ts("dve", xs_b[b], xs_f[b], sm_f[:, 6 + b:7 + b], None, ALU.mult, ALU.bypass,
               [f"xsf{b}", f"rs{b}"], [f"xsb{b}"])
            for kc in range(KC):
                tr(bank16(b, 128, kc * 128), xs_b[b][:, kc * 128:(kc + 1) * 128], ident_b,
                   [f"xsb{b}", "ident_b"], pk(b))
            gb = g.unsqueeze(2).to_broadcast([128, 8, 128])
            tt("dve", aT[:, :, t * 128:(t + 1) * 128], r3(bank16(b, 1024), 8), gb, ALU.mult,
               list(pk(b)) + ["gains"], [f"aT{t // 4}"])

    def phase_na(u):
        aT = r3(A.bf16(AT_OFF, 8 * HTOK), 8)
        oT = r3(A.bf16(OT0_OFF, 8 * TOK), 8)
        qT = [A.bf16(40 * KB + b * 4 * KB, TOK) for b in range(2)]
        kT = [A.bf16(48 * KB + b * 5 * KB, HTOK) for b in range(2)]
        Vg = [r3(A.bf16(170 * KB + k * 5248, 20 * 130), 20) for k in range(4)]
        Wvg = r3(A.bf16(96 * KB, 8 * 512), 8)
        Wp = [[r3(A.bf16(107 * KB + (b * 3 + k) * 2 * KB, 1024), 8) for k in range(3)] for b in range(2)]
        Ut = [A.f32(119 * KB + b * 7 * KB, 2 * ND * 64) for b in range(2)]
        Ui = [A.f32(156 * KB + b * 7 * KB, 2 * ND * 64) for b in range(2)]
        tmp = [A.f32(133 * KB + b * 4 * KB, 1024) for b in range(2)]
        PT = [A.bf16(60 * KB + b * 2 * KB, 1024) for b in range(2)]
        otok = [A.bf16(58 * KB + b * 256, 128) for b in range(2)]
        rcp = A.f32(59 * KB, 8)
        ld(rv_f[:, 3:51], rvd[u], [], ["rvu"])
        for k in range(4):
            pg.op("pool", lambda e, k=k: e.memset(Vg[k][:, :, 64:65], 1.0), [], [f"V{k}"])
            pg.op("pool", lambda e, k=k: e.memset(Vg[k][:, :, 129:130], 1.0), [], [f"V{k}"])
        aT_keys = [f"aT{i}" for i in range(5)]

        def load_w(c):
            b = c % 2
            for k, nm in enumerate(("na_q", "na_k")):
                ld(Wp[b][k], r3(wb[nm][c], 8), [wkey(nm, c)], [f"W{b}_{k}"])
            ld(Ut[b], utab[c], [], [f"U{b}"])
            ld(Ui[b], utab2[c], [], [f"Ui{b}"])

        load_w(0)
        for c in range(8):
            b = c % 2
            if c + 1 < 8:
                load_w(c + 1)
            for t4 in range(4):
                pb = t4 % 2
                mm(bank(pb), [(Wp[b][0][:, kc, :], aT[:, kc, 256 + t4 * 512: 256 + (t4 + 1) * 512]) for kc in range(KC)],
                   [f"W{b}_0"] + aT_keys, pk(pb))
                act(qT[b][:, t4 * 512:(t4 + 1) * 512], bank(pb), AF.Copy, pk(pb), [f"q{b}"], scale=0.125)
            for t5 in range(5):
                pb = t5 % 2
                mm(bank(pb), [(Wp[b][1][:, kc, :], aT[:, kc, t5 * 512:(t5 + 1) * 512]) for kc in range(KC)],
                   [f"W{b}_1"] + aT_keys, pk(pb))
                cp("act" if t5 % 2 else "dve", kT[b][:, t5 * 512:(t5 + 1) * 512], bank(pb), pk(pb), [f"k{b}"])
            if c % 4 == 0:
                ld(Wvg, r3(wb["na_v"][c // 4], 8), [wkey("na_v", c // 4)], ["Wvg"])
                vall = A.bf16(170 * KB, 4 * 2624).rearrange("p (k x) -> p k x", k=4)
                for t in range(20):
                    pb = 2 + t % 2
                    mm(bank(pb), [(aT[:, kc, t * 128:(t + 1) * 128], Wvg[:, kc, :]) for kc in range(KC)],
                       ["Wvg"] + aT_keys, pk(pb))
                    vout = vall[:, :, t * 130:(t + 1) * 130].rearrange("p k (h e) -> p k h e", h=2)[:, :, :, 0:64]
                    cp("dve" if t % 2 else "act", vout, bank(pb).rearrange("p (k h e) -> p k h e", k=4, h=2),
                       pk(pb), [f"V{k}" for k in range(4)])
            def row_qk(i, b=b):
                l, tiles, masks = na_row_spec(i)
                sb = 4 + 2 * (i % 2)

                def qk(e, b=b, i=i, tiles=tiles, sb=sb):
                    ins = None
                    for hd in range(2):
                        for j, t in enumerate(tiles):
                            ins = e.matmul(bank(sb + hd, 64, j * 64),
                                           kT[b][hd * 64:(hd + 1) * 64, t * 128:(t + 1) * 128],
                                           qT[b][hd * 64:(hd + 1) * 64, i * 64:(i + 1) * 64],
                                           start=True, stop=True)
                    return ins
                pg.op("pe", qk, [f"q{b}", f"k{b}"], pk(sb, sb + 1))

            def row_rest(i, b=b, c=c):
                l, tiles, masks = na_row_spec(i)
                nt = len(tiles)
                sb = 4 + 2 * (i % 2)
                pob = 2 + i % 2
                tb = i % 2
                skeys = pk(sb, sb + 1)
                d0 = 2 * tiles[0] - l + 7
                s_in = PS[:, sb * 512: sb * 512 + 1024].rearrange("p (h j q) -> p h j q", h=2, j=8)[:, :, 0:nt, :]
                edge = masks[0] >= 3
                Utab_ = Ut[b] if edge else Ui[b]
                ukey = f"U{b}" if edge else f"Ui{b}"
                u_in = Utab_.rearrange("p (h d q) -> p h d q", h=2, d=ND)[:, :, d0:d0 + 2 * nt - 1:2, :]
                t4v = tmp[tb].rearrange("p (h j q) -> p h j q", h=2, j=8)
                tt("dve", t4v[:, :, 0:nt, :], s_in, u_in, ALU.add, list(skeys) + [ukey], [f"tmp{tb}"])
                p_out = PT[tb].rearrange("p (h j q) -> p h j q", h=2, j=8)
                if edge:
                    for j in range(nt):
                        mcol = masks[j]
                        act(p_out[:, :, j:j + 1, :], t4v[:, :, j:j + 1, :],
                            AF.Exp, [f"tmp{tb}", "rvu", "rvc"], [f"PT{tb}"], bias=rv_f[:, mcol:mcol + 1])
                else:
                    act(p_out[:, :, 0:nt, :], t4v[:, :, 0:nt, :], AF.Exp, [f"tmp{tb}"], [f"PT{tb}"])

            def row_pv(i, b=b, c=c):
                l, tiles, masks = na_row_spec(i)
                nt = len(tiles)
                pob = 2 + i % 2
                tb = i % 2

                def pv(e, c=c, tiles=tiles, pob=pob, tb=tb, nt=nt):
                    ins = None
                    for hd in range(2):
                        for j, t in enumerate(tiles):
                            ins = e.matmul(bank(pob, 65, hd * 65)[0:64, :],
                                           PT[tb].rearrange("p (h j q) -> p h j q", h=2, j=8)[:, hd, j, :],
                                           Vg[c % 4][:, t, hd * 65:(hd + 1) * 65],
                                           start=(j == 0), stop=(j == nt - 1))
                    return ins
                pg.op("pe", pv, [f"PT{tb}", f"V{c % 4}"], pk(pob))

            def row_fin(i, b=b, c=c):
                pob = 2 + i % 2
                half = i % 2
                ob = (i // 2) % 2
                po3 = r3(bank(pob, 130), 2)[0:64]
                pg.op("dve", lambda e, po3=po3, half=half: e.reciprocal(rcp[0:64, half * 2:half * 2 + 2], po3[:, :, 64]),
                      pk(pob), [f"rcp{half}"])
                rb = rcp[0:64, half * 2:half * 2 + 2].unsqueeze(2).to_broadcast([64, 2, 64])
                tt("dve", r3(otok[ob], 2)[half * 64:(half + 1) * 64], po3[:, :, 0:64], rb, ALU.mult,
                   list(pk(pob)) + [f"rcp{half}"], [f"otok{ob}"])
                if half == 1:
                    tpb = ob
                    tr(bank16(tpb, 128), otok[ob], ident_b, [f"otok{ob}", "ident_b"], pk(tpb))
                    cp("act", oT[:, c, (i - 1) * 64:(i + 1) * 64], bank16(tpb, 128), pk(tpb), [f"oT{c}"])

            NROWS = cfg.get("na_rows", 32)
            for n in range(-2, NROWS + 1):
                if 0 <= n + 2 < NROWS:
                    row_qk(n + 2)
                if 0 <= n + 1 < NROWS:
                    row_rest(n + 1)
                if 0 <= n < NROWS:
                    row_pv(n)
                if 0 <= n - 1 < NROWS:
                    row_fin(n - 1)

    def proj8(wname, src3, src_keys, m3, ntok, woff, kcn=8, evac_scale=None, mtag="m"):
        Wd = [r3(A.bf16(woff + b * (kcn * 256), kcn * 128), kcn) for b in range(2)]
        ld(Wd[0], r3(wb[wname][0], kcn), [wkey(wname, 0)], ["Wd0"])
        n = 0
        for oc in range(8):
            b = oc % 2
            if oc + 1 < 8:
                ld(Wd[1 - b], r3(wb[wname][oc + 1], kcn), [wkey(wname, oc + 1)], [f"Wd{1 - b}"])
            for t4 in range(ntok // 512):
                pb = n % 4
                n += 1
                mm(bank(pb), [(Wd[b][:, k, :], src3[:, k, t4 * 512:(t4 + 1) * 512]) for k in range(kcn)],
                   [f"Wd{b}"] + src_keys, pk(pb))
                cp("act", m3[:, oc, t4 * 512:(t4 + 1) * 512], bank(pb), pk(pb), [f"{mtag}{oc}_{t4}"])

    def postnorm_residual(u, m3, ntok, tok0, gvec, sq_off, misc_off, x_init):
        hT = hT_view()
        sq_b = r3(A.bf16(sq_off, 8 * 512), 8)
        rstd = [A.f32(misc_off + b * 2 * KB, 512) for b in range(2)]
        tmpf = [A.f32(misc_off + 4 * KB + b * 2 * KB, 512) for b in range(2)]
        xst = [A.f32(misc_off + 8 * KB + b * 4 * KB, 1024) for b in range(2)]
        for t4 in range(ntok // 512):
            gt = (tok0 // 512) + t4
            b = t4 % 2
            norm_stats(lambda kc: (m3[:, kc, t4 * 512:(t4 + 1) * 512], [f"m{kc}_{t4}"]), 512, sq_b, rstd[b],
                       4 + b, 1.0 / D, f"rstd{b}")
            if x_init:
                for s in range(4):
                    xb = s % 2
                    tok = tok0 + t4 * 512 + s * 128
                    ld(xst[xb], xu[u, 256 + tok: 256 + tok + 128, :], [], [f"xst{xb}"])
                    for kc in range(KC):
                        tr(bank(6 + kc // 4, 128, (kc % 4) * 128), xst[xb][:, kc * 128:(kc + 1) * 128], ident_f,
                           [f"xst{xb}", "cst"], pk(6 + kc // 4))
                    cp("act", hT[:, :, tok:tok + 128],
                       PS[:, 6 * 512: 8 * 512].rearrange("p (k t) -> p k t", k=8), pk(6, 7), [f"h{gt}"])
            for kc in range(KC):
                tb = kc % 2
                stt(tmpf[tb], m3[:, kc, t4 * 512:(t4 + 1) * 512], gvec[:, kc:kc + 1], rstd[b], ALU.mult, ALU.mult,
                    [f"m{kc}_{t4}", f"rstd{b}", "gains"], [f"pnt{tb}"])
                tt("pool" if kc % 2 else "dve", hT[:, kc, tok0 + t4 * 512: tok0 + (t4 + 1) * 512],
                   hT[:, kc, tok0 + t4 * 512: tok0 + (t4 + 1) * 512], tmpf[tb], ALU.add,
                   [f"pnt{tb}", f"h{gt}"], [f"h{gt}"])

    def rmsnorm_fm(dst3, ntok, tok0, gvec, sq_off, misc_off, dtag):
        hT = hT_view()
        sq_b = r3(A.bf16(sq_off, 8 * 512), 8)
        rstd = [A.f32(misc_off + b * 2 * KB, 512) for b in range(2)]
        for t4 in range(ntok // 512):
            gt = (tok0 // 512) + t4
            b = t4 % 2
            norm_stats(lambda kc: (hT[:, kc, tok0 + t4 * 512: tok0 + (t4 + 1) * 512], [f"h{gt}"]), 512, sq_b,
                       rstd[b], 6 + b, 1.0 / D, f"rstd{b}")
            for kc in range(KC):
                src = hT[:, kc, tok0 + t4 * 512: tok0 + (t4 + 1) * 512]
                if gvec is not None:
                    stt(dst3[:, kc, t4 * 512:(t4 + 1) * 512], src, gvec[:, kc:kc + 1], rstd[b], ALU.mult, ALU.mult,
                        [f"h{gt}", f"rstd{b}", "gains"], [f"{dtag}{t4}"])
                else:
                    tt("dve", dst3[:, kc, t4 * 512:(t4 + 1) * 512], src, rstd[b], ALU.mult,
                       [f"h{gt}", f"rstd{b}"], [f"{dtag}{t4}"])

    def phase_wo_pn(u, L, wname, oT_off, m_off, w_off, sq_off, misc_off, x_init):
        oT = r3(A.bf16(oT_off, 8 * TOK), 8)
        m3 = r3(A.f32(m_off, 8 * 1024), 8)
        for half in range(2):
            src = oT[:, :, half * 1024:(half + 1) * 1024]
            proj8(wname, src, [f"oT{c}" for c in range(8)], m3, 1024, w_off)
            postnorm_residual(u, m3, 1024, half * 1024, gain(1, L), sq_off, misc_off, x_init)

    def phase_ffn(u, L):
        hT = hT_view()
        a2 = r3(A.bf16(64 * KB, 8 * 1024), 8)
        actT = r3(A.bf16(80 * KB, NF * 1024), NF)
        m3 = r3(A.f32(124 * KB, 8 * 1024), 8)
        Wgu = [[r3(A.bf16(156 * KB + (b * 2 + k) * 2 * KB, 1024), 8) for k in range(2)] for b in range(2)]
        sg = [A.f32(175 * KB + b * 2 * KB, 512) for b in range(2)]
        a2k = ["a2_0", "a2_1"]
        rmsnorm_fm(a2, 1024, 0, gain(2, L), 179 * KB, 187 * KB, "a2_")
        for half in range(2):

            def load_gu(f):
                b = f % 2
                ld(Wgu[b][0], r3(wb[f"fg{L}"][f], 8), [wkey(f"fg{L}", f)], [f"Wg{b}"])
                ld(Wgu[b][1], r3(wb[f"fu{L}"][f], 8), [wkey(f"fu{L}", f)], [f"Wu{b}"])
            load_gu(0)
            n = 0
            for f in range(NF):
                b = f % 2
                if f + 1 < NF:
                    load_gu(f + 1)
                for t2 in range(2):
                    pg_b = n % 2
                    pu_b = 2 + n % 2
                    sb_ = n % 2
                    n += 1
                    rhs = lambda kc: a2[:, kc, t2 * 512:(t2 + 1) * 512]
                    mm(bank(pg_b), [(Wgu[b][0][:, kc, :], rhs(kc)) for kc in range(KC)], [f"Wg{b}"] + a2k, pk(pg_b))
                    mm(bank(pu_b), [(Wgu[b][1][:, kc, :], rhs(kc)) for kc in range(KC)], [f"Wu{b}"] + a2k, pk(pu_b))
                    act(sg[sb_], bank(pg_b), AF.Silu, pk(pg_b), [f"sg{sb_}"])
                    tt("dve", actT[:, f, t2 * 512:(t2 + 1) * 512], sg[sb_], bank(pu_b), ALU.mult,
                       [f"sg{sb_}"] + list(pk(pu_b)), [f"act{f}"])
            proj8(f"fd{L}", actT, [f"act{f}" for f in range(NF)], m3, 1024, 164 * KB, kcn=NF)
            if half == 0:
                rmsnorm_fm(a2, 1024, 1024, gain(2, L), 179 * KB, 187 * KB, "a2_")
            postnorm_residual(u, m3, 1024, half * 1024, gain(3, L), 179 * KB, 187 * KB, False)

    def phase_ple(u, L, psrc):
        hT = hT_view()
        rT = r3(A.bf16(64 * KB, 8 * TOK), 8)
        pT = r3(A.bf16(96 * KB, 2 * TOK), 2)
        Wg = [r3(A.bf16(104 * KB + b * 2 * KB, 1024), 8) for b in range(2)]
        Wpp = [r3(A.bf16(108 * KB + b * 512, 256), 2) for b in range(2)]
        pst = [A.f32(109 * KB + b * KB, 256) for b in range(2)]
        sig = [A.f32(111 * KB + b * 2 * KB, 512) for b in range(2)]
        tmpf = [A.f32(115 * KB + b * 2 * KB, 512) for b in range(2)]
        rmsnorm_fm(rT, TOK, 0, None, 119 * KB, 127 * KB, "rT_")
        for t in range(16):
            b = t % 2
            ld(pst[b], psrc[t * 128:(t + 1) * 128, :], [], [f"pst{b}"])
            for j in range(2):
                tr(bank(4 + b, 128, j * 128), pst[b][:, j * 128:(j + 1) * 128], ident_f, [f"pst{b}", "cst"], pk(4 + b))
            cp("act", pT[:, :, t * 128:(t + 1) * 128], r3(bank(4 + b, 256), 2), pk(4 + b), [f"pT{t // 4}"])
        ld(Wg[0], r3(wb[f"pg{L}"][0], 8), [wkey(f"pg{L}", 0)], ["Wpg0"])
        ld(Wpp[0], r3(wb[f"pp{L}"][0], 2), [wkey(f"pp{L}", 0)], ["Wpp0"])
        n = 0
        for oc in range(8):
            b = oc % 2
            if oc + 1 < 8:
                ld(Wg[1 - b], r3(wb[f"pg{L}"][oc + 1], 8), [wkey(f"pg{L}", oc + 1)], [f"Wpg{1 - b}"])
                ld(Wpp[1 - b], r3(wb[f"pp{L}"][oc + 1], 2), [wkey(f"pp{L}", oc + 1)], [f"Wpp{1 - b}"])
            for t4 in range(4):
                gb_ = n % 2
                pb_ = 2 + n % 2
                s_ = n % 2
                n += 1
                mm(bank(gb_), [(Wg[b][:, kc, :], rT[:, kc, t4 * 512:(t4 + 1) * 512]) for kc in range(KC)],
                   [f"Wpg{b}", f"rT_{t4}"], pk(gb_))
                mm(bank(pb_), [(Wpp[b][:, j, :], pT[:, j, t4 * 512:(t4 + 1) * 512]) for j in range(2)],
                   [f"Wpp{b}", f"pT{t4}"], pk(pb_))
                act(sig[s_], bank(gb_), AF.Sigmoid, pk(gb_), [f"sig{s_}"])
                tt("dve", tmpf[s_], sig[s_], bank(pb_), ALU.mult, [f"sig{s_}"] + list(pk(pb_)), [f"plt{s_}"])
                tt("pool", hT[:, oc, t4 * 512:(t4 + 1) * 512], hT[:, oc, t4 * 512:(t4 + 1) * 512], tmpf[s_], ALU.add,
                   [f"plt{s_}", f"h{t4}"], [f"h{t4}"])

    QT_OFF = 96 * KB

    def phase_kv(u, with_q):
        a1 = r3(A.bf16(64 * KB, 8 * TOK), 8)
        qT = r3(A.bf16(QT_OFF, 8 * TOK), 8)
        kv_f = [A.f32(128 * KB + i * 2 * KB, 512) for i in range(6)]
        knb = [A.bf16(140 * KB + b * KB, 512) for b in range(2)]
        sqb = [A.bf16(142 * KB + b * KB, 512) for b in range(2)]
        kTg = [A.bf16(144 * KB + b * 4 * KB, TOK) for b in range(2)]
        Vst = r4(A.bf16(152 * KB, 4 * 16 * 128), 4, 16)
        Wk = [r3(A.bf16(168 * KB + b * 2 * KB, 1024), 8) for b in range(2)]
        Wv = r3(A.bf16(172 * KB, 8 * 512), 8)
        cs = [[A.f32(180 * KB + (b * 2 + k) * 2 * KB, 512) for k in range(2)] for b in range(2)]
        rmsnorm_fm(a1, TOK, 0, gain(0, 1), 188 * KB, 128 * KB + 8 * KB, "a1_")
        pg.barrier()
        a1k = [f"a1_{t}" for t in range(4)]
        heads = [("gk", g, False) for g in range(4)]
        if with_q:
            heads += [("gq", h, True) for h in range(8)]
        kraw = [kv_f[0], kv_f[1]]
        kn = [kv_f[2], kv_f[3]]
        rstd2 = [kv_f[4], kv_f[5]]
        t1 = [A.f32(188 * KB + b * 2 * KB, 512) for b in range(2)]
        t2 = [A.f32(192 * KB + b * 2 * KB, 512) for b in range(2)]
        work = [(hi, t4) for hi in range(len(heads)) for t4 in range(4)]

        def stage_a(n):
            hi, t4 = work[n]
            wn, hidx, isq = heads[hi]
            b = hi % 2
            cb = n % 2
            if t4 == 0:
                ld(Wk[b], r3(wb[wn][hidx], 8), [wkey(wn, hidx)], [f"Wk{b}"])
            gcol = gqk_f[:, 2:3] if isq else gqk_f[:, 1:2]
            ld(cs[cb][0], ropec[u, :, t4 * 512:(t4 + 1) * 512], [], [f"cos{cb}"])
            ld(cs[cb][1], ropes[u, :, t4 * 512:(t4 + 1) * 512], [], [f"sin{cb}"])
            mm(bank(cb), [(Wk[b][:, kc, :], a1[:, kc, t4 * 512:(t4 + 1) * 512]) for kc in range(KC)],
               [f"Wk{b}"] + a1k, pk(cb))
            cp("act", kraw[cb], bank(cb), pk(cb), [f"kraw{cb}"])
            act(sqb[cb], bank(cb), AF.Square, pk(cb), [f"sqb{cb}"])
            mm(bank(2 + cb), [(ones_b, sqb[cb])], [f"sqb{cb}", "ones_b"], pk(2 + cb))
            act(rstd2[cb], bank(2 + cb), AF.Sqrt, pk(2 + cb), [f"krs{cb}"], scale=1.0 / 128, bias=sm_f[:, 0:1])
            pg.op("dve", lambda e, r=rstd2[cb]: e.reciprocal(r, r), [f"krs{cb}"], [f"krs{cb}"])
            stt(kn[cb], kraw[cb], gcol, rstd2[cb], ALU.mult, ALU.mult, [f"kraw{cb}", f"krs{cb}", "gqk", "gqs"], [f"kn{cb}"])
            cp("pool", knb[cb], kn[cb], [f"kn{cb}"], [f"knb{cb}"])

        def stage_b(n):
            hi, t4 = work[n]
            wn, hidx, isq = heads[hi]
            b = hi % 2
            cb = n % 2
            mm(bank(4 + cb), [(rm_b, knb[cb])], [f"knb{cb}", "rm_b"], pk(4 + cb))
            tt("pool", t1[cb], kn[cb], cs[cb][0], ALU.mult, [f"kn{cb}", f"cos{cb}"], [f"t1{cb}"])
            tt("dve", t2[cb], bank(4 + cb), cs[cb][1], ALU.mult, list(pk(4 + cb)) + [f"sin{cb}"], [f"t2{cb}"])
            if isq:
                tt("dve", qT[:, hidx, t4 * 512:(t4 + 1) * 512], t1[cb], t2[cb], ALU.add, [f"t1{cb}", f"t2{cb}"], [f"qT{hidx}"])
            else:
                tt("dve", kTg[b][:, t4 * 512:(t4 + 1) * 512], t1[cb], t2[cb], ALU.add, [f"t1{cb}", f"t2{cb}"], [f"kTg{b}"])
                if t4 == 3:
                    ld(kvs[u, hidx, 0], kTg[b], [f"kTg{b}"], [f"kvs{u}_{hidx}_0"], q="pool")

        stage_a(0)
        for n in range(len(work)):
            if n + 1 < len(work):
                stage_a(n + 1)
            stage_b(n)
        ld(Wv, r3(wb["gv"][0], 8), [wkey("gv", 0)], ["Wv"])
        for t in range(16):
            pb = 6 + t % 2
            mm(bank(pb), [(a1[:, kc, t * 128:(t + 1) * 128], Wv[:, kc, :]) for kc in range(KC)], ["Wv"] + a1k, pk(pb))
            cp("act" if t % 2 else "dve", Vst[:, :, t, :], r3(bank(pb), 4), pk(pb), ["Vst"])
        for g in range(4):
            ld(kvs[u, g, 1], Vst[:, g].rearrange("p t d -> p (t d)"), ["Vst"], [f"kvs{u}_{g}_1"], q="pool")

    OT1_OFF = 64 * KB

    def phase_attn(u, key_units):
        qT = r3(A.bf16(QT_OFF, 8 * TOK), 8)
        oT = r3(A.bf16(OT1_OFF, 8 * TOK), 8)
        ring = [(A.bf16(128 * KB + b * 8 * KB, TOK), r3(A.bf16(128 * KB + b * 8 * KB + 4 * KB, TOK), 16))
                for b in range(3)]
        NPT = 6
        PT = [A.bf16(152 * KB + b * KB, 512) for b in range(NPT)]
        rc = [A.f32(164 * KB + b * 2 * KB, 512) for b in range(2)]
        seq = [(g, q4, ku) for g in range(4) for q4 in range(4) for ku in key_units]

        def load(idx):
            g, q4, ku = seq[idx]
            b = idx % 3
            ld(ring[b][0], kvs[ku, g, 0], [f"kvs{ku}_{g}_0"], [f"rk{b}"])
            ld(ring[b][1].rearrange("p t d -> p (t d)"), kvs[ku, g, 1], [f"kvs{ku}_{g}_1"], [f"rv{b}"])
        its = [(idx, kt, hh) for idx in range(len(seq)) for kt in range(16) for hh in range(2)]

        def emit_s(n):
            idx, kt, hh = its[n]
            g, q4, ku = seq[idx]
            b = idx % 3
            h = 2 * g + hh
            sbk = 4 + n % 4
            mm(bank(sbk), [(ring[b][0][:, kt * 128:(kt + 1) * 128], qT[:, h, q4 * 512:(q4 + 1) * 512])],
               [f"rk{b}", f"qT{h}"], pk(sbk))
            act(PT[n % NPT], bank(sbk), AF.Exp, pk(sbk), [f"PTa{n % NPT}"])

        PP = [[A.bf16(160 * KB + (hh * 2 + m) * KB, 512) for m in range(2)] for hh in range(2)]
        osb = [A.f32(168 * KB + b * 2 * KB, 512) for b in range(2)]
        pending = []

        def flush(upto):
            while pending and pending[0][0] <= upto:
                _, fn_ = pending.pop(0)
                fn_()

        def emit_pv(n):
            idx, kt, hh = its[n]
            g, q4, ku = seq[idx]
            b = idx % 3
            first = (ku == key_units[0])
            last = (ku == key_units[-1])
            st = first and kt == 0
            sp_ = last and kt == 15
            pt = PT[n % NPT]
            Vt = ring[b][1]
            flush(n)
            pg.op("pe", lambda e, hh=hh, pt=pt, Vt=Vt, kt=kt, st=st, sp_=sp_:
                  e.matmul(bank(hh), Vt[:, kt, :], pt, start=st, stop=sp_),
                  [f"PTa{n % NPT}", f"rv{b}"], pk(hh))
            if kt % 2 == 1:
                m_ = (kt // 2) % 2
                pp = PP[hh][m_]
                tt("pool" if hh == 0 else "dve", pp, PT[n % NPT], PT[(n - 2) % NPT], ALU.add,
                   [f"PTa{n % NPT}", f"PTa{(n - 2) % NPT}"], [f"PP{hh}{m_}"])
                st2 = first and kt == 1
                sp2 = last and kt == 15

                def ones_mm(hh=hh, pp=pp, st2=st2, sp2=sp2, m_=m_):
                    pg.op("pe", lambda e: e.matmul(bank(2 + hh), ones_b, pp, start=st2, stop=sp2),
                          [f"PP{hh}{m_}", "ones_b"], pk(2 + hh))
                pending.append((n + 3, ones_mm))
            if sp_:
                flush(1 << 60)
                h = 2 * g + hh
                cp("act", osb[hh], bank(hh), pk(hh), [f"osb{hh}"])
                cp("act", rc[hh], bank(2 + hh), pk(2 + hh), [f"rc{hh}"])
                pg.op("dve", lambda e, hh=hh: e.reciprocal(rc[hh], rc[hh]), [f"rc{hh}"], [f"rc{hh}"])
                tt("dve", oT[:, h, q4 * 512:(q4 + 1) * 512], osb[hh], rc[hh], ALU.mult,
                   [f"osb{hh}", f"rc{hh}"], [f"oT{h}"])
            if kt == 0 and hh == 0 and idx + 2 < len(seq):
                load(idx + 2)

        load(0)
        if len(seq) > 1:
            load(1)
        AHEAD = cfg.get("ahead", 2)
        for n in range(min(AHEAD, len(its))):
            emit_s(n)
        for n in range(len(its)):
            if n + AHEAD < len(its):
                emit_s(n + AHEAD)
            emit_pv(n)

    def phase_out(dst):
        hT = hT_view()
        ost = [A.f32(64 * KB + b * 4 * KB, 1024) for b in range(2)]
        for t in range(16):
            b = t % 2
            for kc in range(KC):
                tr(bank(2 * b + kc // 4, 128, (kc % 4) * 128), hT[:, kc, t * 128:(t + 1) * 128], ident_f,
                   [f"h{t // 4}", "cst"], pk(2 * b + kc // 4))
            cp("act" if b else "dve", ost[b], PS[:, 2 * b * 512: (2 * b + 2) * 512], pk(2 * b, 2 * b + 1), [f"ost{b}"])
            ld(dst[t * 128:(t + 1) * 128, :], ost[b], [f"ost{b}"], [f"y{t}"], q="pool")

    def layer0(u):
        phase_x(u)
        if cfg.get("stop") == "x":
            pg.barrier()
            return
        phase_na(u)
        pg.barrier()
        if cfg.get("stop") == "na":
            oT = r3(A.bf16(OT0_OFF, 8 * TOK), 8)
            stg = [A.f32(100 * KB + b * 2 * KB, 512) for b in range(2)]
            ypv = yp.rearrange("(k p a) f -> k p (a f)", k=8, p=128)
            n = 0
            for kc in range(8):
                for t4 in range(4):
                    b = n % 2
                    n += 1
                    cp("dve", stg[b], oT[:, kc, t4 * 512:(t4 + 1) * 512], [], [f"dstg{b}"])
                    ld(ypv[kc][:, t4 * 512:(t4 + 1) * 512], stg[b], [f"dstg{b}"], [f"dy{n}"], q="pool")
            pg.barrier()
            return
        phase_wo_pn(u, 0, "na_o", OT0_OFF, 96 * KB, 128 * KB, 132 * KB, 140 * KB, True)
        pg.barrier()
        if cfg.get("stop") == "wo":
            return
        phase_ffn(u, 0)
        pg.barrier()
        if cfg.get("stop") == "ffn":
            return
        phase_ple(u, 0, p0u[u])
        pg.barrier()

    def layer1_rest(u, pidx, dst):
        phase_wo_pn(u, 1, "go", OT1_OFF, 96 * KB, 128 * KB, 132 * KB, 140 * KB, False)
        pg.barrier()
        phase_ffn(u, 1)
        pg.barrier()
        phase_ple(u, 1, p1u[pidx])
        pg.barrier()
        phase_out(dst)
        pg.barrier()

    for u in units_l0:
        layer0(u)
        if dbg == "l0" and u == units_l0[-1]:
            if cfg.get("stop") != "na":
                phase_out(yp)
            pg.barrier()
            break
        if not do_l1:
            continue
        is_q = u in (U_OWN, U_PROMPT)
        phase_kv(u, is_q)
        pg.barrier()
        if u == U_OWN:
            phase_attn(u, list(range(8)))
            pg.barrier()
            layer1_rest(u, 0, ys)
        elif u == U_PROMPT:
            phase_attn(u, [U_PROMPT])
            pg.barrier()
            layer1_rest(u, 1, yp)
    pg.barrier()

    for s in pg.sem_names:
        pg.handles[s] = nc.alloc_semaphore(f"s_{s}")
    with nc.Block() as block:
        pg.emit(block)
    return nc, pg


def _na_static():
    qc = np.arange(64)[:, None]
    kc = np.arange(64)[None, :]
    wstart = np.clip(qc - 8, 0, 48)
    colmask = (kc >= wstart) & (kc < wstart + 16)
    return colmask


def _utab(rpb, interior=False):
    colmask = _na_static()
    out = np.full((8, 128, 2, ND, 64), NEG, np.float32)
    kp = np.arange(128)
    kcol = kp % 64
    khalf = kp // 64
    qcs = np.arange(64)
    ci = np.clip(kcol[:, None] - qcs[None, :] + 15, 0, 30)
    cm = colmask.T[kcol, :]
    for c in range(8):
        for hd in range(2):
            h = 2 * c + hd
            for di in range(ND):
                d = di - 7
                ri = d + 7 + khalf
                ok = (ri >= 0) & (ri <= 14)
                ric = np.clip(ri, 0, 14)
                vals = rpb[h][ric[:, None], ci]
                if interior:
                    ok = ok & (ri >= 3) & (ri <= 10)
                out[c, :, hd, di, :] = np.where(cm & ok[:, None], vals, np.float32(NEG))
    return out.reshape(8, 128, 2 * ND * 64)


def _rv_masks(g0, R):
    out = np.zeros((128, 48), np.float32)
    for e, l in enumerate(list(range(4, 8)) + list(range(32, 36))):
        g = g0 + l
        rs = min(max(g - 4, 0), R - 8)
        tiles = range(0, 6) if l < 8 else range(14, 20)
        for j, t in enumerate(tiles):
            for half in range(2):
                gk = g0 + 2 * t + half
                valid = (rs <= gk < rs + 8)
                if not valid:
                    out[half * 64:(half + 1) * 64, e * 6 + j] = NEG
    return out


def _rope_tables(tok0):
    t = np.arange(tok0, tok0 + TOK)
    row = (t // 64).astype(np.float32)
    col = (t % 64).astype(np.float32)
    inv = (np.float32(10000.0) ** (-np.arange(0, 64, 2, dtype=np.float32) / np.float32(64))).astype(np.float32)
    ang = np.zeros((128, TOK), np.float32)
    for d in range(128):
        f = d % 32
        pos = row if d < 64 else col
        ang[d] = (pos * inv[f]).astype(np.float32)
    return np.cos(ang).astype(np.float32), np.sin(ang).astype(np.float32)


def _consts():
    c = np.zeros((128, 387), np.float32)
    c[:, 0:128] = np.eye(128, dtype=np.float32)
    c[:, 128:256] = 1.0
    rm = np.zeros((128, 128), np.float32)
    for m in range(128):
        if (m % 64) < 32:
            rm[m + 32, m] = -1.0
        else:
            rm[m - 32, m] = 1.0
    c[:, 256:384] = rm
    c[0:64, 385] = NEG
    c[64:128, 386] = NEG
    return c


_CACHE = {}


def _prep_inputs(inp):
    xs = inp["x_sample"][0]
    xpad = np.concatenate([np.zeros((256, D), np.float32), xs, np.zeros((256, D), np.float32)], axis=0)
    zeros_h = np.zeros((256, D), np.float32)
    g_all = np.stack([inp["mix_pre_norm"], inp["mix_post_norm"], inp["ffn_pre_norm"], inp["ffn_post_norm"]], 0)
    gains = np.ascontiguousarray(g_all.reshape(4, 2, 8, 128).transpose(3, 0, 1, 2).reshape(128, 64)).astype(np.float32)
    gqk = np.ascontiguousarray(np.stack([inp["gqa_q_norm"][0], inp["gqa_k_norm"][0]], 1)).astype(np.float32)
    utab = _utab(np.asarray(inp["na_rpb"][0], np.float32))
    utab2 = _utab(np.asarray(inp["na_rpb"][0], np.float32), interior=True)
    cst = _consts()
    rope_p = _rope_tables(0)
    maps = []
    for c in range(NCORES):
        chunks = [(c + 1 + u) % 8 for u in range(7)] + [c]
        xu = np.empty((NUNITS, HTOK, D), np.float32)
        p0 = np.empty((NUNITS, TOK, 256), np.float32)
        rv = np.empty((NUNITS, 128, 48), np.float32)
        rc = np.empty((NUNITS, 128, TOK), np.float32)
        rs = np.empty((NUNITS, 128, TOK), np.float32)
        for u, gj in enumerate(chunks):
            xu[u] = xpad[gj * TOK: gj * TOK + HTOK]
            p0[u] = inp["p_sample"][0, 0, gj * TOK:(gj + 1) * TOK]
            rv[u] = _rv_masks(32 * gj - 4, 256)
            key = ("rope", gj)
            if key not in _CACHE:
                _CACHE[key] = _rope_tables(gj * TOK)
            rc[u], rs[u] = _CACHE[key]
        xu[U_PROMPT] = np.concatenate([zeros_h, inp["x_prompt"][c], zeros_h], 0)
        p0[U_PROMPT] = inp["p_prompt"][0, c]
        rv[U_PROMPT] = _rv_masks(-4, 32)
        rc[U_PROMPT], rs[U_PROMPT] = rope_p
        p1 = np.stack([inp["p_sample"][1, 0, c * TOK:(c + 1) * TOK], inp["p_prompt"][1, c]], 0)
        maps.append({
            "xu": xu, "p0u": p0, "p1u": np.ascontiguousarray(p1), "rv": rv, "ropec": rc, "ropes": rs,
            "utab": utab, "utab2": utab2, "cst": cst, "gains": gains, "gqk": gqk,
            "na_w_qkv": inp["na_w_qkv"][0], "na_w_o": inp["na_w_o"][0],
            "gqa_w_qkv": inp["gqa_w_qkv"][0], "gqa_w_o": inp["gqa_w_o"][0],
            "ffn_w_gate_up": inp["ffn_w_gate_up"], "ffn_w_down": inp["ffn_w_down"],
            "ple_w_gate": inp["ple_w_gate"], "ple_w_proj": inp["ple_w_proj"],
        })
    return maps


def kernel(**inputs):
    inp = {k: np.asarray(v) for k, v in inputs.items()}
    maps = _prep_inputs(inp)
    nc, pg = build()
    res = run_bass_kernel_spmd(nc, maps, core_ids=list(range(NCORES)))
    y_prompt = np.stack([np.asarray(res.results[c]["yp"], np.float32) for c in range(NCORES)], 0)
    y_sample = np.concatenate([np.asarray(res.results[c]["ys"], np.float32) for c in range(NCORES)], 0)[None]
    return (y_prompt, y_sample)
```

```python
import numpy as np
import concourse.bass as bass
import concourse.mybir as mybir
from concourse.bass_utils import run_bass_kernel_spmd

F32 = mybir.dt.float32
BF16 = mybir.dt.bfloat16
AF = mybir.ActivationFunctionType
ALU = mybir.AluOpType

NCORES = 8
D = 1024
KC = 8
TOK = 2048
HTOK = 2560
DFF = 2816
NF = 22
EPS = 1e-6
NEG = -30000.0
NUNITS = 9
U_OWN = 7
U_PROMPT = 8
ND = 14
KB = 1024


class Prog:
    ENG = ("pe", "act", "dve", "pool", "sp")

    def __init__(self, nc, n_dma_sp=24, n_dma_pool=8):
        self.nc = nc
        self.ops = {e: [] for e in self.ENG}
        self.cnt = {}
        self.waited = {e: {} for e in self.ENG}
        self.last_w = {}
        self.readers = {}
        self.sem_names = ["pe", "act", "dve", "pool"]
        self.dma_slots = {"sp": [f"dsp{i}" for i in range(n_dma_sp)],
                          "pool": [f"dpl{i}" for i in range(n_dma_pool)]}
        self.dma_next = {"sp": 0, "pool": 0}
        for q in self.dma_slots.values():
            self.sem_names += q
        for s in self.sem_names:
            self.cnt[s] = 0
        self.handles = {}
        self.nops = 0

    def _deps(self, reads, writes):
        deps = set()
        for k in reads:
            if k in self.last_w:
                deps.add(self.last_w[k])
        for k in writes:
            if k in self.last_w:
                deps.add(self.last_w[k])
            for r in self.readers.get(k, ()):
                deps.add(r)
        return deps

    def _record(self, reads, writes, done):
        for k in reads:
            self.readers.setdefault(k, []).append(done)
        for k in writes:
            self.last_w[k] = done
            self.readers[k] = []

    def _waits(self, eng, deps):
        need = {}
        for (s, v) in deps:
            if eng == "pe" and s == "pe":
                continue
            if self.waited[eng].get(s, 0) < v:
                need[s] = max(need.get(s, 0), v)
        for s, v in need.items():
            self.waited[eng][s] = v
        return list(need.items())

    def op(self, eng, fn, reads=(), writes=()):
        deps = self._deps(reads, writes)
        waits = self._waits(eng, deps)
        self.cnt[eng] += 1
        done = (eng, self.cnt[eng])
        self.ops[eng].append((waits, fn, (eng, 1)))
        self._record(reads, writes, done)
        self.nops += 1
        return done

    def dma(self, q, fn, reads=(), writes=()):
        slots = self.dma_slots[q]
        s = slots[self.dma_next[q] % len(slots)]
        self.dma_next[q] += 1
        deps = self._deps(reads, writes)
        if self.cnt[s] > 0:
            deps.add((s, self.cnt[s]))
        waits = self._waits(q, deps)
        self.cnt[s] += 16
        done = (s, self.cnt[s])
        self.ops[q].append((waits, fn, (s, 16)))
        self._record(reads, writes, done)
        self.nops += 1
        return done

    def barrier(self, engines=None):
        deps = set((s, c) for s, c in self.cnt.items() if c > 0)
        for e in (engines or self.ENG):
            waits = self._waits(e, set(deps))
            if waits:
                self.ops[e].append((waits, None, None))
        self.last_w = {}
        self.readers = {}

    def emit(self, block):
        H = self.handles

        class FirstCatcher:
            def __init__(self, e):
                self._e = e
                self.first = None

            def __getattr__(self, name):
                attr = getattr(self._e, name)
                if not callable(attr):
                    return attr

                def w(*a, **k):
                    r = attr(*a, **k)
                    if self.first is None and hasattr(r, "then_inc"):
                        self.first = r
                    return r
                return w

        def run(eng_name):
            def body(e):
                for (waits, fn, sig) in self.ops[eng_name]:
                    if fn is None:
                        for (s, v) in waits:
                            e.wait_ge(H[s], v)
                        continue
                    for (s, v) in waits[:-1]:
                        e.wait_ge(H[s], v)
                    fc = FirstCatcher(e)
                    ins = fn(fc)
                    if waits:
                        s, v = waits[-1]
                        fc.first._wait_ge(H[s], v)
                    if sig is not None:
                        ins.then_inc(H[sig[0]], sig[1])
            return body

        block.tensor(run("pe"))
        block.scalar(run("act"))
        block.vector(run("dve"))
        block.gpsimd(run("pool"))
        block.sync(run("sp"))


class Arena:
    def __init__(self, nc, nbytes):
        self.t = nc.alloc_sbuf_tensor("arena", [128, nbytes // 4], F32)
        self.t16 = self.t.bitcast(BF16)
        self.nbytes = nbytes

    def f32(self, off, n):
        assert off % 4 == 0 and off + 4 * n <= self.nbytes, (off, n)
        return self.t[:, off // 4: off // 4 + n]

    def bf16(self, off, n):
        assert off % 2 == 0 and off + 2 * n <= self.nbytes, (off, n)
        return self.t16[:, off // 2: off // 2 + n]


def r3(ap, a):
    return ap.rearrange("p (a b) -> p a b", a=a)


def r4(ap, a, b):
    return ap.rearrange("p (a b c) -> p a b c", a=a, b=b)


def na_row_spec(i):
    l = i + 4
    if l < 8:
        tiles = list(range(0, 6))
        masks = [3 + (l - 4) * 6 + j for j in range(6)]
    elif l >= 32:
        tiles = list(range(14, 20))
        masks = [3 + 24 + (l - 32) * 6 + j for j in range(6)]
    elif l % 2 == 0:
        tiles = list(range(l // 2 - 2, l // 2 + 2))
        masks = [0, 0, 0, 0]
    else:
        t0 = (l - 5) // 2
        tiles = list(range(t0, t0 + 5))
        masks = [1, 0, 0, 0, 2]
    return l, tiles, masks


def build(cfg=None):
    cfg = cfg or {}
    units_l0 = cfg.get("units", list(range(NUNITS)))
    do_l1 = cfg.get("do_l1", True)
    dbg = cfg.get("dbg", None)

    nc = bass.Bass("TRN2", target_bir_lowering=False)
    pg = Prog(nc)

    def din(name, shape, dt=F32):
        return nc.dram_tensor(name, list(shape), dt, kind="ExternalInput").ap()

    xu = din("xu", [NUNITS, HTOK, D])
    p0u = din("p0u", [NUNITS, TOK, 256])
    p1u = din("p1u", [2, TOK, 256])
    rvd = din("rv", [NUNITS, 128, 48])
    ropec = din("ropec", [NUNITS, 128, TOK])
    ropes = din("ropes", [NUNITS, 128, TOK])
    utab = din("utab", [8, 128, 2 * ND * 64])
    utab2 = din("utab2", [8, 128, 2 * ND * 64])
    cst = din("cst", [128, 387])
    gains = din("gains", [128, 64])
    gqk = din("gqk", [128, 2])
    w_na_qkv = din("na_w_qkv", [D, 3072])
    w_na_o = din("na_w_o", [D, D])
    w_gqa_qkv = din("gqa_w_qkv", [D, 2048])
    w_gqa_o = din("gqa_w_o", [D, D])
    w_gu = din("ffn_w_gate_up", [2, D, 2 * DFF])
    w_dn = din("ffn_w_down", [2, DFF, D])
    w_pg = din("ple_w_gate", [2, D, D])
    w_pp = din("ple_w_proj", [2, 256, D])

    yp = nc.dram_tensor("yp", [TOK, D], F32, kind="ExternalOutput").ap()
    ys = nc.dram_tensor("ys", [TOK, D], F32, kind="ExternalOutput").ap()

    def wscratch(name, npanel, kcn, ow):
        return nc.dram_tensor(name, [npanel, 128, kcn * ow], BF16).ap()

    wb = {
        "na_q": wscratch("wb_na_q", 8, 8, 128), "na_k": wscratch("wb_na_k", 8, 8, 128),
        "na_v": wscratch("wb_na_v", 2, 8, 512), "na_o": wscratch("wb_na_o", 8, 8, 128),
        "gq": wscratch("wb_gq", 8, 8, 128), "gk": wscratch("wb_gk", 4, 8, 128),
        "gv": wscratch("wb_gv", 1, 8, 512), "go": wscratch("wb_go", 8, 8, 128),
    }
    for L in range(2):
        wb[f"fg{L}"] = wscratch(f"wb_fg{L}", NF, 8, 128)
        wb[f"fu{L}"] = wscratch(f"wb_fu{L}", NF, 8, 128)
        wb[f"fd{L}"] = wscratch(f"wb_fd{L}", 8, NF, 128)
        wb[f"pg{L}"] = wscratch(f"wb_pg{L}", 8, 8, 128)
        wb[f"pp{L}"] = wscratch(f"wb_pp{L}", 8, 2, 128)
    kvs = nc.dram_tensor("kvs", [NUNITS, 4, 2, 128, TOK], BF16).ap()

    A = Arena(nc, 206 * KB)
    PB = 196 * KB
    cst_f = A.f32(PB, 387); PB += 387 * 4 + 4
    ident_f = cst_f[:, 0:128]
    ident_b = A.bf16(PB, 128); PB += 256
    ones_b = A.bf16(PB, 128); PB += 256
    rm_b = A.bf16(PB, 128); PB += 256
    gains_f = A.f32(PB, 64); PB += 256
    gqk_f = A.f32(PB, 4); PB += 16
    rv_f = A.f32(PB, 51); PB += 208
    sm_f = A.f32(PB, 64); PB += 256
    assert PB <= 206 * KB

    def gain(kind, layer):
        i = (kind * 2 + layer) * 8
        return gains_f[:, i:i + 8]

    PS = nc.alloc_psum_tensor("psum_all", [128, 4096], F32)
    PS16 = PS.bitcast(BF16)

    def bank(i, n=512, o=0):
        return PS[:, i * 512 + o: i * 512 + o + n]

    def bank16(i, n=1024, o=0):
        return PS16[:, i * 1024 + o: i * 1024 + o + n]

    def pk(*banks):
        return tuple(f"ps{b}" for b in banks)

    def mm(out, pairs, reads, writes):
        def fn(e):
            n = len(pairs)
            ins = None
            for i, (l, r) in enumerate(pairs):
                ins = e.matmul(out, l, r, start=(i == 0), stop=(i == n - 1))
            return ins
        return pg.op("pe", fn, reads, writes)

    def tr(out, in_, idn, reads, writes):
        return pg.op("pe", lambda e: e.transpose(out, in_, idn), reads, writes)

    def act(out, in_, func, reads, writes, **kw):
        return pg.op("act", lambda e: e.activation(out=out, in_=in_, func=func, **kw), reads, writes)

    def tt(eng, out, a, b, op, reads, writes):
        return pg.op(eng, lambda e: e.tensor_tensor(out, a, b, op), reads, writes)

    def ts(eng, out, a, s1, s2, op0, op1, reads, writes):
        return pg.op(eng, lambda e: e.tensor_scalar(out, a, s1, s2, op0, op1), reads, writes)

    def stt(out, a, s, b, op0, op1, reads, writes):
        return pg.op("dve", lambda e: e.scalar_tensor_tensor(out, a, s, b, op0, op1), reads, writes)

    def cp(eng, out, in_, reads, writes):
        if eng == "act":
            return pg.op("act", lambda e: e.copy(out, in_), reads, writes)
        return pg.op(eng, lambda e: e.tensor_copy(out, in_), reads, writes)

    def ld(out, in_, reads, writes, q="sp"):
        return pg.dma(q, lambda e: e.dma_start(out=out, in_=in_), reads, writes)

    ld(cst_f, cst, [], ["cst"])
    ld(gains_f, gains, [], ["gains"])
    ld(gqk_f[:, 0:2], gqk, [], ["gqk"])
    cp("dve", ident_b, cst_f[:, 0:128], ["cst"], ["ident_b"])
    cp("dve", ones_b, cst_f[:, 128:256], ["cst"], ["ones_b"])
    cp("dve", rm_b, cst_f[:, 256:384], ["cst"], ["rm_b"])
    cp("dve", rv_f[:, 0:3], cst_f[:, 384:387], ["cst"], ["rvc"])
    ts("dve", gqk_f[:, 2:3], gqk_f[:, 0:1], float(128 ** -0.5), None, ALU.mult, ALU.bypass, ["gqk"], ["gqs"])

    conv = []
    for c in range(8):
        conv.append((w_na_qkv, c * 128, 8, 128, "na_q", c))
        conv.append((w_na_qkv, 1024 + c * 128, 8, 128, "na_k", c))
    for g in range(2):
        conv.append((w_na_qkv, 2048 + g * 512, 8, 512, "na_v", g))
    for c in range(8):
        conv.append((w_na_o, c * 128, 8, 128, "na_o", c))
    for L in range(2):
        for f in range(NF):
            conv.append((w_gu[L], f * 128, 8, 128, f"fg{L}", f))
            conv.append((w_gu[L], DFF + f * 128, 8, 128, f"fu{L}", f))
        for c in range(8):
            conv.append((w_dn[L], c * 128, NF, 128, f"fd{L}", c))
            conv.append((w_pg[L], c * 128, 8, 128, f"pg{L}", c))
            conv.append((w_pp[L], c * 128, 2, 128, f"pp{L}", c))
    for c in range(8):
        conv.append((w_gqa_qkv, c * 128, 8, 128, "gq", c))
    for c in range(4):
        conv.append((w_gqa_qkv, 1024 + c * 128, 8, 128, "gk", c))
    conv.append((w_gqa_qkv, 1536, 8, 512, "gv", 0))
    for c in range(8):
        conv.append((w_gqa_o, c * 128, 8, 128, "go", c))

    NSTG = 8
    stg_f = [A.f32(i * 16 * KB, 4096) for i in range(NSTG)]
    stg_b = [A.bf16(128 * KB + i * 8 * KB, 4096) for i in range(NSTG)]
    ceng = ["pool", "dve", "act"]
    for i, (src, col0, kcn, ow, name, panel) in enumerate(conv):
        b = i % NSTG
        n = kcn * ow
        srcv = src.rearrange("(kc p) o -> p kc o", p=128)[:, :, col0:col0 + ow]
        ld(r3(stg_f[b][:, 0:n], kcn), srcv, [], [f"stgf{b}"])
        cp(ceng[i % 3], stg_b[b][:, 0:n], stg_f[b][:, 0:n], [f"stgf{b}"], [f"stgb{b}"])
        ld(wb[name][panel], stg_b[b][:, 0:n], [f"stgb{b}"], [f"wb_{name}_{panel}"], q="pool")
    pg.barrier()
    WB_KEYS = {}

    def wkey(name, panel):
        return f"wb_{name}_{panel}"

    HT_OFF = 0

    def hT_view():
        return r3(A.f32(HT_OFF, 8 * TOK), 8)

    def norm_stats(src_fn, ntok, sq_b, rstd_f, psb, inv_n, rkey):
        for kc in range(KC):
            s_ap, rk = src_fn(kc)
            eng = "act" if kc % 2 == 0 else "pool"
            if eng == "act":
                act(sq_b[:, kc, 0:ntok], s_ap, AF.Square, rk, [f"nsq{kc}"])
            else:
                tt("pool", sq_b[:, kc, 0:ntok], s_ap, s_ap, ALU.mult, rk, [f"nsq{kc}"])
        mm(bank(psb, ntok), [(ones_b, sq_b[:, kc, 0:ntok]) for kc in range(KC)],
           [f"nsq{kc}" for kc in range(KC)] + ["ones_b"], pk(psb))
        act(rstd_f[:, 0:ntok], bank(psb, ntok), AF.Sqrt, pk(psb), [rkey], scale=inv_n, bias=sm_f[:, 0:1])
        pg.op("dve", lambda e: e.reciprocal(rstd_f[:, 0:ntok], rstd_f[:, 0:ntok]), [rkey], [rkey])

    pg.op("pool", lambda e: e.memset(sm_f[:, 0:1], EPS), [], ["eps"])
    pg.barrier()

    AT_OFF = 0
    OT0_OFF = 64 * KB

    def phase_x(u):
        aT = r3(A.bf16(AT_OFF, 8 * HTOK), 8)
        NXB = 3
        xs_f = [A.f32(141 * KB, 1024), A.f32(145 * KB, 1024), A.f32(191 * KB, 1024)]
        xs_b = [A.bf16(149 * KB, 1024), A.bf16(151 * KB, 1024), A.bf16(104 * KB, 1024)]
        junk = A.bf16(153 * KB, 1024)
        g = gain(0, 0)
        for t in range(HTOK // 128):
            b = t % NXB
            ss, sr, rs = sm_f[:, 2 + b:3 + b], sm_f[:, 8 + b:9 + b], sm_f[:, 12 + b:13 + b]
            ld(xs_f[b], xu[u, t * 128:(t + 1) * 128, :], [], [f"xsf{b}"])
            act(junk, xs_f[b], AF.Square, [f"xsf{b}"], ["junk", f"ss{b}"], accum_out=ss)
            act(sr, ss, AF.Sqrt, [f"ss{b}", "eps"], [f"sr{b}"], scale=1.0 / D, bias=sm_f[:, 0:1])
            pg.op("dve", lambda e, rs=rs, sr=sr: e.reciprocal(rs, sr), [f"sr{b}"], [f"rs{b}"])
            ts("dve", xs_b[b], xs_f[b], rs, None, ALU.mult, ALU.bypass, [f"xsf{b}", f"rs{b}"], [f"xsb{b}"])
            for kc in range(KC):
                tr(bank16(b, 128, kc * 128), xs_b[b][:, kc * 128:(kc + 1) * 128], ident_b,
                   [f"xsb{b}", "ident_b"], pk(b))
            gb = g.unsqueeze(2).to_broadcast([128, 8, 128])
            tt("dve", aT[:, :, t * 128:(t + 1) * 128], r3(bank16(b, 1024), 8), gb, ALU.mult,
               list(pk(b)) + ["gains"], [f"aT{t // 4}"])

    def phase_na(u):
        aT = r3(A.bf16(AT_OFF, 8 * HTOK), 8)
        oT = r3(A.bf16(OT0_OFF, 8 * TOK), 8)
        qT = [A.bf16(40 * KB + b * 4 * KB, TOK) for b in range(2)]
        kT = [A.bf16(48 * KB + b * 5 * KB, HTOK) for b in range(2)]
        Vg = [r3(A.bf16(170 * KB + k * 5248, 20 * 130), 20) for k in range(4)]
        Wvg = r3(A.bf16(96 * KB, 8 * 512), 8)
        Wp = [[r3(A.bf16(107 * KB + (b * 3 + k) * 2 * KB, 1024), 8) for k in range(3)] for b in range(2)]
        Ut = [A.f32(119 * KB + b * 7 * KB, 2 * ND * 64) for b in range(2)]
        Ui = [A.f32(156 * KB + b * 7 * KB, 2 * ND * 64) for b in range(2)]
        tmp = [A.f32(133 * KB + b * 4 * KB, 1024) for b in range(2)]
        PT = [A.bf16(60 * KB + b * 2 * KB, 1024) for b in range(2)]
        otok = [A.bf16(58 * KB + b * 256, 128) for b in range(2)]
        rcp = A.f32(59 * KB, 8)
        ld(rv_f[:, 3:51], rvd[u], [], ["rvu"])
        for k in range(4):
            pg.op("pool", lambda e, k=k: e.memset(Vg[k][:, :, 64:65], 1.0), [], [f"V{k}"])
            pg.op("pool", lambda e, k=k: e.memset(Vg[k][:, :, 129:130], 1.0), [], [f"V{k}"])
        aT_keys = [f"aT{i}" for i in range(5)]

        def load_w(c):
            b = c % 2
            for k, nm in enumerate(("na_q", "na_k")):
                ld(Wp[b][k], r3(wb[nm][c], 8), [wkey(nm, c)], [f"W{b}_{k}"])
            ld(Ut[b], utab[c], [], [f"U{b}"])
            ld(Ui[b], utab2[c], [], [f"Ui{b}"])

        load_w(0)
        for c in range(8):
            b = c % 2
            if c + 1 < 8:
                load_w(c + 1)
            for t4 in range(4):
                pb = t4 % 2
                mm(bank(pb), [(Wp[b][0][:, kc, :], aT[:, kc, 256 + t4 * 512: 256 + (t4 + 1) * 512]) for kc in range(KC)],
                   [f"W{b}_0"] + aT_keys, pk(pb))
                act(qT[b][:, t4 * 512:(t4 + 1) * 512], bank(pb), AF.Copy, pk(pb), [f"q{b}"], scale=0.125)
            for t5 in range(5):
                pb = t5 % 2
                mm(bank(pb), [(Wp[b][1][:, kc, :], aT[:, kc, t5 * 512:(t5 + 1) * 512]) for kc in range(KC)],
                   [f"W{b}_1"] + aT_keys, pk(pb))
                cp("act" if t5 % 2 else "dve", kT[b][:, t5 * 512:(t5 + 1) * 512], bank(pb), pk(pb), [f"k{b}"])
            if c % 4 == 0:
                ld(Wvg, r3(wb["na_v"][c // 4], 8), [wkey("na_v", c // 4)], ["Wvg"])
                vall = A.bf16(170 * KB, 4 * 2624).rearrange("p (k x) -> p k x", k=4)
                for t in range(20):
                    pb = 2 + t % 2
                    mm(bank(pb), [(aT[:, kc, t * 128:(t + 1) * 128], Wvg[:, kc, :]) for kc in range(KC)],
                       ["Wvg"] + aT_keys, pk(pb))
                    vout = vall[:, :, t * 130:(t + 1) * 130].rearrange("p k (h e) -> p k h e", h=2)[:, :, :, 0:64]
                    cp("dve" if t % 2 else "act", vout, bank(pb).rearrange("p (k h e) -> p k h e", k=4, h=2),
                       pk(pb), [f"V{k}" for k in range(4)])
            def row_qk(i, b=b):
                l, tiles, masks = na_row_spec(i)
                sb = 4 + 2 * (i % 2)

                def qk(e, b=b, i=i, tiles=tiles, sb=sb):
                    ins = None
                    for hd in range(2):
                        for j, t in enumerate(tiles):
                            ins = e.matmul(bank(sb + hd, 64, j * 64),
                                           kT[b][hd * 64:(hd + 1) * 64, t * 128:(t + 1) * 128],
                                           qT[b][hd * 64:(hd + 1) * 64, i * 64:(i + 1) * 64],
                                           start=True, stop=True)
                    return ins
                pg.op("pe", qk, [f"q{b}", f"k{b}"], pk(sb, sb + 1))

            def row_rest(i, b=b, c=c):
                l, tiles, masks = na_row_spec(i)
                nt = len(tiles)
                sb = 4 + 2 * (i % 2)
                pob = 2 + i % 2
                tb = i % 2
                skeys = pk(sb, sb + 1)
                d0 = 2 * tiles[0] - l + 7
                s_in = PS[:, sb * 512: sb * 512 + 1024].rearrange("p (h j q) -> p h j q", h=2, j=8)[:, :, 0:nt, :]
                edge = masks[0] >= 3
                Utab_ = Ut[b] if edge else Ui[b]
                ukey = f"U{b}" if edge else f"Ui{b}"
                u_in = Utab_.rearrange("p (h d q) -> p h d q", h=2, d=ND)[:, :, d0:d0 + 2 * nt - 1:2, :]
                t4v = tmp[tb].rearrange("p (h j q) -> p h j q", h=2, j=8)
                tt("dve", t4v[:, :, 0:nt, :], s_in, u_in, ALU.add, list(skeys) + [ukey], [f"tmp{tb}"])
                p_out = PT[tb].rearrange("p (h j q) -> p h j q", h=2, j=8)
                if edge:
                    for j in range(nt):
                        mcol = masks[j]
                        act(p_out[:, :, j:j + 1, :], t4v[:, :, j:j + 1, :],
                            AF.Exp, [f"tmp{tb}", "rvu", "rvc"], [f"PT{tb}"], bias=rv_f[:, mcol:mcol + 1])
                else:
                    act(p_out[:, :, 0:nt, :], t4v[:, :, 0:nt, :], AF.Exp, [f"tmp{tb}"], [f"PT{tb}"])

            def row_pv(i, b=b, c=c):
                l, tiles, masks = na_row_spec(i)
                nt = len(tiles)
                pob = 2 + i % 2
                tb = i % 2

                def pv(e, c=c, tiles=tiles, pob=pob, tb=tb, nt=nt):
                    ins = None
                    for hd in range(2):
                        for j, t in enumerate(tiles):
                            ins = e.matmul(bank(pob, 65, hd * 65)[0:64, :],
                                           PT[tb].rearrange("p (h j q) -> p h j q", h=2, j=8)[:, hd, j, :],
                                           Vg[c % 4][:, t, hd * 65:(hd + 1) * 65],
                                           start=(j == 0), stop=(j == nt - 1))
                    return ins
                pg.op("pe", pv, [f"PT{tb}", f"V{c % 4}"], pk(pob))

            def row_fin(i, b=b, c=c):
                pob = 2 + i % 2
                half = i % 2
                ob = (i // 2) % 2
                po3 = r3(bank(pob, 130), 2)[0:64]
                pg.op("dve", lambda e, po3=po3, half=half: e.reciprocal(rcp[0:64, half * 2:half * 2 + 2], po3[:, :, 64]),
                      pk(pob), [f"rcp{half}"])
                rb = rcp[0:64, half * 2:half * 2 + 2].unsqueeze(2).to_broadcast([64, 2, 64])
                tt("dve", r3(otok[ob], 2)[half * 64:(half + 1) * 64], po3[:, :, 0:64], rb, ALU.mult,
                   list(pk(pob)) + [f"rcp{half}"], [f"otok{ob}"])
                if half == 1:
                    tpb = ob
                    tr(bank16(tpb, 128), otok[ob], ident_b, [f"otok{ob}", "ident_b"], pk(tpb))
                    cp("act", oT[:, c, (i - 1) * 64:(i + 1) * 64], bank16(tpb, 128), pk(tpb), [f"oT{c}"])

            NROWS = cfg.get("na_rows", 32)
            for n in range(-2, NROWS + 1):
                if 0 <= n + 2 < NROWS:
                    row_qk(n + 2)
                if 0 <= n + 1 < NROWS:
                    row_rest(n + 1)
                if 0 <= n < NROWS:
                    row_pv(n)
                if 0 <= n - 1 < NROWS:
                    row_fin(n - 1)

    def proj8(wname, src3, src_keys, m3, ntok, woff, kcn=8, evac_scale=None, mtag="m"):
        Wd = [r3(A.bf16(woff + b * (kcn * 256), kcn * 128), kcn) for b in range(2)]
        ld(Wd[0], r3(wb[wname][0], kcn), [wkey(wname, 0)], ["Wd0"])
        n = 0
        for oc in range(8):
            b = oc % 2
            if oc + 1 < 8:
                ld(Wd[1 - b], r3(wb[wname][oc + 1], kcn), [wkey(wname, oc + 1)], [f"Wd{1 - b}"])
            for t4 in range(ntok // 512):
                pb = n % 4
                n += 1
                mm(bank(pb), [(Wd[b][:, k, :], src3[:, k, t4 * 512:(t4 + 1) * 512]) for k in range(kcn)],
                   [f"Wd{b}"] + src_keys, pk(pb))
                cp("act", m3[:, oc, t4 * 512:(t4 + 1) * 512], bank(pb), pk(pb), [f"{mtag}{oc}_{t4}"])

    def postnorm_residual(u, m3, ntok, tok0, gvec, sq_off, misc_off, x_init):
        hT = hT_view()
        sq_b = r3(A.bf16(sq_off, 8 * 512), 8)
        rstd = [A.f32(misc_off + b * 2 * KB, 512) for b in range(2)]
        tmpf = [A.f32(misc_off + 4 * KB + b * 2 * KB, 512) for b in range(2)]
        xst = [A.f32(misc_off + 8 * KB + b * 4 * KB, 1024) for b in range(2)]
        for t4 in range(ntok // 512):
            gt = (tok0 // 512) + t4
            b = t4 % 2
            norm_stats(lambda kc: (m3[:, kc, t4 * 512:(t4 + 1) * 512], [f"m{kc}_{t4}"]), 512, sq_b, rstd[b],
                       4 + b, 1.0 / D, f"rstd{b}")
            if x_init:
                for s in range(4):
                    xb = s % 2
                    tok = tok0 + t4 * 512 + s * 128
                    ld(xst[xb], xu[u, 256 + tok: 256 + tok + 128, :], [], [f"xst{xb}"])
                    for kc in range(KC):
                        tr(bank(6 + kc // 4, 128, (kc % 4) * 128), xst[xb][:, kc * 128:(kc + 1) * 128], ident_f,
                           [f"xst{xb}", "cst"], pk(6 + kc // 4))
                    cp("act", hT[:, :, tok:tok + 128],
                       PS[:, 6 * 512: 8 * 512].rearrange("p (k t) -> p k t", k=8), pk(6, 7), [f"h{gt}"])
            for kc in range(KC):
                tb = kc % 2
                stt(tmpf[tb], m3[:, kc, t4 * 512:(t4 + 1) * 512], gvec[:, kc:kc + 1], rstd[b], ALU.mult, ALU.mult,
                    [f"m{kc}_{t4}", f"rstd{b}", "gains"], [f"pnt{tb}"])
                tt("pool" if kc % 2 else "dve", hT[:, kc, tok0 + t4 * 512: tok0 + (t4 + 1) * 512],
                   hT[:, kc, tok0 + t4 * 512: tok0 + (t4 + 1) * 512], tmpf[tb], ALU.add,
                   [f"pnt{tb}", f"h{gt}"], [f"h{gt}"])

    def rmsnorm_fm(dst3, ntok, tok0, gvec, sq_off, misc_off, dtag):
        hT = hT_view()
        sq_b = r3(A.bf16(sq_off, 8 * 512), 8)
        rstd = [A.f32(misc_off + b * 2 * KB, 512) for b in range(2)]
        for t4 in range(ntok // 512):
            gt = (tok0 // 512) + t4
            b = t4 % 2
            norm_stats(lambda kc: (hT[:, kc, tok0 + t4 * 512: tok0 + (t4 + 1) * 512], [f"h{gt}"]), 512, sq_b,
                       rstd[b], 6 + b, 1.0 / D, f"rstd{b}")
            for kc in range(KC):
                src = hT[:, kc, tok0 + t4 * 512: tok0 + (t4 + 1) * 512]
                if gvec is not None:
                    stt(dst3[:, kc, t4 * 512:(t4 + 1) * 512], src, gvec[:, kc:kc + 1], rstd[b], ALU.mult, ALU.mult,
                        [f"h{gt}", f"rstd{b}", "gains"], [f"{dtag}{t4}"])
                else:
                    tt("dve", dst3[:, kc, t4 * 512:(t4 + 1) * 512], src, rstd[b], ALU.mult,
                       [f"h{gt}", f"rstd{b}"], [f"{dtag}{t4}"])

    def phase_wo_pn(u, L, wname, oT_off, m_off, w_off, sq_off, misc_off, x_init):
        oT = r3(A.bf16(oT_off, 8 * TOK), 8)
        m3 = r3(A.f32(m_off, 8 * 1024), 8)
        for half in range(2):
            src = oT[:, :, half * 1024:(half + 1) * 1024]
            proj8(wname, src, [f"oT{c}" for c in range(8)], m3, 1024, w_off)
            postnorm_residual(u, m3, 1024, half * 1024, gain(1, L), sq_off, misc_off, x_init)

    def phase_ffn(u, L):
        hT = hT_view()
        a2 = r3(A.bf16(64 * KB, 8 * 1024), 8)
        actT = r3(A.bf16(80 * KB, NF * 1024), NF)
        m3 = r3(A.f32(124 * KB, 8 * 1024), 8)
        Wgu = [[r3(A.bf16(156 * KB + (b * 2 + k) * 2 * KB, 1024), 8) for k in range(2)] for b in range(2)]
        sg = [A.f32(175 * KB + b * 2 * KB, 512) for b in range(2)]
        a2k = ["a2_0", "a2_1"]
        rmsnorm_fm(a2, 1024, 0, gain(2, L), 179 * KB, 187 * KB, "a2_")
        for half in range(2):

            def load_gu(f):
                b = f % 2
                ld(Wgu[b][0], r3(wb[f"fg{L}"][f], 8), [wkey(f"fg{L}", f)], [f"Wg{b}"])
                ld(Wgu[b][1], r3(wb[f"fu{L}"][f], 8), [wkey(f"fu{L}", f)], [f"Wu{b}"])
            load_gu(0)
            n = 0
            for f in range(NF):
                b = f % 2
                if f + 1 < NF:
                    load_gu(f + 1)
                for t2 in range(2):
                    pg_b = n % 2
                    pu_b = 2 + n % 2
                    sb_ = n % 2
                    n += 1
                    rhs = lambda kc: a2[:, kc, t2 * 512:(t2 + 1) * 512]
                    mm(bank(pg_b), [(Wgu[b][0][:, kc, :], rhs(kc)) for kc in range(KC)], [f"Wg{b}"] + a2k, pk(pg_b))
                    mm(bank(pu_b), [(Wgu[b][1][:, kc, :], rhs(kc)) for kc in range(KC)], [f"Wu{b}"] + a2k, pk(pu_b))
                    act(sg[sb_], bank(pg_b), AF.Silu, pk(pg_b), [f"sg{sb_}"])
                    tt("dve", actT[:, f, t2 * 512:(t2 + 1) * 512], sg[sb_], bank(pu_b), ALU.mult,
                       [f"sg{sb_}"] + list(pk(pu_b)), [f"act{f}"])
            proj8(f"fd{L}", actT, [f"act{f}" for f in range(NF)], m3, 1024, 164 * KB, kcn=NF)
            if half == 0:
                rmsnorm_fm(a2, 1024, 1024, gain(2, L), 179 * KB, 187 * KB, "a2_")
            postnorm_residual(u, m3, 1024, half * 1024, gain(3, L), 179 * KB, 187 * KB, False)

    def phase_ple(u, L, psrc):
        hT = hT_view()
        rT = r3(A.bf16(64 * KB, 8 * TOK), 8)
        pT = r3(A.bf16(96 * KB, 2 * TOK), 2)
        Wg = [r3(A.bf16(104 * KB + b * 2 * KB, 1024), 8) for b in range(2)]
        Wpp = [r3(A.bf16(108 * KB + b * 512, 256), 2) for b in range(2)]
        pst = [A.f32(109 * KB + b * KB, 256) for b in range(2)]
        sig = [A.f32(111 * KB + b * 2 * KB, 512) for b in range(2)]
        tmpf = [A.f32(115 * KB + b * 2 * KB, 512) for b in range(2)]
        rmsnorm_fm(rT, TOK, 0, None, 119 * KB, 127 * KB, "rT_")
        for t in range(16):
            b = t % 2
            ld(pst[b], psrc[t * 128:(t + 1) * 128, :], [], [f"pst{b}"])
            for j in range(2):
                tr(bank(4 + b, 128, j * 128), pst[b][:, j * 128:(j + 1) * 128], ident_f, [f"pst{b}", "cst"], pk(4 + b))
            cp("act", pT[:, :, t * 128:(t + 1) * 128], r3(bank(4 + b, 256), 2), pk(4 + b), [f"pT{t // 4}"])
        ld(Wg[0], r3(wb[f"pg{L}"][0], 8), [wkey(f"pg{L}", 0)], ["Wpg0"])
        ld(Wpp[0], r3(wb[f"pp{L}"][0], 2), [wkey(f"pp{L}", 0)], ["Wpp0"])
        n = 0
        for oc in range(8):
            b = oc % 2
            if oc + 1 < 8:
                ld(Wg[1 - b], r3(wb[f"pg{L}"][oc + 1], 8), [wkey(f"pg{L}", oc + 1)], [f"Wpg{1 - b}"])
                ld(Wpp[1 - b], r3(wb[f"pp{L}"][oc + 1], 2), [wkey(f"pp{L}", oc + 1)], [f"Wpp{1 - b}"])
            for t4 in range(4):
                gb_ = n % 2
                pb_ = 2 + n % 2
                s_ = n % 2
                n += 1
                mm(bank(gb_), [(Wg[b][:, kc, :], rT[:, kc, t4 * 512:(t4 + 1) * 512]) for kc in range(KC)],
                   [f"Wpg{b}", f"rT_{t4}"], pk(gb_))
                mm(bank(pb_), [(Wpp[b][:, j, :], pT[:, j, t4 * 512:(t4 + 1) * 512]) for j in range(2)],
                   [f"Wpp{b}", f"pT{t4}"], pk(pb_))
                act(sig[s_], bank(gb_), AF.Sigmoid, pk(gb_), [f"sig{s_}"])
                tt("dve", tmpf[s_], sig[s_], bank(pb_), ALU.mult, [f"sig{s_}"] + list(pk(pb_)), [f"plt{s_}"])
                tt("pool", hT[:, oc, t4 * 512:(t4 + 1) * 512], hT[:, oc, t4 * 512:(t4 + 1) * 512], tmpf[s_], ALU.add,
                   [f"plt{s_}", f"h{t4}"], [f"h{t4}"])

    QT_OFF = 96 * KB

    def phase_kv(u, with_q):
        a1 = r3(A.bf16(64 * KB, 8 * TOK), 8)
        qT = r3(A.bf16(QT_OFF, 8 * TOK), 8)
        kv_f = [A.f32(128 * KB + i * 2 * KB, 512) for i in range(6)]
        knb = [A.bf16(140 * KB + b * KB, 512) for b in range(2)]
        sqb = [A.bf16(142 * KB + b * KB, 512) for b in range(2)]
        kTg = [A.bf16(144 * KB + b * 4 * KB, TOK) for b in range(2)]
        Vst = r4(A.bf16(152 * KB, 4 * 16 * 128), 4, 16)
        Wk = [r3(A.bf16(168 * KB + b * 2 * KB, 1024), 8) for b in range(2)]
        Wv = r3(A.bf16(172 * KB, 8 * 512), 8)
        cs = [[A.f32(180 * KB + (b * 2 + k) * 2 * KB, 512) for k in range(2)] for b in range(2)]
        rmsnorm_fm(a1, TOK, 0, gain(0, 1), 188 * KB, 128 * KB + 8 * KB, "a1_")
        pg.barrier()
        a1k = [f"a1_{t}" for t in range(4)]
        heads = [("gk", g, False) for g in range(4)]
        if with_q:
            heads += [("gq", h, True) for h in range(8)]
        kraw = [kv_f[0], kv_f[1]]
        kn = [kv_f[2], kv_f[3]]
        rstd2 = [kv_f[4], kv_f[5]]
        t1 = [A.f32(188 * KB + b * 2 * KB, 512) for b in range(2)]
        t2 = [A.f32(192 * KB + b * 2 * KB, 512) for b in range(2)]
        work = [(hi, t4) for hi in range(len(heads)) for t4 in range(4)]

        def stage_a(n):
            hi, t4 = work[n]
            wn, hidx, isq = heads[hi]
            b = hi % 2
            cb = n % 2
            if t4 == 0:
                ld(Wk[b], r3(wb[wn][hidx], 8), [wkey(wn, hidx)], [f"Wk{b}"])
            gcol = gqk_f[:, 2:3] if isq else gqk_f[:, 1:2]
            ld(cs[cb][0], ropec[u, :, t4 * 512:(t4 + 1) * 512], [], [f"cos{cb}"])
            ld(cs[cb][1], ropes[u, :, t4 * 512:(t4 + 1) * 512], [], [f"sin{cb}"])
            mm(bank(cb), [(Wk[b][:, kc, :], a1[:, kc, t4 * 512:(t4 + 1) * 512]) for kc in range(KC)],
               [f"Wk{b}"] + a1k, pk(cb))
            cp("act", kraw[cb], bank(cb), pk(cb), [f"kraw{cb}"])
            act(sqb[cb], bank(cb), AF.Square, pk(cb), [f"sqb{cb}"])
            mm(bank(2 + cb), [(ones_b, sqb[cb])], [f"sqb{cb}", "ones_b"], pk(2 + cb))
            act(rstd2[cb], bank(2 + cb), AF.Sqrt, pk(2 + cb), [f"krs{cb}"], scale=1.0 / 128, bias=sm_f[:, 0:1])
            pg.op("dve", lambda e, r=rstd2[cb]: e.reciprocal(r, r), [f"krs{cb}"], [f"krs{cb}"])
            stt(kn[cb], kraw[cb], gcol, rstd2[cb], ALU.mult, ALU.mult, [f"kraw{cb}", f"krs{cb}", "gqk", "gqs"], [f"kn{cb}"])
            cp("pool", knb[cb], kn[cb], [f"kn{cb}"], [f"knb{cb}"])

        def stage_b(n):
            hi, t4 = work[n]
            wn, hidx, isq = heads[hi]
            b = hi % 2
            cb = n % 2
            mm(bank(4 + cb), [(rm_b, knb[cb])], [f"knb{cb}", "rm_b"], pk(4 + cb))
            tt("pool", t1[cb], kn[cb], cs[cb][0], ALU.mult, [f"kn{cb}", f"cos{cb}"], [f"t1{cb}"])
            tt("dve", t2[cb], bank(4 + cb), cs[cb][1], ALU.mult, list(pk(4 + cb)) + [f"sin{cb}"], [f"t2{cb}"])
            if isq:
                tt("dve", qT[:, hidx, t4 * 512:(t4 + 1) * 512], t1[cb], t2[cb], ALU.add, [f"t1{cb}", f"t2{cb}"], [f"qT{hidx}"])
            else:
                tt("dve", kTg[b][:, t4 * 512:(t4 + 1) * 512], t1[cb], t2[cb], ALU.add, [f"t1{cb}", f"t2{cb}"], [f"kTg{b}"])
                if t4 == 3:
                    ld(kvs[u, hidx, 0], kTg[b], [f"kTg{b}"], [f"kvs{u}_{hidx}_0"], q="pool")

        stage_a(0)
        for n in range(len(work)):
            if n + 1 < len(work):
                stage_a(n + 1)
            stage_b(n)
        ld(Wv, r3(wb["gv"][0], 8), [wkey("gv", 0)], ["Wv"])
        for t in range(16):
            pb = 6 + t % 2
            mm(bank(pb), [(a1[:, kc, t * 128:(t + 1) * 128], Wv[:, kc, :]) for kc in range(KC)], ["Wv"] + a1k, pk(pb))
            cp("act" if t % 2 else "dve", Vst[:, :, t, :], r3(bank(pb), 4), pk(pb), ["Vst"])
        for g in range(4):
            ld(kvs[u, g, 1], Vst[:, g].rearrange("p t d -> p (t d)"), ["Vst"], [f"kvs{u}_{g}_1"], q="pool")

    OT1_OFF = 64 * KB

    def phase_attn(u, key_units):
        qT = r3(A.bf16(QT_OFF, 8 * TOK), 8)
        oT = r3(A.bf16(OT1_OFF, 8 * TOK), 8)
        ring = [(A.bf16(128 * KB + b * 8 * KB, TOK), r3(A.bf16(128 * KB + b * 8 * KB + 4 * KB, TOK), 16))
                for b in range(3)]
        NPT = 6
        PT = [A.bf16(152 * KB + b * KB, 512) for b in range(NPT)]
        rc = [A.f32(164 * KB + b * 2 * KB, 512) for b in range(2)]
        seq = [(g, q4, ku) for g in range(4) for q4 in range(4) for ku in key_units]

        def load(idx):
            g, q4, ku = seq[idx]
            b = idx % 3
            ld(ring[b][0], kvs[ku, g, 0], [f"kvs{ku}_{g}_0"], [f"rk{b}"])
            ld(ring[b][1].rearrange("p t d -> p (t d)"), kvs[ku, g, 1], [f"kvs{ku}_{g}_1"], [f"rv{b}"])
        its = [(idx, kt, hh) for idx in range(len(seq)) for kt in range(16) for hh in range(2)]

        def emit_s(n):
            idx, kt, hh = its[n]
            g, q4, ku = seq[idx]
            b = idx % 3
            h = 2 * g + hh
            sbk = 4 + n % 4
            mm(bank(sbk), [(ring[b][0][:, kt * 128:(kt + 1) * 128], qT[:, h, q4 * 512:(q4 + 1) * 512])],
               [f"rk{b}", f"qT{h}"], pk(sbk))
            act(PT[n % NPT], bank(sbk), AF.Exp, pk(sbk), [f"PTa{n % NPT}"])

        PP = [[A.bf16(160 * KB + (hh * 2 + m) * KB, 512) for m in range(2)] for hh in range(2)]
        osb = [A.f32(168 * KB + b * 2 * KB, 512) for b in range(2)]
        pending = []

        def flush(upto):
            while pending and pending[0][0] <= upto:
                _, fn_ = pending.pop(0)
                fn_()

        def emit_pv(n):
            idx, kt, hh = its[n]
            g, q4, ku = seq[idx]
            b = idx % 3
            first = (ku == key_units[0])
            last = (ku == key_units[-1])
            st = first and kt == 0
            sp_ = last and kt == 15
            pt = PT[n % NPT]
            Vt = ring[b][1]
            flush(n)
            pg.op("pe", lambda e, hh=hh, pt=pt, Vt=Vt, kt=kt, st=st, sp_=sp_:
                  e.matmul(bank(hh), Vt[:, kt, :], pt, start=st, stop=sp_),
                  [f"PTa{n % NPT}", f"rv{b}"], pk(hh))
            if kt % 2 == 1:
                m_ = (kt // 2) % 2
                pp = PP[hh][m_]
                tt("pool" if hh == 0 else "dve", pp, PT[n % NPT], PT[(n - 2) % NPT], ALU.add,
                   [f"PTa{n % NPT}", f"PTa{(n - 2) % NPT}"], [f"PP{hh}{m_}"])
                st2 = first and kt == 1
                sp2 = last and kt == 15

                def ones_mm(hh=hh, pp=pp, st2=st2, sp2=sp2, m_=m_):
                    pg.op("pe", lambda e: e.matmul(bank(2 + hh), ones_b, pp, start=st2, stop=sp2),
                          [f"PP{hh}{m_}", "ones_b"], pk(2 + hh))
                pending.append((n + 3, ones_mm))
            if sp_:
                flush(1 << 60)
                h = 2 * g + hh
                cp("act", osb[hh], bank(hh), pk(hh), [f"osb{hh}"])
                cp("act", rc[hh], bank(2 + hh), pk(2 + hh), [f"rc{hh}"])
                pg.op("dve", lambda e, hh=hh: e.reciprocal(rc[hh], rc[hh]), [f"rc{hh}"], [f"rc{hh}"])
                tt("dve", oT[:, h, q4 * 512:(q4 + 1) * 512], osb[hh], rc[hh], ALU.mult,
                   [f"osb{hh}", f"rc{hh}"], [f"oT{h}"])
            if kt == 0 and hh == 0 and idx + 2 < len(seq):
                load(idx + 2)

        load(0)
        if len(seq) > 1:
            load(1)
        AHEAD = cfg.get("ahead", 2)
        for n in range(min(AHEAD, len(its))):
            emit_s(n)
        for n in range(len(its)):
            if n + AHEAD < len(its):
                emit_s(n + AHEAD)
            emit_pv(n)

    def phase_out(dst):
        hT = hT_view()
        ost = [A.f32(64 * KB + b * 4 * KB, 1024) for b in range(2)]
        for t in range(16):
            b = t % 2
            for kc in range(KC):
                tr(bank(2 * b + kc // 4, 128, (kc % 4) * 128), hT[:, kc, t * 128:(t + 1) * 128], ident_f,
                   [f"h{t // 4}", "cst"], pk(2 * b + kc // 4))
            cp("act" if b else "dve", ost[b], PS[:, 2 * b * 512: (2 * b + 2) * 512], pk(2 * b, 2 * b + 1), [f"ost{b}"])
            ld(dst[t * 128:(t + 1) * 128, :], ost[b], [f"ost{b}"], [f"y{t}"], q="pool")

    def layer0(u):
        phase_x(u)
        if cfg.get("stop") == "x":
            pg.barrier()
            return
        phase_na(u)
        pg.barrier()
        if cfg.get("stop") == "na":
            oT = r3(A.bf16(OT0_OFF, 8 * TOK), 8)
            stg = [A.f32(100 * KB + b * 2 * KB, 512) for b in range(2)]
            ypv = yp.rearrange("(k p a) f -> k p (a f)", k=8, p=128)
            n = 0
            for kc in range(8):
                for t4 in range(4):
                    b = n % 2
                    n += 1
                    cp("dve", stg[b], oT[:, kc, t4 * 512:(t4 + 1) * 512], [], [f"dstg{b}"])
                    ld(ypv[kc][:, t4 * 512:(t4 + 1) * 512], stg[b], [f"dstg{b}"], [f"dy{n}"], q="pool")
            pg.barrier()
            return
        phase_wo_pn(u, 0, "na_o", OT0_OFF, 96 * KB, 128 * KB, 132 * KB, 140 * KB, True)
        pg.barrier()
        if cfg.get("stop") == "wo":
            return
        phase_ffn(u, 0)
        pg.barrier()
        if cfg.get("stop") == "ffn":
            return
        phase_ple(u, 0, p0u[u])
        pg.barrier()

    def layer1_rest(u, pidx, dst):
        phase_wo_pn(u, 1, "go", OT1_OFF, 96 * KB, 128 * KB, 132 * KB, 140 * KB, False)
        pg.barrier()
        phase_ffn(u, 1)
        pg.barrier()
        phase_ple(u, 1, p1u[pidx])
        pg.barrier()
        phase_out(dst)
        pg.barrier()

    for u in units_l0:
        layer0(u)
        if dbg == "l0" and u == units_l0[-1]:
            if cfg.get("stop") != "na":
                phase_out(yp)
            pg.barrier()
            break
        if not do_l1:
            continue
        is_q = u in (U_OWN, U_PROMPT)
        phase_kv(u, is_q)
        pg.barrier()
        if u == U_OWN:
            phase_attn(u, list(range(8)))
            pg.barrier()
            layer1_rest(u, 0, ys)
        elif u == U_PROMPT:
            phase_attn(u, [U_PROMPT])
            pg.barrier()
            layer1_rest(u, 1, yp)
    pg.barrier()

    for s in pg.sem_names:
        pg.handles[s] = nc.alloc_semaphore(f"s_{s}")
    with nc.Block() as block:
        pg.emit(block)
    return nc, pg


def _na_static():
    qc = np.arange(64)[:, None]
    kc = np.arange(64)[None, :]
    wstart = np.clip(qc - 8, 0, 48)
    colmask = (kc >= wstart) & (kc < wstart + 16)
    return colmask


def _utab(rpb, interior=False):
    colmask = _na_static()
    out = np.full((8, 128, 2, ND, 64), NEG, np.float32)
    kp = np.arange(128)
    kcol = kp % 64
    khalf = kp // 64
    qcs = np.arange(64)
    ci = np.clip(kcol[:, None] - qcs[None, :] + 15, 0, 30)
    cm = colmask.T[kcol, :]
    for c in range(8):
        for hd in range(2):
            h = 2 * c + hd
            for di in range(ND):
                d = di - 7
                ri = d + 7 + khalf
                ok = (ri >= 0) & (ri <= 14)
                ric = np.clip(ri, 0, 14)
                vals = rpb[h][ric[:, None], ci]
                if interior:
                    ok = ok & (ri >= 3) & (ri <= 10)
                out[c, :, hd, di, :] = np.where(cm & ok[:, None], vals, np.float32(NEG))
    return out.reshape(8, 128, 2 * ND * 64)


def _rv_masks(g0, R):
    out = np.zeros((128, 48), np.float32)
    for e, l in enumerate(list(range(4, 8)) + list(range(32, 36))):
        g = g0 + l
        rs = min(max(g - 4, 0), R - 8)
        tiles = range(0, 6) if l < 8 else range(14, 20)
        for j, t in enumerate(tiles):
            for half in range(2):
                gk = g0 + 2 * t + half
                valid = (rs <= gk < rs + 8)
                if not valid:
                    out[half * 64:(half + 1) * 64, e * 6 + j] = NEG
    return out


def _rope_tables(tok0):
    t = np.arange(tok0, tok0 + TOK)
    row = (t // 64).astype(np.float32)
    col = (t % 64).astype(np.float32)
    inv = (np.float32(10000.0) ** (-np.arange(0, 64, 2, dtype=np.float32) / np.float32(64))).astype(np.float32)
    ang = np.zeros((128, TOK), np.float32)
    for d in range(128):
        f = d % 32
        pos = row if d < 64 else col
        ang[d] = (pos * inv[f]).astype(np.float32)
    return np.cos(ang).astype(np.float32), np.sin(ang).astype(np.float32)


def _consts():
    c = np.zeros((128, 387), np.float32)
    c[:, 0:128] = np.eye(128, dtype=np.float32)
    c[:, 128:256] = 1.0
    rm = np.zeros((128, 128), np.float32)
    for m in range(128):
        if (m % 64) < 32:
            rm[m + 32, m] = -1.0
        else:
            rm[m - 32, m] = 1.0
    c[:, 256:384] = rm
    c[0:64, 385] = NEG
    c[64:128, 386] = NEG
    return c


_CACHE = {}


def _prep_inputs(inp):
    xs = inp["x_sample"][0]
    xpad = np.concatenate([np.zeros((256, D), np.float32), xs, np.zeros((256, D), np.float32)], axis=0)
    zeros_h = np.zeros((256, D), np.float32)
    g_all = np.stack([inp["mix_pre_norm"], inp["mix_post_norm"], inp["ffn_pre_norm"], inp["ffn_post_norm"]], 0)
    gains = np.ascontiguousarray(g_all.reshape(4, 2, 8, 128).transpose(3, 0, 1, 2).reshape(128, 64)).astype(np.float32)
    gqk = np.ascontiguousarray(np.stack([inp["gqa_q_norm"][0], inp["gqa_k_norm"][0]], 1)).astype(np.float32)
    utab = _utab(np.asarray(inp["na_rpb"][0], np.float32))
    utab2 = _utab(np.asarray(inp["na_rpb"][0], np.float32), interior=True)
    cst = _consts()
    rope_p = _rope_tables(0)
    maps = []
    for c in range(NCORES):
        chunks = [(c + 1 + u) % 8 for u in range(7)] + [c]
        xu = np.empty((NUNITS, HTOK, D), np.float32)
        p0 = np.empty((NUNITS, TOK, 256), np.float32)
        rv = np.empty((NUNITS, 128, 48), np.float32)
        rc = np.empty((NUNITS, 128, TOK), np.float32)
        rs = np.empty((NUNITS, 128, TOK), np.float32)
        for u, gj in enumerate(chunks):
            xu[u] = xpad[gj * TOK: gj * TOK + HTOK]
            p0[u] = inp["p_sample"][0, 0, gj * TOK:(gj + 1) * TOK]
            rv[u] = _rv_masks(32 * gj - 4, 256)
            key = ("rope", gj)
            if key not in _CACHE:
                _CACHE[key] = _rope_tables(gj * TOK)
            rc[u], rs[u] = _CACHE[key]
        xu[U_PROMPT] = np.concatenate([zeros_h, inp["x_prompt"][c], zeros_h], 0)
        p0[U_PROMPT] = inp["p_prompt"][0, c]
        rv[U_PROMPT] = _rv_masks(-4, 32)
        rc[U_PROMPT], rs[U_PROMPT] = rope_p
        p1 = np.stack([inp["p_sample"][1, 0, c * TOK:(c + 1) * TOK], inp["p_prompt"][1, c]], 0)
        maps.append({
            "xu": xu, "p0u": p0, "p1u": np.ascontiguousarray(p1), "rv": rv, "ropec": rc, "ropes": rs,
            "utab": utab, "utab2": utab2, "cst": cst, "gains": gains, "gqk": gqk,
            "na_w_qkv": inp["na_w_qkv"][0], "na_w_o": inp["na_w_o"][0],
            "gqa_w_qkv": inp["gqa_w_qkv"][0], "gqa_w_o": inp["gqa_w_o"][0],
            "ffn_w_gate_up": inp["ffn_w_gate_up"], "ffn_w_down": inp["ffn_w_down"],
            "ple_w_gate": inp["ple_w_gate"], "ple_w_proj": inp["ple_w_proj"],
        })
    return maps


def kernel(**inputs):
    inp = {k: np.asarray(v) for k, v in inputs.items()}
    maps = _prep_inputs(inp)
    nc, pg = build()
    res = run_bass_kernel_spmd(nc, maps, core_ids=list(range(NCORES)))
    y_prompt = np.stack([np.asarray(res.results[c]["yp"], np.float32) for c in range(NCORES)], 0)
    y_sample = np.concatenate([np.asarray(res.results[c]["ys"], np.float32) for c in range(NCORES)], 0)[None]
    return (y_prompt, y_sample)
```

```python
import numpy as np
import concourse.bass as bass
import concourse.mybir as mybir
from concourse.bass_utils import run_bass_kernel_spmd

F32 = mybir.dt.float32
BF16 = mybir.dt.bfloat16
AF = mybir.ActivationFunctionType
ALU = mybir.AluOpType

NCORES = 8
D = 1024
KC = 8
TOK = 2048
HTOK = 2560
DFF = 2816
NF = 22
EPS = 1e-6
NEG = -30000.0
NUNITS = 9
U_OWN = 7
U_PROMPT = 8
ND = 14
KB = 1024


class Prog:
    ENG = ("pe", "act", "dve", "pool", "sp")

    def __init__(self, nc, n_dma_sp=24, n_dma_pool=8):
        self.nc = nc
        self.ops = {e: [] for e in self.ENG}
        self.cnt = {}
        self.waited = {e: {} for e in self.ENG}
        self.last_w = {}
        self.readers = {}
        self.sem_names = ["pe", "act", "dve", "pool"]
        self.dma_slots = {"sp": [f"dsp{i}" for i in range(n_dma_sp)],
                          "pool": [f"dpl{i}" for i in range(n_dma_pool)]}
        self.dma_next = {"sp": 0, "pool": 0}
        for q in self.dma_slots.values():
            self.sem_names += q
        for s in self.sem_names:
            self.cnt[s] = 0
        self.handles = {}
        self.nops = 0

    def _deps(self, reads, writes):
        deps = set()
        for k in reads:
            if k in self.last_w:
                deps.add(self.last_w[k])
        for k in writes:
            if k in self.last_w:
                deps.add(self.last_w[k])
            for r in self.readers.get(k, ()):
                deps.add(r)
        return deps

    def _record(self, reads, writes, done):
        for k in reads:
            self.readers.setdefault(k, []).append(done)
        for k in writes:
            self.last_w[k] = done
            self.readers[k] = []

    def _waits(self, eng, deps):
        need = {}
        for (s, v) in deps:
            if eng == "pe" and s == "pe":
                continue
            if self.waited[eng].get(s, 0) < v:
                need[s] = max(need.get(s, 0), v)
        for s, v in need.items():
            self.waited[eng][s] = v
        return list(need.items())

    def op(self, eng, fn, reads=(), writes=()):
        deps = self._deps(reads, writes)
        waits = self._waits(eng, deps)
        self.cnt[eng] += 1
        done = (eng, self.cnt[eng])
        self.ops[eng].append((waits, fn, (eng, 1)))
        self._record(reads, writes, done)
        self.nops += 1
        return done

    def dma(self, q, fn, reads=(), writes=()):
        slots = self.dma_slots[q]
        s = slots[self.dma_next[q] % len(slots)]
        self.dma_next[q] += 1
        deps = self._deps(reads, writes)
        if self.cnt[s] > 0:
            deps.add((s, self.cnt[s]))
        waits = self._waits(q, deps)
        self.cnt[s] += 16
        done = (s, self.cnt[s])
        self.ops[q].append((waits, fn, (s, 16)))
        self._record(reads, writes, done)
        self.nops += 1
        return done

    def barrier(self, engines=None):
        deps = set((s, c) for s, c in self.cnt.items() if c > 0)
        for e in (engines or self.ENG):
            waits = self._waits(e, set(deps))
            if waits:
                self.ops[e].append((waits, None, None))
        self.last_w = {}
        self.readers = {}

    def emit(self, block):
        H = self.handles

        class FirstCatcher:
            def __init__(self, e):
                self._e = e
                self.first = None

            def __getattr__(self, name):
                attr = getattr(self._e, name)
                if not callable(attr):
                    return attr

                def w(*a, **k):
                    r = attr(*a, **k)
                    if self.first is None and hasattr(r, "then_inc"):
                        self.first = r
                    return r
                return w

        def run(eng_name):
            def body(e):
                for (waits, fn, sig) in self.ops[eng_name]:
                    if fn is None:
                        for (s, v) in waits:
                            e.wait_ge(H[s], v)
                        continue
                    for (s, v) in waits[:-1]:
                        e.wait_ge(H[s], v)
                    fc = FirstCatcher(e)
                    ins = fn(fc)
                    if waits:
                        s, v = waits[-1]
                        fc.first._wait_ge(H[s], v)
                    if sig is not None:
                        ins.then_inc(H[sig[0]], sig[1])
            return body

        block.tensor(run("pe"))
        block.scalar(run("act"))
        block.vector(run("dve"))
        block.gpsimd(run("pool"))
        block.sync(run("sp"))


class Arena:
    def __init__(self, nc, nbytes):
        self.t = nc.alloc_sbuf_tensor("arena", [128, nbytes // 4], F32)
        self.t16 = self.t.bitcast(BF16)
        self.nbytes = nbytes

    def f32(self, off, n):
        assert off % 4 == 0 and off + 4 * n <= self.nbytes, (off, n)
        return self.t[:, off // 4: off // 4 + n]

    def bf16(self, off, n):
        assert off % 2 == 0 and off + 2 * n <= self.nbytes, (off, n)
        return self.t16[:, off // 2: off // 2 + n]


def r3(ap, a):
    return ap.rearrange("p (a b) -> p a b", a=a)


def r4(ap, a, b):
    return ap.rearrange("p (a b c) -> p a b c", a=a, b=b)


def na_row_spec(i):
    l = i + 4
    if l < 8:
        tiles = list(range(0, 6))
        masks = [3 + (l - 4) * 6 + j for j in range(6)]
    elif l >= 32:
        tiles = list(range(14, 20))
        masks = [3 + 24 + (l - 32) * 6 + j for j in range(6)]
    elif l % 2 == 0:
        tiles = list(range(l // 2 - 2, l // 2 + 2))
        masks = [0, 0, 0, 0]
    else:
        t0 = (l - 5) // 2
        tiles = list(range(t0, t0 + 5))
        masks = [1, 0, 0, 0, 2]
    return l, tiles, masks


def build(cfg=None):
    cfg = cfg or {}
    units_l0 = cfg.get("units", list(range(NUNITS)))
    do_l1 = cfg.get("do_l1", True)
    dbg = cfg.get("dbg", None)

    nc = bass.Bass("TRN2", target_bir_lowering=False)
    pg = Prog(nc)

    def din(name, shape, dt=F32):
        return nc.dram_tensor(name, list(shape), dt, kind="ExternalInput").ap()

    xu = din("xu", [NUNITS, HTOK, D])
    p0u = din("p0u", [NUNITS, TOK, 256])
    p1u = din("p1u", [2, TOK, 256])
    rvd = din("rv", [NUNITS, 128, 48])
    ropec = din("ropec", [NUNITS, 128, TOK])
    ropes = din("ropes", [NUNITS, 128, TOK])
    utab = din("utab", [8, 128, 2 * ND * 64])
    utab2 = din("utab2", [8, 128, 2 * ND * 64])
    cst = din("cst", [128, 387])
    gains = din("gains", [128, 64])
    gqk = din("gqk", [128, 2])
    w_na_qkv = din("na_w_qkv", [D, 3072])
    w_na_o = din("na_w_o", [D, D])
    w_gqa_qkv = din("gqa_w_qkv", [D, 2048])
    w_gqa_o = din("gqa_w_o", [D, D])
    w_gu = din("ffn_w_gate_up", [2, D, 2 * DFF])
    w_dn = din("ffn_w_down", [2, DFF, D])
    w_pg = din("ple_w_gate", [2, D, D])
    w_pp = din("ple_w_proj", [2, 256, D])

    yp = nc.dram_tensor("yp", [TOK, D], F32, kind="ExternalOutput").ap()
    ys = nc.dram_tensor("ys", [TOK, D], F32, kind="ExternalOutput").ap()

    def wscratch(name, npanel, kcn, ow):
        return nc.dram_tensor(name, [npanel, 128, kcn * ow], BF16).ap()

    wb = {
        "na_q": wscratch("wb_na_q", 8, 8, 128), "na_k": wscratch("wb_na_k", 8, 8, 128),
        "na_v": wscratch("wb_na_v", 2, 8, 512), "na_o": wscratch("wb_na_o", 8, 8, 128),
        "gq": wscratch("wb_gq", 8, 8, 128), "gk": wscratch("wb_gk", 4, 8, 128),
        "gv": wscratch("wb_gv", 1, 8, 512), "go": wscratch("wb_go", 8, 8, 128),
    }
    for L in range(2):
        wb[f"fg{L}"] = wscratch(f"wb_fg{L}", NF, 8, 128)
        wb[f"fu{L}"] = wscratch(f"wb_fu{L}", NF, 8, 128)
        wb[f"fd{L}"] = wscratch(f"wb_fd{L}", 8, NF, 128)
        wb[f"pg{L}"] = wscratch(f"wb_pg{L}", 8, 8, 128)
        wb[f"pp{L}"] = wscratch(f"wb_pp{L}", 8, 2, 128)
    kvs = nc.dram_tensor("kvs", [NUNITS, 4, 2, 128, TOK], BF16).ap()

    A = Arena(nc, 206 * KB)
    PB = 196 * KB
    cst_f = A.f32(PB, 387); PB += 387 * 4 + 4
    ident_f = cst_f[:, 0:128]
    ident_b = A.bf16(PB, 128); PB += 256
    ones_b = A.bf16(PB, 128); PB += 256
    rm_b = A.bf16(PB, 128); PB += 256
    gains_f = A.f32(PB, 64); PB += 256
    gqk_f = A.f32(PB, 4); PB += 16
    rv_f = A.f32(PB, 51); PB += 208
    sm_f = A.f32(PB, 64); PB += 256
    assert PB <= 206 * KB

    def gain(kind, layer):
        i = (kind * 2 + layer) * 8
        return gains_f[:, i:i + 8]

    PS = nc.alloc_psum_tensor("psum_all", [128, 4096], F32)
    PS16 = PS.bitcast(BF16)

    def bank(i, n=512, o=0):
        return PS[:, i * 512 + o: i * 512 + o + n]

    def bank16(i, n=1024, o=0):
        return PS16[:, i * 1024 + o: i * 1024 + o + n]

    def pk(*banks):
        return tuple(f"ps{b}" for b in banks)

    def mm(out, pairs, reads, writes):
        def fn(e):
            n = len(pairs)
            ins = None
            for i, (l, r) in enumerate(pairs):
                ins = e.matmul(out, l, r, start=(i == 0), stop=(i == n - 1))
            return ins
        return pg.op("pe", fn, reads, writes)

    def tr(out, in_, idn, reads, writes):
        return pg.op("pe", lambda e: e.transpose(out, in_, idn), reads, writes)

    def act(out, in_, func, reads, writes, **kw):
        return pg.op("act", lambda e: e.activation(out=out, in_=in_, func=func, **kw), reads, writes)

    def tt(eng, out, a, b, op, reads, writes):
        return pg.op(eng, lambda e: e.tensor_tensor(out, a, b, op), reads, writes)

    def ts(eng, out, a, s1, s2, op0, op1, reads, writes):
        return pg.op(eng, lambda e: e.tensor_scalar(out, a, s1, s2, op0, op1), reads, writes)

    def stt(out, a, s, b, op0, op1, reads, writes):
        return pg.op("dve", lambda e: e.scalar_tensor_tensor(out, a, s, b, op0, op1), reads, writes)

    def cp(eng, out, in_, reads, writes):
        if eng == "act":
            return pg.op("act", lambda e: e.copy(out, in_), reads, writes)
        return pg.op(eng, lambda e: e.tensor_copy(out, in_), reads, writes)

    def ld(out, in_, reads, writes, q="sp"):
        return pg.dma(q, lambda e: e.dma_start(out=out, in_=in_), reads, writes)

    ld(cst_f, cst, [], ["cst"])
    ld(gains_f, gains, [], ["gains"])
    ld(gqk_f[:, 0:2], gqk, [], ["gqk"])
    cp("dve", ident_b, cst_f[:, 0:128], ["cst"], ["ident_b"])
    cp("dve", ones_b, cst_f[:, 128:256], ["cst"], ["ones_b"])
    cp("dve", rm_b, cst_f[:, 256:384], ["cst"], ["rm_b"])
    cp("dve", rv_f[:, 0:3], cst_f[:, 384:387], ["cst"], ["rvc"])
    ts("dve", gqk_f[:, 2:3], gqk_f[:, 0:1], float(128 ** -0.5), None, ALU.mult, ALU.bypass, ["gqk"], ["gqs"])

    conv = []
    for c in range(8):
        conv.append((w_na_qkv, c * 128, 8, 128, "na_q", c))
        conv.append((w_na_qkv, 1024 + c * 128, 8, 128, "na_k", c))
    for g in range(2):
        conv.append((w_na_qkv, 2048 + g * 512, 8, 512, "na_v", g))
    for c in range(8):
        conv.append((w_na_o, c * 128, 8, 128, "na_o", c))
    for L in range(2):
        for f in range(NF):
            conv.append((w_gu[L], f * 128, 8, 128, f"fg{L}", f))
            conv.append((w_gu[L], DFF + f * 128, 8, 128, f"fu{L}", f))
        for c in range(8):
            conv.append((w_dn[L], c * 128, NF, 128, f"fd{L}", c))
            conv.append((w_pg[L], c * 128, 8, 128, f"pg{L}", c))
            conv.append((w_pp[L], c * 128, 2, 128, f"pp{L}", c))
    for c in range(8):
        conv.append((w_gqa_qkv, c * 128, 8, 128, "gq", c))
    for c in range(4):
        conv.append((w_gqa_qkv, 1024 + c * 128, 8, 128, "gk", c))
    conv.append((w_gqa_qkv, 1536, 8, 512, "gv", 0))
    for c in range(8):
        conv.append((w_gqa_o, c * 128, 8, 128, "go", c))

    NSTG = 8
    stg_f = [A.f32(i * 16 * KB, 4096) for i in range(NSTG)]
    stg_b = [A.bf16(128 * KB + i * 8 * KB, 4096) for i in range(NSTG)]
    ceng = ["pool", "dve", "act"]
    for i, (src, col0, kcn, ow, name, panel) in enumerate(conv):
        b = i % NSTG
        n = kcn * ow
        srcv = src.rearrange("(kc p) o -> p kc o", p=128)[:, :, col0:col0 + ow]
        ld(r3(stg_f[b][:, 0:n], kcn), srcv, [], [f"stgf{b}"])
        cp(ceng[i % 3], stg_b[b][:, 0:n], stg_f[b][:, 0:n], [f"stgf{b}"], [f"stgb{b}"])
        ld(wb[name][panel], stg_b[b][:, 0:n], [f"stgb{b}"], [f"wb_{name}_{panel}"], q="pool")
    pg.barrier()
    WB_KEYS = {}

    def wkey(name, panel):
        return f"wb_{name}_{panel}"

    HT_OFF = 0

    def hT_view():
        return r3(A.f32(HT_OFF, 8 * TOK), 8)

    def norm_stats(src_fn, ntok, sq_b, rstd_f, psb, inv_n, rkey):
        for kc in range(KC):
            s_ap, rk = src_fn(kc)
            eng = "act" if kc % 2 == 0 else "pool"
            if eng == "act":
                act(sq_b[:, kc, 0:ntok], s_ap, AF.Square, rk, [f"nsq{kc}"])
            else:
                tt("pool", sq_b[:, kc, 0:ntok], s_ap, s_ap, ALU.mult, rk, [f"nsq{kc}"])
        mm(bank(psb, ntok), [(ones_b, sq_b[:, kc, 0:ntok]) for kc in range(KC)],
           [f"nsq{kc}" for kc in range(KC)] + ["ones_b"], pk(psb))
        act(rstd_f[:, 0:ntok], bank(psb, ntok), AF.Sqrt, pk(psb), [rkey], scale=inv_n, bias=sm_f[:, 0:1])
        pg.op("dve", lambda e: e.reciprocal(rstd_f[:, 0:ntok], rstd_f[:, 0:ntok]), [rkey], [rkey])

    pg.op("pool", lambda e: e.memset(sm_f[:, 0:1], EPS), [], ["eps"])
    pg.barrier()

    AT_OFF = 0
    OT0_OFF = 64 * KB

    def phase_x(u):
        aT = r3(A.bf16(AT_OFF, 8 * HTOK), 8)
        NXB = 3
        xs_f = [A.f32(141 * KB, 1024), A.f32(145 * KB, 1024), A.f32(191 * KB, 1024)]
        xs_b = [A.bf16(149 * KB, 1024), A.bf16(151 * KB, 1024), A.bf16(104 * KB, 1024)]
        junk = A.bf16(153 * KB, 1024)
        g = gain(0, 0)
        for t in range(HTOK // 128):
            b = t % NXB
            ss, sr, rs = sm_f[:, 2 + b:3 + b], sm_f[:, 8 + b:9 + b], sm_f[:, 12 + b:13 + b]
            ld(xs_f[b], xu[u, t * 128:(t + 1) * 128, :], [], [f"xsf{b}"])
            act(junk, xs_f[b], AF.Square, [f"xsf{b}"], ["junk", f"ss{b}"], accum_out=ss)
            act(sr, ss, AF.Sqrt, [f"ss{b}", "eps"], [f"sr{b}"], scale=1.0 / D, bias=sm_f[:, 0:1])
            pg.op("dve", lambda e, rs=rs, sr=sr: e.reciprocal(rs, sr), [f"sr{b}"], [f"rs{b}"])
            ts("dve", xs_b[b], xs_f[b], rs, None, ALU.mult, ALU.bypass, [f"xsf{b}", f"rs{b}"], [f"xsb{b}"])
            for kc in range(KC):
                tr(bank16(b, 128, kc * 128), xs_b[b][:, kc * 128:(kc + 1) * 128], ident_b,
                   [f"xsb{b}", "ident_b"], pk(b))
            gb = g.unsqueeze(2).to_broadcast([128, 8, 128])
            tt("dve", aT[:, :, t * 128:(t + 1) * 128], r3(bank16(b, 1024), 8), gb, ALU.mult,
               list(pk(b)) + ["gains"], [f"aT{t // 4}"])

    def phase_na(u):
        aT = r3(A.bf16(AT_OFF, 8 * HTOK), 8)
        oT = r3(A.bf16(OT0_OFF, 8 * TOK), 8)
        qT = [A.bf16(40 * KB + b * 4 * KB, TOK) for b in range(2)]
        kT = [A.bf16(48 * KB + b * 5 * KB, HTOK) for b in range(2)]
        Vg = [r3(A.bf16(170 * KB + k * 5248, 20 * 130), 20) for k in range(4)]
        Wvg = r3(A.bf16(96 * KB, 8 * 512), 8)
        Wp = [[r3(A.bf16(107 * KB + (b * 3 + k) * 2 * KB, 1024), 8) for k in range(3)] for b in range(2)]
        Ut = [A.f32(119 * KB + b * 7 * KB, 2 * ND * 64) for b in range(2)]
        Ui = [A.f32(156 * KB + b * 7 * KB, 2 * ND * 64) for b in range(2)]
        tmp = [A.f32(133 * KB + b * 4 * KB, 1024) for b in range(2)]
        PT = [A.bf16(60 * KB + b * 2 * KB, 1024) for b in range(2)]
        otok = [A.bf16(58 * KB + b * 256, 128) for b in range(2)]
        rcp = A.f32(59 * KB, 8)
        ld(rv_f[:, 3:51], rvd[u], [], ["rvu"])
        for k in range(4):
            pg.op("pool", lambda e, k=k: e.memset(Vg[k][:, :, 64:65], 1.0), [], [f"V{k}"])
            pg.op("pool", lambda e, k=k: e.memset(Vg[k][:, :, 129:130], 1.0), [], [f"V{k}"])
        aT_keys = [f"aT{i}" for i in range(5)]

        def load_w(c):
            b = c % 2
            for k, nm in enumerate(("na_q", "na_k")):
                ld(Wp[b][k], r3(wb[nm][c], 8), [wkey(nm, c)], [f"W{b}_{k}"])
            ld(Ut[b], utab[c], [], [f"U{b}"])
            ld(Ui[b], utab2[c], [], [f"Ui{b}"])

        load_w(0)
        for c in range(8):
            b = c % 2
            if c + 1 < 8:
                load_w(c + 1)
            for t4 in range(4):
                pb = t4 % 2
                mm(bank(pb), [(Wp[b][0][:, kc, :], aT[:, kc, 256 + t4 * 512: 256 + (t4 + 1) * 512]) for kc in range(KC)],
                   [f"W{b}_0"] + aT_keys, pk(pb))
                act(qT[b][:, t4 * 512:(t4 + 1) * 512], bank(pb), AF.Copy, pk(pb), [f"q{b}"], scale=0.125)
            for t5 in range(5):
                pb = t5 % 2
                mm(bank(pb), [(Wp[b][1][:, kc, :], aT[:, kc, t5 * 512:(t5 + 1) * 512]) for kc in range(KC)],
                   [f"W{b}_1"] + aT_keys, pk(pb))
                cp("act" if t5 % 2 else "dve", kT[b][:, t5 * 512:(t5 + 1) * 512], bank(pb), pk(pb), [f"k{b}"])
            if c % 4 == 0:
                ld(Wvg, r3(wb["na_v"][c // 4], 8), [wkey("na_v", c // 4)], ["Wvg"])
                vall = A.bf16(170 * KB, 4 * 2624).rearrange("p (k x) -> p k x", k=4)
                for t in range(20):
                    pb = 2 + t % 2
                    mm(bank(pb), [(aT[:, kc, t * 128:(t + 1) * 128], Wvg[:, kc, :]) for kc in range(KC)],
                       ["Wvg"] + aT_keys, pk(pb))
                    vout = vall[:, :, t * 130:(t + 1) * 130].rearrange("p k (h e) -> p k h e", h=2)[:, :, :, 0:64]
                    cp("dve" if t % 2 else "act", vout, bank(pb).rearrange("p (k h e) -> p k h e", k=4, h=2),
                       pk(pb), [f"V{k}" for k in range(4)])
            def row_qk(i, b=b):
                l, tiles, masks = na_row_spec(i)
                sb = 4 + 2 * (i % 2)

                def qk(e, b=b, i=i, tiles=tiles, sb=sb):
                    ins = None
                    for hd in range(2):
                        for j, t in enumerate(tiles):
                            ins = e.matmul(bank(sb + hd, 64, j * 64),
                                           kT[b][hd * 64:(hd + 1) * 64, t * 128:(t + 1) * 128],
                                           qT[b][hd * 64:(hd + 1) * 64, i * 64:(i + 1) * 64],
                                           start=True, stop=True)
                    return ins
                pg.op("pe", qk, [f"q{b}", f"k{b}"], pk(sb, sb + 1))

            def row_rest(i, b=b, c=c):
                l, tiles, masks = na_row_spec(i)
                nt = len(tiles)
                sb = 4 + 2 * (i % 2)
                pob = 2 + i % 2
                tb = i % 2
                skeys = pk(sb, sb + 1)
                d0 = 2 * tiles[0] - l + 7
                s_in = PS[:, sb * 512: sb * 512 + 1024].rearrange("p (h j q) -> p h j q", h=2, j=8)[:, :, 0:nt, :]
                edge = masks[0] >= 3
                Utab_ = Ut[b] if edge else Ui[b]
                ukey = f"U{b}" if edge else f"Ui{b}"
                u_in = Utab_.rearrange("p (h d q) -> p h d q", h=2, d=ND)[:, :, d0:d0 + 2 * nt - 1:2, :]
                t4v = tmp[tb].rearrange("p (h j q) -> p h j q", h=2, j=8)
                tt("dve", t4v[:, :, 0:nt, :], s_in, u_in, ALU.add, list(skeys) + [ukey], [f"tmp{tb}"])
                p_out = PT[tb].rearrange("p (h j q) -> p h j q", h=2, j=8)
                if edge:
                    for j in range(nt):
                        mcol = masks[j]
                        act(p_out[:, :, j:j + 1, :], t4v[:, :, j:j + 1, :],
                            AF.Exp, [f"tmp{tb}", "rvu", "rvc"], [f"PT{tb}"], bias=rv_f[:, mcol:mcol + 1])
                else:
                    act(p_out[:, :, 0:nt, :], t4v[:, :, 0:nt, :], AF.Exp, [f"tmp{tb}"], [f"PT{tb}"])

            def row_pv(i, b=b, c=c):
                l, tiles, masks = na_row_spec(i)
                nt = len(tiles)
                pob = 2 + i % 2
                tb = i % 2

                def pv(e, c=c, tiles=tiles, pob=pob, tb=tb, nt=nt):
                    ins = None
                    for hd in range(2):
                        for j, t in enumerate(tiles):
                            ins = e.matmul(bank(pob, 65, hd * 65)[0:64, :],
                                           PT[tb].rearrange("p (h j q) -> p h j q", h=2, j=8)[:, hd, j, :],
                                           Vg[c % 4][:, t, hd * 65:(hd + 1) * 65],
                                           start=(j == 0), stop=(j == nt - 1))
                    return ins
                pg.op("pe", pv, [f"PT{tb}", f"V{c % 4}"], pk(pob))

            def row_fin(i, b=b, c=c):
                pob = 2 + i % 2
                half = i % 2
                ob = (i // 2) % 2
                po3 = r3(bank(pob, 130), 2)[0:64]
                pg.op("dve", lambda e, po3=po3, half=half: e.reciprocal(rcp[0:64, half * 2:half * 2 + 2], po3[:, :, 64]),
                      pk(pob), [f"rcp{half}"])
                rb = rcp[0:64, half * 2:half * 2 + 2].unsqueeze(2).to_broadcast([64, 2, 64])
                tt("dve", r3(otok[ob], 2)[half * 64:(half + 1) * 64], po3[:, :, 0:64], rb, ALU.mult,
                   list(pk(pob)) + [f"rcp{half}"], [f"otok{ob}"])
                if half == 1:
                    tpb = ob
                    tr(bank16(tpb, 128), otok[ob], ident_b, [f"otok{ob}", "ident_b"], pk(tpb))
                    cp("act", oT[:, c, (i - 1) * 64:(i + 1) * 64], bank16(tpb, 128), pk(tpb), [f"oT{c}"])

            NROWS = cfg.get("na_rows", 32)
            for n in range(-2, NROWS + 1):
                if 0 <= n + 2 < NROWS:
                    row_qk(n + 2)
                if 0 <= n + 1 < NROWS:
                    row_rest(n + 1)
                if 0 <= n < NROWS:
                    row_pv(n)
                if 0 <= n - 1 < NROWS:
                    row_fin(n - 1)

    def proj8(wname, src3, src_keys, m3, ntok, woff, kcn=8, evac_scale=None, mtag="m"):
        Wd = [r3(A.bf16(woff + b * (kcn * 256), kcn * 128), kcn) for b in range(2)]
        ld(Wd[0], r3(wb[wname][0], kcn), [wkey(wname, 0)], ["Wd0"])
        n = 0
        for oc in range(8):
            b = oc % 2
            if oc + 1 < 8:
                ld(Wd[1 - b], r3(wb[wname][oc + 1], kcn), [wkey(wname, oc + 1)], [f"Wd{1 - b}"])
            for t4 in range(ntok // 512):
                pb = n % 4
                n += 1
                mm(bank(pb), [(Wd[b][:, k, :], src3[:, k, t4 * 512:(t4 + 1) * 512]) for k in range(kcn)],
                   [f"Wd{b}"] + src_keys, pk(pb))
                cp("act", m3[:, oc, t4 * 512:(t4 + 1) * 512], bank(pb), pk(pb), [f"{mtag}{oc}_{t4}"])

    def postnorm_residual(u, m3, ntok, tok0, gvec, sq_off, misc_off, x_init):
        hT = hT_view()
        sq_b = r3(A.bf16(sq_off, 8 * 512), 8)
        rstd = [A.f32(misc_off + b * 2 * KB, 512) for b in range(2)]
        tmpf = [A.f32(misc_off + 4 * KB + b * 2 * KB, 512) for b in range(2)]
        xst = [A.f32(misc_off + 8 * KB + b * 4 * KB, 1024) for b in range(3)] if x_init else None
        for t4 in range(ntok // 512):
            gt = (tok0 // 512) + t4
            b = t4 % 2
            norm_stats(lambda kc: (m3[:, kc, t4 * 512:(t4 + 1) * 512], [f"m{kc}_{t4}"]), 512, sq_b, rstd[b],
                       4 + b, 1.0 / D, f"rstd{b}")
            if x_init:
                for s in range(4):
                    xb = (t4 * 4 + s) % 3
                    tok = tok0 + t4 * 512 + s * 128
                    ld(xst[xb], xu[u, 256 + tok: 256 + tok + 128, :], [], [f"xst{xb}"])
                    for kc in range(KC):
                        tr(bank(6 + kc // 4, 128, (kc % 4) * 128), xst[xb][:, kc * 128:(kc + 1) * 128], ident_f,
                           [f"xst{xb}", "cst"], pk(6 + kc // 4))
                    cp("act", hT[:, :, tok:tok + 128],
                       PS[:, 6 * 512: 8 * 512].rearrange("p (k t) -> p k t", k=8), pk(6, 7), [f"h{gt}"])
            for kc in range(KC):
                tb = kc % 2
                stt(tmpf[tb], m3[:, kc, t4 * 512:(t4 + 1) * 512], gvec[:, kc:kc + 1], rstd[b], ALU.mult, ALU.mult,
                    [f"m{kc}_{t4}", f"rstd{b}", "gains"], [f"pnt{tb}"])
                tt("pool" if kc % 2 else "dve", hT[:, kc, tok0 + t4 * 512: tok0 + (t4 + 1) * 512],
                   hT[:, kc, tok0 + t4 * 512: tok0 + (t4 + 1) * 512], tmpf[tb], ALU.add,
                   [f"pnt{tb}", f"h{gt}"], [f"h{gt}"])

    def rmsnorm_fm(dst3, ntok, tok0, gvec, sq_off, misc_off, dtag):
        hT = hT_view()
        sq_b = r3(A.bf16(sq_off, 8 * 512), 8)
        rstd = [A.f32(misc_off + b * 2 * KB, 512) for b in range(2)]
        for t4 in range(ntok // 512):
            gt = (tok0 // 512) + t4
            b = t4 % 2
            norm_stats(lambda kc: (hT[:, kc, tok0 + t4 * 512: tok0 + (t4 + 1) * 512], [f"h{gt}"]), 512, sq_b,
                       rstd[b], 6 + b, 1.0 / D, f"rstd{b}")
            for kc in range(KC):
                src = hT[:, kc, tok0 + t4 * 512: tok0 + (t4 + 1) * 512]
                if gvec is not None:
                    stt(dst3[:, kc, t4 * 512:(t4 + 1) * 512], src, gvec[:, kc:kc + 1], rstd[b], ALU.mult, ALU.mult,
                        [f"h{gt}", f"rstd{b}", "gains"], [f"{dtag}{t4}"])
                else:
                    tt("dve", dst3[:, kc, t4 * 512:(t4 + 1) * 512], src, rstd[b], ALU.mult,
                       [f"h{gt}", f"rstd{b}"], [f"{dtag}{t4}"])

    def phase_wo_pn(u, L, wname, oT_off, m_off, w_off, sq_off, misc_off, x_init):
        oT = r3(A.bf16(oT_off, 8 * TOK), 8)
        m3 = r3(A.f32(m_off, 8 * 1024), 8)
        for half in range(2):
            src = oT[:, :, half * 1024:(half + 1) * 1024]
            proj8(wname, src, [f"oT{c}" for c in range(8)], m3, 1024, w_off)
            postnorm_residual(u, m3, 1024, half * 1024, gain(1, L), sq_off, misc_off, x_init)

    def phase_ffn(u, L):
        hT = hT_view()
        a2 = r3(A.bf16(64 * KB, 8 * 1024), 8)
        actT = r3(A.bf16(80 * KB, NF * 1024), NF)
        m3 = r3(A.f32(124 * KB, 8 * 1024), 8)
        Wgu = [[r3(A.bf16(156 * KB + (b * 2 + k) * 2 * KB, 1024), 8) for k in range(2)] for b in range(2)]
        sg = [A.f32(175 * KB + b * 2 * KB, 512) for b in range(2)]
        a2k = ["a2_0", "a2_1"]
        rmsnorm_fm(a2, 1024, 0, gain(2, L), 179 * KB, 187 * KB, "a2_")
        for half in range(2):

            def load_gu(f):
                b = f % 2
                ld(Wgu[b][0], r3(wb[f"fg{L}"][f], 8), [wkey(f"fg{L}", f)], [f"Wg{b}"])
                ld(Wgu[b][1], r3(wb[f"fu{L}"][f], 8), [wkey(f"fu{L}", f)], [f"Wu{b}"])
            load_gu(0)
            n = 0
            for f in range(NF):
                b = f % 2
                if f + 1 < NF:
                    load_gu(f + 1)
                for t2 in range(2):
                    pg_b = n % 2
                    pu_b = 2 + n % 2
                    sb_ = n % 2
                    n += 1
                    rhs = lambda kc: a2[:, kc, t2 * 512:(t2 + 1) * 512]
                    mm(bank(pg_b), [(Wgu[b][0][:, kc, :], rhs(kc)) for kc in range(KC)], [f"Wg{b}"] + a2k, pk(pg_b))
                    mm(bank(pu_b), [(Wgu[b][1][:, kc, :], rhs(kc)) for kc in range(KC)], [f"Wu{b}"] + a2k, pk(pu_b))
                    act(sg[sb_], bank(pg_b), AF.Silu, pk(pg_b), [f"sg{sb_}"])
                    tt("dve", actT[:, f, t2 * 512:(t2 + 1) * 512], sg[sb_], bank(pu_b), ALU.mult,
                       [f"sg{sb_}"] + list(pk(pu_b)), [f"act{f}"])
            proj8(f"fd{L}", actT, [f"act{f}" for f in range(NF)], m3, 1024, 164 * KB, kcn=NF)
            if half == 0:
                rmsnorm_fm(a2, 1024, 1024, gain(2, L), 179 * KB, 187 * KB, "a2_")
            postnorm_residual(u, m3, 1024, half * 1024, gain(3, L), 179 * KB, 187 * KB, False)

    def phase_ple(u, L, psrc):
        hT = hT_view()
        rT = r3(A.bf16(64 * KB, 8 * TOK), 8)
        pT = r3(A.bf16(96 * KB, 2 * TOK), 2)
        Wg = [r3(A.bf16(104 * KB + b * 2 * KB, 1024), 8) for b in range(2)]
        Wpp = [r3(A.bf16(108 * KB + b * 512, 256), 2) for b in range(2)]
        pst = [A.f32(109 * KB + b * KB, 256) for b in range(2)]
        sig = [A.f32(111 * KB + b * 2 * KB, 512) for b in range(2)]
        tmpf = [A.f32(115 * KB + b * 2 * KB, 512) for b in range(2)]
        rmsnorm_fm(rT, TOK, 0, None, 119 * KB, 127 * KB, "rT_")
        for t in range(16):
            b = t % 2
            ld(pst[b], psrc[t * 128:(t + 1) * 128, :], [], [f"pst{b}"])
            for j in range(2):
                tr(bank(4 + b, 128, j * 128), pst[b][:, j * 128:(j + 1) * 128], ident_f, [f"pst{b}", "cst"], pk(4 + b))
            cp("act", pT[:, :, t * 128:(t + 1) * 128], r3(bank(4 + b, 256), 2), pk(4 + b), [f"pT{t // 4}"])
        ld(Wg[0], r3(wb[f"pg{L}"][0], 8), [wkey(f"pg{L}", 0)], ["Wpg0"])
        ld(Wpp[0], r3(wb[f"pp{L}"][0], 2), [wkey(f"pp{L}", 0)], ["Wpp0"])
        n = 0
        for oc in range(8):
            b = oc % 2
            if oc + 1 < 8:
                ld(Wg[1 - b], r3(wb[f"pg{L}"][oc + 1], 8), [wkey(f"pg{L}", oc + 1)], [f"Wpg{1 - b}"])
                ld(Wpp[1 - b], r3(wb[f"pp{L}"][oc + 1], 2), [wkey(f"pp{L}", oc + 1)], [f"Wpp{1 - b}"])
            for t4 in range(4):
                gb_ = n % 2
                pb_ = 2 + n % 2
                s_ = n % 2
                n += 1
                mm(bank(gb_), [(Wg[b][:, kc, :], rT[:, kc, t4 * 512:(t4 + 1) * 512]) for kc in range(KC)],
                   [f"Wpg{b}", f"rT_{t4}"], pk(gb_))
                mm(bank(pb_), [(Wpp[b][:, j, :], pT[:, j, t4 * 512:(t4 + 1) * 512]) for j in range(2)],
                   [f"Wpp{b}", f"pT{t4}"], pk(pb_))
                act(sig[s_], bank(gb_), AF.Sigmoid, pk(gb_), [f"sig{s_}"])
                tt("dve", tmpf[s_], sig[s_], bank(pb_), ALU.mult, [f"sig{s_}"] + list(pk(pb_)), [f"plt{s_}"])
                tt("pool", hT[:, oc, t4 * 512:(t4 + 1) * 512], hT[:, oc, t4 * 512:(t4 + 1) * 512], tmpf[s_], ALU.add,
                   [f"plt{s_}", f"h{t4}"], [f"h{t4}"])

    QT_OFF = 96 * KB

    def phase_kv(u, with_q):
        a1 = r3(A.bf16(64 * KB, 8 * TOK), 8)
        qT = r3(A.bf16(QT_OFF, 8 * TOK), 8)
        kv_f = [A.f32(128 * KB + i * 2 * KB, 512) for i in range(6)]
        knb = [A.bf16(140 * KB + b * KB, 512) for b in range(2)]
        sqb = [A.bf16(142 * KB + b * KB, 512) for b in range(2)]
        kTg = [A.bf16(144 * KB + b * 4 * KB, TOK) for b in range(2)]
        Vst = r4(A.bf16(152 * KB, 4 * 16 * 128), 4, 16)
        Wk = [r3(A.bf16(168 * KB + b * 2 * KB, 1024), 8) for b in range(2)]
        Wv = r3(A.bf16(172 * KB, 8 * 512), 8)
        cs = [[A.f32(180 * KB + (b * 2 + k) * 2 * KB, 512) for k in range(2)] for b in range(2)]
        rmsnorm_fm(a1, TOK, 0, gain(0, 1), 188 * KB, 128 * KB + 8 * KB, "a1_")
        pg.barrier()
        a1k = [f"a1_{t}" for t in range(4)]
        heads = [("gk", g, False) for g in range(4)]
        if with_q:
            heads += [("gq", h, True) for h in range(8)]
        kraw = [kv_f[0], kv_f[1]]
        kn = [kv_f[2], kv_f[3]]
        rstd2 = [kv_f[4], kv_f[5]]
        t1 = [A.f32(188 * KB + b * 2 * KB, 512) for b in range(2)]
        t2 = [A.f32(192 * KB + b * 2 * KB, 512) for b in range(2)]
        work = [(hi, t4) for hi in range(len(heads)) for t4 in range(4)]

        def stage_a(n):
            hi, t4 = work[n]
            wn, hidx, isq = heads[hi]
            b = hi % 2
            cb = n % 2
            if t4 == 0:
                ld(Wk[b], r3(wb[wn][hidx], 8), [wkey(wn, hidx)], [f"Wk{b}"])
            gcol = gqk_f[:, 2:3] if isq else gqk_f[:, 1:2]
            ld(cs[cb][0], ropec[u, :, t4 * 512:(t4 + 1) * 512], [], [f"cos{cb}"])
            ld(cs[cb][1], ropes[u, :, t4 * 512:(t4 + 1) * 512], [], [f"sin{cb}"])
            mm(bank(cb), [(Wk[b][:, kc, :], a1[:, kc, t4 * 512:(t4 + 1) * 512]) for kc in range(KC)],
               [f"Wk{b}"] + a1k, pk(cb))
            cp("act", kraw[cb], bank(cb), pk(cb), [f"kraw{cb}"])
            act(sqb[cb], bank(cb), AF.Square, pk(cb), [f"sqb{cb}"])
            mm(bank(2 + cb), [(ones_b, sqb[cb])], [f"sqb{cb}", "ones_b"], pk(2 + cb))
            act(rstd2[cb], bank(2 + cb), AF.Sqrt, pk(2 + cb), [f"krs{cb}"], scale=1.0 / 128, bias=sm_f[:, 0:1])
            pg.op("dve", lambda e, r=rstd2[cb]: e.reciprocal(r, r), [f"krs{cb}"], [f"krs{cb}"])
            stt(kn[cb], kraw[cb], gcol, rstd2[cb], ALU.mult, ALU.mult, [f"kraw{cb}", f"krs{cb}", "gqk", "gqs"], [f"kn{cb}"])
            cp("pool", knb[cb], kn[cb], [f"kn{cb}"], [f"knb{cb}"])

        def stage_b(n):
            hi, t4 = work[n]
            wn, hidx, isq = heads[hi]
            b = hi % 2
            cb = n % 2
            mm(bank(4 + cb), [(rm_b, knb[cb])], [f"knb{cb}", "rm_b"], pk(4 + cb))
            tt("pool", t1[cb], kn[cb], cs[cb][0], ALU.mult, [f"kn{cb}", f"cos{cb}"], [f"t1{cb}"])
            tt("dve", t2[cb], bank(4 + cb), cs[cb][1], ALU.mult, list(pk(4 + cb)) + [f"sin{cb}"], [f"t2{cb}"])
            if isq:
                tt("dve", qT[:, hidx, t4 * 512:(t4 + 1) * 512], t1[cb], t2[cb], ALU.add, [f"t1{cb}", f"t2{cb}"], [f"qT{hidx}"])
            else:
                tt("dve", kTg[b][:, t4 * 512:(t4 + 1) * 512], t1[cb], t2[cb], ALU.add, [f"t1{cb}", f"t2{cb}"], [f"kTg{b}"])
                if t4 == 3:
                    ld(kvs[u, hidx, 0], kTg[b], [f"kTg{b}"], [f"kvs{u}_{hidx}_0"], q="pool")

        stage_a(0)
        for n in range(len(work)):
            if n + 1 < len(work):
                stage_a(n + 1)
            stage_b(n)
        ld(Wv, r3(wb["gv"][0], 8), [wkey("gv", 0)], ["Wv"])
        for t in range(16):
            pb = 6 + t % 2
            mm(bank(pb), [(a1[:, kc, t * 128:(t + 1) * 128], Wv[:, kc, :]) for kc in range(KC)], ["Wv"] + a1k, pk(pb))
            cp("act" if t % 2 else "dve", Vst[:, :, t, :], r3(bank(pb), 4), pk(pb), ["Vst"])
        for g in range(4):
            ld(kvs[u, g, 1], Vst[:, g].rearrange("p t d -> p (t d)"), ["Vst"], [f"kvs{u}_{g}_1"], q="pool")

    OT1_OFF = 64 * KB

    def phase_attn(u, key_units):
        qT = r3(A.bf16(QT_OFF, 8 * TOK), 8)
        oT = r3(A.bf16(OT1_OFF, 8 * TOK), 8)
        ring = [(A.bf16(128 * KB + b * 8 * KB, TOK), r3(A.bf16(128 * KB + b * 8 * KB + 4 * KB, TOK), 16))
                for b in range(3)]
        NPT = 6
        PT = [A.bf16(152 * KB + b * KB, 512) for b in range(NPT)]
        rc = [A.f32(164 * KB + b * 2 * KB, 512) for b in range(2)]
        seq = [(g, q4, ku) for g in range(4) for q4 in range(4) for ku in key_units]

        def load(idx):
            g, q4, ku = seq[idx]
            b = idx % 3
            ld(ring[b][0], kvs[ku, g, 0], [f"kvs{ku}_{g}_0"], [f"rk{b}"])
            ld(ring[b][1].rearrange("p t d -> p (t d)"), kvs[ku, g, 1], [f"kvs{ku}_{g}_1"], [f"rv{b}"])
        its = [(idx, kt, hh) for idx in range(len(seq)) for kt in range(16) for hh in range(2)]

        def emit_s(n):
            idx, kt, hh = its[n]
            g, q4, ku = seq[idx]
            b = idx % 3
            h = 2 * g + hh
            sbk = 4 + n % 4
            mm(bank(sbk), [(ring[b][0][:, kt * 128:(kt + 1) * 128], qT[:, h, q4 * 512:(q4 + 1) * 512])],
               [f"rk{b}", f"qT{h}"], pk(sbk))
            act(PT[n % NPT], bank(sbk), AF.Exp, pk(sbk), [f"PTa{n % NPT}"])

        PP = [[A.bf16(160 * KB + (hh * 2 + m) * KB, 512) for m in range(2)] for hh in range(2)]
        osb = [A.f32(168 * KB + b * 2 * KB, 512) for b in range(2)]
        pending = []

        def flush(upto):
            while pending and pending[0][0] <= upto:
                _, fn_ = pending.pop(0)
                fn_()

        def emit_pv(n):
            idx, kt, hh = its[n]
            g, q4, ku = seq[idx]
            b = idx % 3
            first = (ku == key_units[0])
            last = (ku == key_units[-1])
            st = first and kt == 0
            sp_ = last and kt == 15
            pt = PT[n % NPT]
            Vt = ring[b][1]
            flush(n)
            pg.op("pe", lambda e, hh=hh, pt=pt, Vt=Vt, kt=kt, st=st, sp_=sp_:
                  e.matmul(bank(hh), Vt[:, kt, :], pt, start=st, stop=sp_),
                  [f"PTa{n % NPT}", f"rv{b}"], pk(hh))
            if kt % 2 == 1:
                m_ = (kt // 2) % 2
                pp = PP[hh][m_]
                tt("pool" if hh == 0 else "dve", pp, PT[n % NPT], PT[(n - 2) % NPT], ALU.add,
                   [f"PTa{n % NPT}", f"PTa{(n - 2) % NPT}"], [f"PP{hh}{m_}"])
                st2 = first and kt == 1
                sp2 = last and kt == 15

                def ones_mm(hh=hh, pp=pp, st2=st2, sp2=sp2, m_=m_):
                    pg.op("pe", lambda e: e.matmul(bank(2 + hh), ones_b, pp, start=st2, stop=sp2),
                          [f"PP{hh}{m_}", "ones_b"], pk(2 + hh))
                pending.append((n + 3, ones_mm))
            if sp_:
                flush(1 << 60)
                h = 2 * g + hh
                cp("act", osb[hh], bank(hh), pk(hh), [f"osb{hh}"])
                cp("act", rc[hh], bank(2 + hh), pk(2 + hh), [f"rc{hh}"])
                pg.op("dve", lambda e, hh=hh: e.reciprocal(rc[hh], rc[hh]), [f"rc{hh}"], [f"rc{hh}"])
                tt("dve", oT[:, h, q4 * 512:(q4 + 1) * 512], osb[hh], rc[hh], ALU.mult,
                   [f"osb{hh}", f"rc{hh}"], [f"oT{h}"])
            if kt == 0 and hh == 0 and idx + 2 < len(seq):
                load(idx + 2)

        load(0)
        if len(seq) > 1:
            load(1)
        AHEAD = cfg.get("ahead", 2)
        for n in range(min(AHEAD, len(its))):
            emit_s(n)
        for n in range(len(its)):
            if n + AHEAD < len(its):
                emit_s(n + AHEAD)
            emit_pv(n)

    def phase_out(dst):
        hT = hT_view()
        ost = [A.f32(64 * KB + b * 4 * KB, 1024) for b in range(2)]
        for t in range(16):
            b = t % 2
            for kc in range(KC):
                tr(bank(2 * b + kc // 4, 128, (kc % 4) * 128), hT[:, kc, t * 128:(t + 1) * 128], ident_f,
                   [f"h{t // 4}", "cst"], pk(2 * b + kc // 4))
            cp("act" if b else "dve", ost[b], PS[:, 2 * b * 512: (2 * b + 2) * 512], pk(2 * b, 2 * b + 1), [f"ost{b}"])
            ld(dst[t * 128:(t + 1) * 128, :], ost[b], [f"ost{b}"], [f"y{t}"], q="pool")

    def layer0(u):
        phase_x(u)
        if cfg.get("stop") == "x":
            pg.barrier()
            return
        phase_na(u)
        pg.barrier()
        if cfg.get("stop") == "na":
            oT = r3(A.bf16(OT0_OFF, 8 * TOK), 8)
            stg = [A.f32(100 * KB + b * 2 * KB, 512) for b in range(2)]
            ypv = yp.rearrange("(k p a) f -> k p (a f)", k=8, p=128)
            n = 0
            for kc in range(8):
                for t4 in range(4):
                    b = n % 2
                    n += 1
                    cp("dve", stg[b], oT[:, kc, t4 * 512:(t4 + 1) * 512], [], [f"dstg{b}"])
                    ld(ypv[kc][:, t4 * 512:(t4 + 1) * 512], stg[b], [f"dstg{b}"], [f"dy{n}"], q="pool")
            pg.barrier()
            return
        phase_wo_pn(u, 0, "na_o", OT0_OFF, 96 * KB, 128 * KB, 132 * KB, 140 * KB, True)
        pg.barrier()
        if cfg.get("stop") == "wo":
            return
        phase_ffn(u, 0)
        pg.barrier()
        if cfg.get("stop") == "ffn":
            return
        phase_ple(u, 0, p0u[u])
        pg.barrier()

    def layer1_rest(u, pidx, dst):
        phase_wo_pn(u, 1, "go", OT1_OFF, 96 * KB, 128 * KB, 132 * KB, 140 * KB, False)
        pg.barrier()
        phase_ffn(u, 1)
        pg.barrier()
        phase_ple(u, 1, p1u[pidx])
        pg.barrier()
        phase_out(dst)
        pg.barrier()

    for u in units_l0:
        layer0(u)
        if dbg == "l0" and u == units_l0[-1]:
            if cfg.get("stop") != "na":
                phase_out(yp)
            pg.barrier()
            break
        if not do_l1:
            continue
        is_q = u in (U_OWN, U_PROMPT)
        phase_kv(u, is_q)
        pg.barrier()
        if u == U_OWN:
            phase_attn(u, list(range(8)))
            pg.barrier()
            layer1_rest(u, 0, ys)
        elif u == U_PROMPT:
            phase_attn(u, [U_PROMPT])
            pg.barrier()
            layer1_rest(u, 1, yp)
    pg.barrier()

    for s in pg.sem_names:
        pg.handles[s] = nc.alloc_semaphore(f"s_{s}")
    with nc.Block() as block:
        pg.emit(block)
    return nc, pg


def _na_static():
    qc = np.arange(64)[:, None]
    kc = np.arange(64)[None, :]
    wstart = np.clip(qc - 8, 0, 48)
    colmask = (kc >= wstart) & (kc < wstart + 16)
    return colmask


def _utab(rpb, interior=False):
    colmask = _na_static()
    out = np.full((8, 128, 2, ND, 64), NEG, np.float32)
    kp = np.arange(128)
    kcol = kp % 64
    khalf = kp // 64
    qcs = np.arange(64)
    ci = np.clip(kcol[:, None] - qcs[None, :] + 15, 0, 30)
    cm = colmask.T[kcol, :]
    for c in range(8):
        for hd in range(2):
            h = 2 * c + hd
            for di in range(ND):
                d = di - 7
                ri = d + 7 + khalf
                ok = (ri >= 0) & (ri <= 14)
                ric = np.clip(ri, 0, 14)
                vals = rpb[h][ric[:, None], ci]
                if interior:
                    ok = ok & (ri >= 3) & (ri <= 10)
                out[c, :, hd, di, :] = np.where(cm & ok[:, None], vals, np.float32(NEG))
    return out.reshape(8, 128, 2 * ND * 64)


def _rv_masks(g0, R):
    out = np.zeros((128, 48), np.float32)
    for e, l in enumerate(list(range(4, 8)) + list(range(32, 36))):
        g = g0 + l
        rs = min(max(g - 4, 0), R - 8)
        tiles = range(0, 6) if l < 8 else range(14, 20)
        for j, t in enumerate(tiles):
            for half in range(2):
                gk = g0 + 2 * t + half
                valid = (rs <= gk < rs + 8)
                if not valid:
                    out[half * 64:(half + 1) * 64, e * 6 + j] = NEG
    return out


def _rope_tables(tok0):
    t = np.arange(tok0, tok0 + TOK)
    row = (t // 64).astype(np.float32)
    col = (t % 64).astype(np.float32)
    inv = (np.float32(10000.0) ** (-np.arange(0, 64, 2, dtype=np.float32) / np.float32(64))).astype(np.float32)
    ang = np.zeros((128, TOK), np.float32)
    for d in range(128):
        f = d % 32
        pos = row if d < 64 else col
        ang[d] = (pos * inv[f]).astype(np.float32)
    return np.cos(ang).astype(np.float32), np.sin(ang).astype(np.float32)


def _consts():
    c = np.zeros((128, 387), np.float32)
    c[:, 0:128] = np.eye(128, dtype=np.float32)
    c[:, 128:256] = 1.0
    rm = np.zeros((128, 128), np.float32)
    for m in range(128):
        if (m % 64) < 32:
            rm[m + 32, m] = -1.0
        else:
            rm[m - 32, m] = 1.0
    c[:, 256:384] = rm
    c[0:64, 385] = NEG
    c[64:128, 386] = NEG
    return c


_CACHE = {}


def _prep_inputs(inp):
    xs = inp["x_sample"][0]
    xpad = np.concatenate([np.zeros((256, D), np.float32), xs, np.zeros((256, D), np.float32)], axis=0)
    zeros_h = np.zeros((256, D), np.float32)
    g_all = np.stack([inp["mix_pre_norm"], inp["mix_post_norm"], inp["ffn_pre_norm"], inp["ffn_post_norm"]], 0)
    gains = np.ascontiguousarray(g_all.reshape(4, 2, 8, 128).transpose(3, 0, 1, 2).reshape(128, 64)).astype(np.float32)
    gqk = np.ascontiguousarray(np.stack([inp["gqa_q_norm"][0], inp["gqa_k_norm"][0]], 1)).astype(np.float32)
    utab = _utab(np.asarray(inp["na_rpb"][0], np.float32))
    utab2 = _utab(np.asarray(inp["na_rpb"][0], np.float32), interior=True)
    cst = _consts()
    rope_p = _rope_tables(0)
    maps = []
    for c in range(NCORES):
        chunks = [(c + 1 + u) % 8 for u in range(7)] + [c]
        xu = np.empty((NUNITS, HTOK, D), np.float32)
        p0 = np.empty((NUNITS, TOK, 256), np.float32)
        rv = np.empty((NUNITS, 128, 48), np.float32)
        rc = np.empty((NUNITS, 128, TOK), np.float32)
        rs = np.empty((NUNITS, 128, TOK), np.float32)
        for u, gj in enumerate(chunks):
            xu[u] = xpad[gj * TOK: gj * TOK + HTOK]
            p0[u] = inp["p_sample"][0, 0, gj * TOK:(gj + 1) * TOK]
            rv[u] = _rv_masks(32 * gj - 4, 256)
            key = ("rope", gj)
            if key not in _CACHE:
                _CACHE[key] = _rope_tables(gj * TOK)
            rc[u], rs[u] = _CACHE[key]
        xu[U_PROMPT] = np.concatenate([zeros_h, inp["x_prompt"][c], zeros_h], 0)
        p0[U_PROMPT] = inp["p_prompt"][0, c]
        rv[U_PROMPT] = _rv_masks(-4, 32)
        rc[U_PROMPT], rs[U_PROMPT] = rope_p
        p1 = np.stack([inp["p_sample"][1, 0, c * TOK:(c + 1) * TOK], inp["p_prompt"][1, c]], 0)
        maps.append({
            "xu": xu, "p0u": p0, "p1u": np.ascontiguousarray(p1), "rv": rv, "ropec": rc, "ropes": rs,
            "utab": utab, "utab2": utab2, "cst": cst, "gains": gains, "gqk": gqk,
            "na_w_qkv": inp["na_w_qkv"][0], "na_w_o": inp["na_w_o"][0],
            "gqa_w_qkv": inp["gqa_w_qkv"][0], "gqa_w_o": inp["gqa_w_o"][0],
            "ffn_w_gate_up": inp["ffn_w_gate_up"], "ffn_w_down": inp["ffn_w_down"],
            "ple_w_gate": inp["ple_w_gate"], "ple_w_proj": inp["ple_w_proj"],
        })
    return maps


def kernel(**inputs):
    inp = {k: np.asarray(v) for k, v in inputs.items()}
    maps = _prep_inputs(inp)
    nc, pg = build()
    res = run_bass_kernel_spmd(nc, maps, core_ids=list(range(NCORES)))
    y_prompt = np.stack([np.asarray(res.results[c]["yp"], np.float32) for c in range(NCORES)], 0)
    y_sample = np.concatenate([np.asarray(res.results[c]["ys"], np.float32) for c in range(NCORES)], 0)[None]
    return (y_prompt, y_sample)
```
